# Optimizing a Trainium2 kernel written in Bass

```python
import math
import jax
import jax.numpy as jnp
from jax import lax
import numpy as np

D_MODEL = 2048
BATCH = 16
SEQ = 2048
DEPTH = 2

MEM_LEN = 256
GRID_W = 64
Q_BLOCK = 128
NORM_EPS = 1e-6
F32 = jnp.float32

HY_WIDTH = D_MODEL // 2
HY_EMB = 33
HY_BANDS = (HY_EMB - 1) // 2
HY_FILT = 64
HY_SHORT = 3
HY_TARGET = 1e-2
HY_FAST_PCT = 0.3
HY_SLOW_PCT = 1.5
DF_WIDTH = D_MODEL - HY_WIDTH
DF_HEADS = 8
DF_HEAD_DIM = DF_WIDTH // (2 * DF_HEADS)
DF_V_DIM = 2 * DF_HEAD_DIM
DF_QK_WIDTH = DF_HEADS * 2 * DF_HEAD_DIM
EVEN_IN = 3 * HY_WIDTH + 2 * DF_QK_WIDTH + DF_HEADS * DF_V_DIM

HG_WIDTH = D_MODEL // 2
HG_HEADS = 8
HG_DK = HG_WIDTH // HG_HEADS
HG_DV = HG_WIDTH // HG_HEADS
HG_CHUNK = 64
GQ_WIDTH = D_MODEL - HG_WIDTH
GQ_HEADS = 8
GQ_KV_HEADS = 2
GQ_HEAD_DIM = GQ_WIDTH // GQ_HEADS
GQ_GROUP = GQ_HEADS // GQ_KV_HEADS
GQ_KV_WIDTH = GQ_KV_HEADS * GQ_HEAD_DIM
ROPE_THETA = 10000.0
ODD_IN = 5 * HG_WIDTH + GQ_WIDTH + 2 * GQ_KV_WIDTH

CA_HEADS = 4
CA_HEAD_DIM = 128
CA_WIDTH = CA_HEADS * CA_HEAD_DIM
FFN_HIDDEN = 4 * D_MODEL

kernel_name = 'hybrid_hyena_diffattn_hgrn2_axialgqa'


def rms_norm(x, g):
    xf = x.astype(F32)
    y = xf * lax.rsqrt(jnp.mean(xf * xf, axis=-1, keepdims=True) + NORM_EPS)
    return (y * g.astype(F32)).astype(x.dtype)


def alibi_slopes(n_heads):
    return jnp.asarray(np.array([2.0 ** (-8.0 * (h + 1) / n_heads) for h in range(n_heads)], dtype=np.float32))


def sweep_query_blocks(block_fn, q):
    *lead, L, d = q.shape
    nb = L // Q_BLOCK
    qb = jnp.moveaxis(q.reshape(tuple(lead) + (nb, Q_BLOCK, d)), -3, 0)
    starts = jnp.arange(nb, dtype=jnp.int32) * Q_BLOCK
    out = lax.map(lambda a: block_fn(a[0], a[1]), (qb, starts))
    out = jnp.moveaxis(out, 0, -3)
    return out.reshape(out.shape[:-3] + (L, out.shape[-1]))


def short_conv(u, w, b):
    L = u.shape[1]
    p = HY_SHORT // 2
    up = jnp.pad(u, ((0, 0), (p, p), (0, 0)))
    y = sum(up[:, j:j + L] * w[j] for j in range(HY_SHORT))
    return y + b


def hyena_filters(L, w1, b1, w2, b2, w3, b3, w4, sin_freq):
    t = jnp.linspace(0.0, 1.0, L, dtype=F32)[:, None]
    w = (2.0 * math.pi / L) * jnp.arange(L, dtype=F32)[:, None]
    f = jnp.linspace(1e-4, HY_BANDS - 1, HY_BANDS, dtype=F32)[None, :]
    z = jnp.concatenate([t, jnp.cos(f * w), -jnp.sin(f * w)], axis=-1)
    freq = sin_freq.astype(F32)
    h = jnp.sin(freq * (z @ w1.astype(F32) + b1.astype(F32)))
    h = jnp.sin(freq * (h @ w2.astype(F32) + b2.astype(F32)))
    h = jnp.sin(freq * (h @ w3.astype(F32) + b3.astype(F32)))
    h = h @ w4.astype(F32)
    max_decay = math.log(HY_TARGET) / HY_FAST_PCT
    min_decay = math.log(HY_TARGET) / HY_SLOW_PCT
    deltas = jnp.linspace(min_decay, max_decay, HY_WIDTH, dtype=F32)
    window = jnp.exp(-t * jnp.abs(deltas)[None, :])
    h_fwd = h[:, :HY_WIDTH] * window
    h_bwd = h[:, HY_WIDTH:] * window
    return jnp.concatenate([h_fwd, jnp.zeros((1, HY_WIDTH), F32), h_bwd[:0:-1]], axis=0)


def two_sided_fft_conv(v, k_circ, bias):
    L = v.shape[1]
    V = jnp.fft.rfft(v, n=2 * L, axis=1)
    K = jnp.fft.rfft(k_circ, n=2 * L, axis=0)
    y = jnp.fft.irfft(V * K[None], n=2 * L, axis=1)[:, :L]
    return y + v * bias.astype(F32)


def hyena_mixer(u, conv_w, conv_b, w1, b1, w2, b2, w3, b3, w4, sin_freq, bias):
    L = u.shape[1]
    uc = short_conv(u, conv_w, conv_b)
    x1, x2, v = jnp.split(uc, 3, axis=-1)
    k_circ = hyena_filters(L, w1, b1, w2, b2, w3, b3, w4, sin_freq)
    z = two_sided_fft_conv((v * x2).astype(F32), k_circ, bias)
    return (z * x1.astype(F32)).astype(u.dtype)


def diff_attention(q, k, v, lam, slopes):
    L = q.shape[-2]
    scale = DF_HEAD_DIM ** -0.5
    pos_k = jnp.arange(L, dtype=jnp.int32)

    def block(qb, q0):
        s = jnp.einsum('bhjqd,bhjkd->bhjqk', qb, k).astype(F32) * scale
        pos_q = q0 + jnp.arange(Q_BLOCK, dtype=jnp.int32)
        dist = jnp.abs(pos_q[:, None] - pos_k[None, :]).astype(F32)
        p = jax.nn.softmax(s - slopes[:, None, None, None] * dist, axis=-1)
        a = p[:, :, 0] - lam * p[:, :, 1]
        return jnp.einsum('bhqk,bhkv->bhqv', a.astype(v.dtype), v)

    return sweep_query_blocks(block, q)


def even_mixer(h, layer, w_in, w_out, conv_w, conv_b, fw1, fb1, fw2, fb2, fw3, fb3, fw4, sin_freq,
               hy_bias, q_g, k_g, lam_q1, lam_k1, lam_q2, lam_k2, subln_g):
    B, L, _ = h.shape
    proj = h @ w_in
    u_a, q_b, k_b, v_b = jnp.split(
        proj, [3 * HY_WIDTH, 3 * HY_WIDTH + DF_QK_WIDTH, 3 * HY_WIDTH + 2 * DF_QK_WIDTH], axis=-1)
    o_a = hyena_mixer(u_a, conv_w, conv_b, fw1, fb1, fw2, fb2, fw3, fb3, fw4, sin_freq, hy_bias)
    q = rms_norm(q_b.reshape(B, L, DF_HEADS, 2, DF_HEAD_DIM), q_g).transpose(0, 2, 3, 1, 4)
    k = rms_norm(k_b.reshape(B, L, DF_HEADS, 2, DF_HEAD_DIM), k_g).transpose(0, 2, 3, 1, 4)
    v = v_b.reshape(B, L, DF_HEADS, DF_V_DIM).transpose(0, 2, 1, 3)
    lam_init = 0.8 - 0.6 * math.exp(-0.3 * layer)
    lam = (jnp.exp(jnp.sum(lam_q1.astype(F32) * lam_k1.astype(F32)))
           - jnp.exp(jnp.sum(lam_q2.astype(F32) * lam_k2.astype(F32))) + lam_init)
    o_b = diff_attention(q, k, v, lam, alibi_slopes(DF_HEADS))
    o_b = rms_norm(o_b, subln_g) * (1.0 - lam_init)
    o_b = o_b.transpose(0, 2, 1, 3).reshape(B, L, DF_WIDTH)
    return jnp.concatenate([o_a, o_b.astype(o_a.dtype)], axis=-1) @ w_out


def chunk_gla(q, k, v, logf):
    B, H, L, dk = q.shape
    dv = v.shape[-1]
    n = L // HG_CHUNK

    def chunks(a):
        return jnp.moveaxis(a.reshape(B, H, n, HG_CHUNK, a.shape[-1]), 2, 0)

    lower = jnp.tril(jnp.ones((HG_CHUNK, HG_CHUNK), dtype=bool))[:, :, None]

    def step(state, inp):
        qc, kc, vc, gc = inp
        b = jnp.cumsum(gc, axis=2)
        b_last = b[:, :, -1:, :]
        o_inter = jnp.einsum('bhtk,bhkv->bhtv', qc * jnp.exp(b), state)
        rel = b[:, :, :, None, :] - b[:, :, None, :, :]
        decay = jnp.exp(jnp.where(lower, rel, -jnp.inf))
        att = jnp.einsum('bhtk,bhsk,bhtsk->bhts', qc, kc, decay)
        o_intra = jnp.einsum('bhts,bhsv->bhtv', att, vc)
        state = (jnp.exp(b_last[:, :, 0])[..., None] * state
                 + jnp.einsum('bhsk,bhsv->bhkv', kc * jnp.exp(b_last - b), vc))
        return state, o_inter + o_intra

    s0 = jnp.zeros((B, H, dk, dv), F32)
    _, o = lax.scan(step, s0, (chunks(q), chunks(k), chunks(v), chunks(logf)))
    return jnp.moveaxis(o, 0, 2).reshape(B, H, L, dv)


def hgrn2_mixer(q, fz_f, fz_b, i, g, lb_f, lb_b, norm_g):
    B, L, _ = q.shape

    def heads(a):
        return a.reshape(B, L, HG_HEADS, -1).transpose(0, 2, 1, 3).astype(F32)

    qh = heads(q) * HG_DK ** -0.5
    vh = heads(i)

    def direction(fz, lb, reverse):
        f = lb.astype(F32) + (1.0 - lb.astype(F32)) * jax.nn.sigmoid(fz.astype(F32))
        args = (qh, heads(1.0 - f), vh, heads(jnp.log(f)))
        if reverse:
            args = tuple(jnp.flip(a, axis=2) for a in args)
        o = chunk_gla(*args)
        return jnp.flip(o, axis=2) if reverse else o

    o = direction(fz_f, lb_f, False) + direction(fz_b, lb_b, True)
    o = rms_norm(o, norm_g).transpose(0, 2, 1, 3).reshape(B, L, HG_WIDTH)
    return (o * jax.nn.silu(g.astype(F32))).astype(q.dtype)


def axial_rope_tables(L):
    rows = L // GRID_W
    r, c = jnp.meshgrid(jnp.arange(rows, dtype=F32), jnp.arange(GRID_W, dtype=F32), indexing='ij')
    r = r.reshape(-1)
    c = c.reshape(-1)
    half = GQ_HEAD_DIM // 2
    inv = ROPE_THETA ** (-jnp.arange(0, half, 2, dtype=F32) / half)
    ang_r = r[:, None] * inv[None, :]
    ang_c = c[:, None] * inv[None, :]
    ang = jnp.concatenate([ang_r, ang_r, ang_c, ang_c], axis=-1)
    return jnp.cos(ang), jnp.sin(ang)


def rotate_axial(x, cos, sin):
    x1, x2, x3, x4 = jnp.split(x, 4, axis=-1)
    rot = jnp.concatenate([-x2, x1, -x4, x3], axis=-1)
    return (x.astype(F32) * cos + rot.astype(F32) * sin).astype(x.dtype)


def gqa_attention(q, k, v):
    scale = GQ_HEAD_DIM ** -0.5

    def block(qb, q0):
        s = jnp.einsum('bgrqd,bgkd->bgrqk', qb, k).astype(F32) * scale
        p = jax.nn.softmax(s, axis=-1)
        return jnp.einsum('bgrqk,bgkd->bgrqd', p.astype(v.dtype), v)

    return sweep_query_blocks(block, q)


def odd_mixer(h, lb_f, lb_b, w_in, w_out, hg_norm_g, q_g, k_g):
    B, L, _ = h.shape
    proj = h @ w_in
    bounds = [HG_WIDTH, 2 * HG_WIDTH, 3 * HG_WIDTH, 4 * HG_WIDTH, 5 * HG_WIDTH,
              5 * HG_WIDTH + GQ_WIDTH, 5 * HG_WIDTH + GQ_WIDTH + GQ_KV_WIDTH]
    q_c, fz_f, fz_b, i_c, g_c, q_d, k_d, v_d = jnp.split(proj, bounds, axis=-1)
    o_c = hgrn2_mixer(q_c, fz_f, fz_b, i_c, g_c, lb_f, lb_b, hg_norm_g)
    cos, sin = axial_rope_tables(L)
    cos = cos[:, None, :]
    sin = sin[:, None, :]
    q = rotate_axial(rms_norm(q_d.reshape(B, L, GQ_HEADS, GQ_HEAD_DIM), q_g), cos, sin)
    k = rotate_axial(rms_norm(k_d.reshape(B, L, GQ_KV_HEADS, GQ_HEAD_DIM), k_g), cos, sin)
    q = q.reshape(B, L, GQ_KV_HEADS, GQ_GROUP, GQ_HEAD_DIM).transpose(0, 2, 3, 1, 4)
    k = k.transpose(0, 2, 1, 3)
    v = v_d.reshape(B, L, GQ_KV_HEADS, GQ_HEAD_DIM).transpose(0, 2, 1, 3)
    o_d = gqa_attention(q, k, v).transpose(0, 3, 1, 2, 4).reshape(B, L, GQ_WIDTH)
    return jnp.concatenate([o_c, o_d.astype(o_c.dtype)], axis=-1) @ w_out


def memory_cross_attention(h, mem_n, w_q, w_kv, w_out, q_g, k_g):
    B, L, _ = h.shape
    M = mem_n.shape[1]
    q = rms_norm((h @ w_q).reshape(B, L, CA_HEADS, CA_HEAD_DIM), q_g)
    k, v = jnp.split(mem_n @ w_kv, 2, axis=-1)
    k = rms_norm(k.reshape(B, M, CA_HEADS, CA_HEAD_DIM), k_g)
    v = v.reshape(B, M, CA_HEADS, CA_HEAD_DIM)
    s = jnp.einsum('bqhd,bkhd->bhqk', q, k).astype(F32) * CA_HEAD_DIM ** -0.5
    p = jax.nn.softmax(s, axis=-1)
    o = jnp.einsum('bhqk,bkhd->bqhd', p.astype(v.dtype), v).reshape(B, L, CA_WIDTH)
    return o @ w_out


def setup_inputs(seed: int = 0) -> dict:
    key = jax.random.key(seed)
    ks = jax.random.split(key, 64)
    cnt = [0]

    def nrm(shape, scale):
        k = ks[cnt[0]]
        cnt[0] += 1
        return jax.random.normal(k, shape, jnp.float32) * scale

    def gain(shape):
        return 1.0 + nrm(shape, 0.05)

    ne = (DEPTH + 1) // 2
    no = DEPTH // 2
    D = D_MODEL
    return {
        'x': nrm((BATCH, SEQ, D), 1.0),
        'mem': nrm((BATCH, MEM_LEN, D), 1.0),
        'norm_mix_g': gain((DEPTH, D)),
        'norm_mem_q_g': gain((DEPTH, D)),
        'norm_mem_kv_g': gain((DEPTH, D)),
        'norm_ffn_g': gain((DEPTH, D)),
        'ev_w_in': nrm((ne, D, EVEN_IN), D ** -0.5),
        'ev_w_out': nrm((ne, HY_WIDTH + DF_WIDTH, D), (HY_WIDTH + DF_WIDTH) ** -0.5),
        'hy_conv_w': nrm((ne, HY_SHORT, 3 * HY_WIDTH), HY_SHORT ** -0.5),
        'hy_conv_b': nrm((ne, 3 * HY_WIDTH), 0.02),
        'hy_fw1': nrm((ne, HY_EMB, HY_FILT), HY_EMB ** -0.5),
        'hy_fb1': nrm((ne, HY_FILT), 0.02),
        'hy_fw2': nrm((ne, HY_FILT, HY_FILT), HY_FILT ** -0.5),
        'hy_fb2': nrm((ne, HY_FILT), 0.02),
        'hy_fw3': nrm((ne, HY_FILT, HY_FILT), HY_FILT ** -0.5),
        'hy_fb3': nrm((ne, HY_FILT), 0.02),
        'hy_fw4': nrm((ne, HY_FILT, 2 * HY_WIDTH), 0.1 * HY_FILT ** -0.5),
        'hy_sin_freq': gain((ne, HY_FILT)),
        'hy_bias': nrm((ne, HY_WIDTH), 0.5),
        'df_q_g': gain((ne, DF_HEAD_DIM)),
        'df_k_g': gain((ne, DF_HEAD_DIM)),
        'df_lam_q1': nrm((ne, DF_HEAD_DIM), 0.1),
        'df_lam_k1': nrm((ne, DF_HEAD_DIM), 0.1),
        'df_lam_q2': nrm((ne, DF_HEAD_DIM), 0.1),
        'df_lam_k2': nrm((ne, DF_HEAD_DIM), 0.1),
        'df_subln_g': gain((ne, DF_V_DIM)),
        'od_w_in': nrm((no, D, ODD_IN), D ** -0.5),
        'od_w_out': nrm((no, HG_WIDTH + GQ_WIDTH, D), (HG_WIDTH + GQ_WIDTH) ** -0.5),
        'hg_lb_logits': nrm((2, DEPTH, HG_WIDTH), 0.1),
        'hg_norm_g': gain((no, HG_DV)),
        'gq_q_g': gain((no, GQ_HEAD_DIM)),
        'gq_k_g': gain((no, GQ_HEAD_DIM)),
        'ca_w_q': nrm((DEPTH, D, CA_WIDTH), D ** -0.5),
        'ca_w_kv': nrm((DEPTH, D, 2 * CA_WIDTH), D ** -0.5),
        'ca_w_out': nrm((DEPTH, CA_WIDTH, D), CA_WIDTH ** -0.5),
        'ca_q_g': gain((DEPTH, CA_HEAD_DIM)),
        'ca_k_g': gain((DEPTH, CA_HEAD_DIM)),
        'mlp_w1': nrm((DEPTH, D, FFN_HIDDEN), D ** -0.5),
        'mlp_w2': nrm((DEPTH, FFN_HIDDEN, D), FFN_HIDDEN ** -0.5),
    }


def reference(x, mem, norm_mix_g, norm_mem_q_g, norm_mem_kv_g, norm_ffn_g,
              ev_w_in, ev_w_out, hy_conv_w, hy_conv_b, hy_fw1, hy_fb1, hy_fw2, hy_fb2,
              hy_fw3, hy_fb3, hy_fw4, hy_sin_freq, hy_bias,
              df_q_g, df_k_g, df_lam_q1, df_lam_k1, df_lam_q2, df_lam_k2, df_subln_g,
              od_w_in, od_w_out, hg_lb_logits, hg_norm_g, gq_q_g, gq_k_g,
              ca_w_q, ca_w_kv, ca_w_out, ca_q_g, ca_k_g, mlp_w1, mlp_w2):
    lb_soft = jax.nn.softmax(hg_lb_logits.astype(F32), axis=1)
    lbs = jnp.cumsum(lb_soft, axis=1)
    lbs = lbs - lbs[:, :1]
    for layer in range(DEPTH):
        j = layer // 2
        h = rms_norm(x, norm_mix_g[layer])
        if layer % 2 == 0:
            mix = even_mixer(h, layer, ev_w_in[j], ev_w_out[j], hy_conv_w[j], hy_conv_b[j],
                             hy_fw1[j], hy_fb1[j], hy_fw2[j], hy_fb2[j], hy_fw3[j], hy_fb3[j],
                             hy_fw4[j], hy_sin_freq[j], hy_bias[j], df_q_g[j], df_k_g[j],
                             df_lam_q1[j], df_lam_k1[j], df_lam_q2[j], df_lam_k2[j], df_subln_g[j])
        else:
            mix = odd_mixer(h, lbs[0, layer], lbs[1, layer], od_w_in[j], od_w_out[j],
                            hg_norm_g[j], gq_q_g[j], gq_k_g[j])
        x = x + mix.astype(x.dtype)
        x = x + memory_cross_attention(rms_norm(x, norm_mem_q_g[layer]), rms_norm(mem, norm_mem_kv_g[layer]),
                                       ca_w_q[layer], ca_w_kv[layer], ca_w_out[layer],
                                       ca_q_g[layer], ca_k_g[layer]).astype(x.dtype)
        h = rms_norm(x, norm_ffn_g[layer])
        x = x + (jnp.square(jax.nn.relu(h @ mlp_w1[layer])) @ mlp_w2[layer]).astype(x.dtype)
    return x
```

```python
import contextlib
import math
import os
import numpy as np
import ml_dtypes
import concourse.bass as bass
import concourse.mybir as mybir
from concourse.bass_utils import run_bass_kernel_spmd

F32 = mybir.dt.float32
BF16 = mybir.dt.bfloat16
AF = mybir.ActivationFunctionType
ALU = mybir.AluOpType
AX = mybir.AxisListType

NCORES = 8
SPC = 2
L = 2048
D = 2048
KC = D // 128
MEM = 256
EPS = 1e-6
FFN = 8192
EVEN_IN = 6144
ODD_IN = 6656


class Buf:
    __slots__ = ("name", "w", "r")

    def __init__(self, name):
        self.name = name
        self.w = {}
        self.r = {}


class Eng:
    def __init__(self, name, h, sem):
        self.name, self.h, self.sem = name, h, sem
        self.n = 0
        self.waited = {}


class KB:
    def __init__(self, nc, es):
        self.nc, self.es = nc, es
        self.sems = {}
        self.eng = {}
        for name, h in (("pe", nc.tensor), ("act", nc.scalar), ("dve", nc.vector),
                        ("pool", nc.gpsimd), ("sp", nc.sync)):
            sem = es.enter_context(nc.semaphore("s_" + name))
            self.eng[name] = Eng(name, h, sem)
            self.sems[name] = sem
        self.dq = {}
        for q, nslots in (("sp", 8), ("pool", 8), ("act", 4)):
            slots = []
            for i in range(nslots):
                key = "d_%s%d" % (q, i)
                self.sems[key] = es.enter_context(nc.semaphore(key))
                slots.append([key, 0])
            self.dq[q] = [slots, 0]
        self.nbuf = 0

    def buf(self, name="b"):
        self.nbuf += 1
        return Buf("%s%d" % (name, self.nbuf))

    def _wait(self, E, key, val):
        if E.waited.get(key, 0) >= val:
            return
        if key == "pe" and E.name == "pe":
            return
        E.h.wait_ge(self.sems[key], val)
        E.waited[key] = val

    def _deps(self, E, r, w):
        need = {}
        for b in r:
            for k_, v in b.w.items():
                if need.get(k_, 0) < v:
                    need[k_] = v
        for b in w:
            for k_, v in b.w.items():
                if need.get(k_, 0) < v:
                    need[k_] = v
            for k_, v in b.r.items():
                if need.get(k_, 0) < v:
                    need[k_] = v
        for k_, v in need.items():
            self._wait(E, k_, v)

    def _reg(self, key, val, r, w):
        for b in r:
            if b.r.get(key, 0) < val:
                b.r[key] = val
        for b in w:
            b.w = {key: val}
            b.r = {}

    def op(self, eng, fn, r=(), w=(), inc=True):
        E = self.eng[eng]
        self._deps(E, r, w)
        ins = fn(E.h)
        val = E.n + 1
        if inc:
            ins.then_inc(E.sem, 1)
            E.n = val
        self._reg(eng, val, r, w)
        return ins

    def dma(self, out, in_, r=(), w=(), q="sp", **kw):
        E = self.eng[q]
        slots, n = self.dq[q]
        slot = slots[n % len(slots)]
        self.dq[q][1] = n + 1
        self._deps(E, r, w)
        if slot[1] > 0:
            self._wait(E, slot[0], slot[1])
        ins = E.h.dma_start(out=out, in_=in_, **kw)
        slot[1] += 16
        ins.then_inc(self.sems[slot[0]], 16)
        self._reg(slot[0], slot[1], r, w)

    def barrier(self, engines=("pe", "act", "dve", "pool", "sp")):
        cur = {}
        for name, E in self.eng.items():
            if E.n > 0:
                cur[name] = E.n
        for q, (slots, n) in self.dq.items():
            for key, v in slots:
                if v > 0:
                    cur[key] = v
        for name in engines:
            E = self.eng[name]
            for key, v in cur.items():
                if key == name:
                    continue
                self._wait(E, key, v)

    def final_wait(self):
        self.barrier(engines=("sp",))


class Stage:
    def __init__(self, kb):
        self.kb = kb
        self.es = contextlib.ExitStack()
        self.cnt = 0

    def __enter__(self):
        self.es.__enter__()
        return self

    def __exit__(self, *a):
        self.kb.barrier()
        return self.es.__exit__(*a)

    def sb(self, name, shape, dt):
        self.kb.nbuf += 1
        t = self.es.enter_context(self.kb.nc.sbuf_tensor("%s_%d" % (name, self.kb.nbuf), list(shape), dt))
        return t, self.kb.buf(name)

    def rot(self, name, shape, dt, n):
        return Rot([self.sb(name + str(i), shape, dt) for i in range(n)])


class Rot:
    def __init__(self, items):
        self.items = items
        self.i = 0

    def next(self):
        it = self.items[self.i % len(self.items)]
        self.i += 1
        return it


TWO_PI = 2.0 * math.pi
LAM_INIT0 = 0.8 - 0.6 * math.exp(-0.3 * 0)
ALL_STAGES = ("pro", "filt", "mix0", "ca0", "mlp0", "mix1", "ca1", "mlp1", "epi")


class Ctx:
    pass


def build(stages=ALL_STAGES, debug=()):
    C = Ctx()
    nc = bass.Bass("TRN2", target_bir_lowering=False)
    C.nc = nc
    C.din = {}
    debug = set(debug)

    def inp(name, shape, dt=F32):
        t = nc.dram_tensor(name, list(shape), dt, kind="ExternalInput").ap()
        C.din[name] = t
        return t

    def scratch(name, shape, dt):
        kind = "ExternalOutput" if name in debug else "Internal"
        return nc.dram_tensor(name, list(shape), dt, kind=kind).ap()

    x = inp("x", [SPC, L, D])
    mem = inp("mem", [SPC, MEM, D])
    y = nc.dram_tensor("y", [SPC, L, D], F32, kind="ExternalOutput").ap()
    gains_d = inp("gains", [128, 4, 2, KC])
    ev_w_in_t = inp("ev_w_in_t", [48, 128, KC, 128])
    ev_w_out_t = inp("ev_w_out_t", [16, 128, KC, 128])
    od_w_in_t = inp("od_w_in_t", [52, 128, KC, 128])
    od_w_out_t = inp("od_w_out_t", [16, 128, KC, 128])
    ca_wq_t = inp("ca_wq_t", [2, 4, 128, KC, 128])
    ca_wk_t = inp("ca_wk_t", [2, 4, 128, KC, 128])
    ca_wv_t = inp("ca_wv_t", [2, 128, KC, 512])
    ca_wo_t = inp("ca_wo_t", [2, 16, 128, 4, 128])
    mlp_w1_t = inp("mlp_w1_t", [2, 64, 128, KC, 128])
    mlp_w2_t = inp("mlp_w2_t", [2, 16, 128, 64, 128])
    hy_cw_d = inp("hy_cw", [128, 3, 24])
    hy_cb_d = inp("hy_cb", [128, 24])
    hy_bias_d = inp("hy_bias", [128, 8])
    hy_fw1_d = inp("hy_fw1", [33, 64])
    hy_fw2_d = inp("hy_fw2", [64, 64])
    hy_fw3_d = inp("hy_fw3", [64, 64])
    hy_fw4_d = inp("hy_fw4", [64, 2048])
    hy_fcols_d = inp("hy_fcols", [64, 4])
    df_cols_d = inp("df_cols", [128, 3])
    df_lam_d = inp("df_lam", [1, 4, 64])
    hg_lb_d = inp("hg_lb", [128, 2, 2, 8])
    od_cols_d = inp("od_cols", [128, 3])
    ca_cols_d = inp("ca_cols", [128, 2, 2])
    c_identf = inp("c_identf", [128, 128])
    c_bd64 = inp("c_bd64", [128, 128])
    c_alibi = inp("c_alibi", [128, 3968])
    c_zT = inp("c_zT", [33, L])
    c_win = inp("c_win", [16, 128, 1024])
    c_m0 = inp("c_m0", [128, 2])
    c_FT = inp("c_FT", [32, 128, 16, 128], BF16)
    c_G = inp("c_G", [4, 2, 128, 16, 512], BF16)
    c_cos = inp("c_cos", [128, L])
    c_sin = inp("c_sin", [128, L])
    c_PT = inp("c_PT", [128, 128])
    c_mask = inp("c_mask", [64, 2, 512])

    xT = scratch("xT", [SPC, D, L], F32)
    oT = scratch("oT", [SPC, D, L], BF16)
    hx1 = scratch("hx1", [SPC, 1024, L], F32)
    hvx = scratch("hvx", [SPC, 1024, L], F32)
    dq = scratch("dq", [SPC, 1024, L], BF16)
    dk = scratch("dk", [SPC, 1024, L], BF16)
    dv = scratch("dv", [SPC, L, 1024], BF16)
    gk = scratch("gk", [SPC, 256, L], BF16)
    gv = scratch("gv", [SPC, L, 256], BF16)
    KA = scratch("KA", [16, 128, 1024], F32)
    KBt = scratch("KB", [16, 128, 1024], F32)
    KD0 = scratch("KD0", [128, 1024], F32)
    xTv = [xT[s].rearrange("(c p) t -> p c t", p=128) for s in range(SPC)]
    oTv = [oT[s].rearrange("(c p) t -> p c t", p=128) for s in range(SPC)]

    es = contextlib.ExitStack()
    with es:
        kb = KB(nc, es)
        C.kb = kb
        b_xT = [[[kb.buf("xT") for _ in range(KC)] for _ in range(4)] for _ in range(SPC)]
        b_oT = [[kb.buf("oT") for _ in range(KC)] for _ in range(SPC)]
        b_scr = {n: [kb.buf(n) for _ in range(SPC)] for n in ("hx1", "hvx", "dq", "dk", "dv", "gk", "gv")}
        b_K = kb.buf("K")
        psA = es.enter_context(nc.psum_tensor("psA", [128, 4, 512], F32))
        psB = es.enter_context(nc.psum_tensor("psB", [128, 4, 512], F32))
        pst = [psA, psA, psA, psA, psB, psB, psB, psB]
        b_ps = [kb.buf("ps") for _ in range(8)]

        def bank(i):
            return pst[i][:, i % 4, :]

        def bank_bf(i):
            return pst[i][:, i % 4, :].bitcast(BF16)

        def xall(s, oc):
            return [b_xT[s][tb][oc] for tb in range(4)]

        def xblk(s, tb):
            return list(b_xT[s][tb])

        def evac(i, out, in_, r, w):
            if i % 2 == 0:
                kb.op("act", lambda e: e.copy(out=out, in_=in_), r=r, w=w)
            else:
                kb.op("dve", lambda e: e.tensor_copy(out=out, in_=in_), r=r, w=w)

        def mm(out, lhsT, rhs, start, stop, r, w, inc=None):
            kb.op("pe", lambda e: e.matmul(out, lhsT=lhsT, rhs=rhs, start=start, stop=stop), r=r, w=w,
                  inc=(stop if inc is None else inc))

        def rstd_inplace(ap, b, src, src_b, inv_n):
            kb.op("dve", lambda e: e.tensor_scalar(out=ap, in0=src, scalar1=inv_n, scalar2=EPS, op0=ALU.mult, op1=ALU.add),
                  r=src_b, w=[b])
            kb.op("act", lambda e: e.activation(out=ap, in_=ap, func=AF.Ln), r=[b], w=[b])
            kb.op("act", lambda e: e.activation(out=ap, in_=ap, func=AF.Exp, scale=-0.5), r=[b], w=[b])

        psA4 = psA[:, :, :]
        psB4 = psB[:, :, :]
        bA = b_ps[0:4]
        bB = b_ps[4:8]

        with Stage(kb) as g:
            identf, b_c = g.sb("identf", [128, 128], F32)
            b_c = kb.buf("const")
            kb.dma(identf[:], c_identf[:, :], w=[b_c])
            identb, _ = g.sb("identb", [128, 128], BF16)
            onesb, _ = g.sb("onesb", [128, 128], BF16)
            onesf, _ = g.sb("onesf", [128, 128], F32)
            bd64f, _ = g.sb("bd64f", [128, 128], F32)
            bd64b, _ = g.sb("bd64b", [128, 128], BF16)
            gains, _ = g.sb("gains", [128, 4, 2, KC], F32)
            kb.dma(bd64f[:], c_bd64[:, :], w=[b_c])
            kb.dma(gains[:], gains_d[:, :, :, :], w=[b_c])
            kb.op("dve", lambda e: e.tensor_copy(out=identb[:], in_=identf[:]), r=[b_c], w=[b_c])
            kb.op("dve", lambda e: e.tensor_copy(out=bd64b[:], in_=bd64f[:]), r=[b_c], w=[b_c])
            kb.op("dve", lambda e: e.memset(onesb[:], 1.0), w=[b_c])
            kb.op("dve", lambda e: e.memset(onesf[:], 1.0), w=[b_c])
            m0, _ = g.sb("m0", [128, 2], F32)
            kb.dma(m0[:], c_m0[:, :], w=[b_c])
            hycw, _ = g.sb("hycw", [128, 3, 24], F32)
            hycb, _ = g.sb("hycb", [128, 24], F32)
            hyb, _ = g.sb("hyb", [128, 8], F32)
            dfc, _ = g.sb("dfc", [128, 3], F32)
            odc, _ = g.sb("odc", [128, 3], F32)
            cac, _ = g.sb("cac", [128, 2, 2], F32)
            kb.dma(hycw[:], hy_cw_d[:, :, :], w=[b_c])
            kb.dma(hycb[:], hy_cb_d[:, :], w=[b_c])
            kb.dma(hyb[:], hy_bias_d[:, :], w=[b_c])
            kb.dma(dfc[:], df_cols_d[:, :], w=[b_c])
            kb.dma(odc[:], od_cols_d[:, :], w=[b_c])
            kb.dma(cac[:], ca_cols_d[:, :, :], w=[b_c])
            neglam, _ = g.sb("neglam", [128, 1], F32)
            sg2, _ = g.sb("sg2", [128, 1], F32)
            lbs, _ = g.sb("lbs", [128, 2, 8], F32)
            oml, _ = g.sb("oml", [128, 2, 8], F32)
            noml, _ = g.sb("noml", [128, 2, 8], F32)
            RgT, _ = g.sb("RgT", [128, 2, 128], BF16)

            with Stage(kb) as st:
                lamv, b_l = st.sb("lamv", [1, 4, 64], F32)
                kb.dma(lamv[:], df_lam_d[:, :, :], w=[b_l])
                pr, b_pr = st.sb("pr", [1, 2, 64], F32)
                sm, b_sm = st.sb("sm", [1, 2], F32)
                kb.op("dve", lambda e: e.tensor_tensor(out=pr[:, 0, :], in0=lamv[:, 0, :], in1=lamv[:, 1, :], op=ALU.mult), r=[b_l], w=[b_pr])
                kb.op("dve", lambda e: e.tensor_tensor(out=pr[:, 1, :], in0=lamv[:, 2, :], in1=lamv[:, 3, :], op=ALU.mult), r=[b_l], w=[b_pr])
                kb.op("dve", lambda e: e.reduce_sum(out=sm[:], in_=pr[:], axis=AX.X), r=[b_pr], w=[b_sm])
                kb.op("act", lambda e: e.activation(out=sm[:], in_=sm[:], func=AF.Exp), r=[b_sm], w=[b_sm])
                nl, b_nl = st.sb("nl", [1, 1], F32)
                kb.op("dve", lambda e: e.tensor_tensor(out=nl[:], in0=sm[:, 1:2], in1=sm[:, 0:1], op=ALU.subtract), r=[b_sm], w=[b_nl])
                kb.op("dve", lambda e: e.tensor_scalar_add(out=nl[:], in0=nl[:], scalar1=-LAM_INIT0), r=[b_nl], w=[b_nl])
                mm(bank(0)[:, 0:1], onesf[0:1, :], nl[0:1, 0:1], True, True, r=[b_nl, b_c], w=[b_ps[0]])
                kb.op("dve", lambda e: e.tensor_copy(out=neglam[:], in_=bank(0)[:, 0:1]), r=[b_ps[0]], w=[b_c])
                kb.op("dve", lambda e: e.tensor_scalar(out=sg2[:], in0=dfc[:, 2:3], scalar1=(1.0 - LAM_INIT0), scalar2=None, op0=ALU.mult), r=[b_c], w=[b_c])
                lg, b_lg = st.sb("lg", [128, 2, 2, 8], F32)
                kb.dma(lg[:], hg_lb_d[:, :, :, :], w=[b_lg])
                kb.op("dve", lambda e: e.tensor_tensor(out=lbs[:], in0=lg[:, :, 1, :], in1=lg[:, :, 0, :], op=ALU.subtract), r=[b_lg], w=[b_c])
                kb.op("act", lambda e: e.activation(out=lbs[:], in_=lbs[:], func=AF.Sigmoid), r=[b_c], w=[b_c])
                kb.op("dve", lambda e: e.tensor_scalar(out=oml[:], in0=lbs[:], scalar1=-1.0, scalar2=1.0, op0=ALU.mult, op1=ALU.add), r=[b_c], w=[b_c])
                kb.op("dve", lambda e: e.tensor_scalar(out=noml[:], in0=oml[:], scalar1=-1.0, scalar2=None, op0=ALU.mult), r=[b_c], w=[b_c])
                ptf, b_pt = st.sb("ptf", [128, 128], F32)
                kb.dma(ptf[:], c_PT[:, :], w=[b_pt])
                kb.op("dve", lambda e: e.tensor_scalar(out=RgT[:, 0, :], in0=ptf[:], scalar1=odc[:, 1:2], scalar2=None, op0=ALU.mult), r=[b_pt, b_c], w=[b_c])
                kb.op("dve", lambda e: e.tensor_scalar(out=RgT[:, 1, :], in0=ptf[:], scalar1=odc[:, 2:3], scalar2=None, op0=ALU.mult), r=[b_pt, b_c], w=[b_c])

            def norm_to_hT(s, which, layer, hT, b_hT):
                with Stage(kb) as st:
                    xb = st.rot("nx", [128, KC, 512], F32, 2)
                    sq, b_sq = st.sb("nsq", [128, KC, 512], BF16)
                    rs = st.rot("nrs", [128, 512], F32, 2)
                    for tb in range(4):
                        xt, b_xt = xb.next()
                        kb.dma(xt[:], xTv[s][:, :, tb * 512:(tb + 1) * 512], r=xblk(s, tb), w=[b_xt])
                        kb.op("act", lambda e: e.activation(out=sq[:], in_=xt[:], func=AF.Square), r=[b_xt], w=[b_sq])
                        for c in range(KC):
                            mm(bank(4), onesb[:], sq[:, c, :], c == 0, c == KC - 1, r=[b_sq, b_c], w=[b_ps[4]])
                        r_, b_r = rs.next()
                        rstd_inplace(r_[:], b_r, bank(4), [b_ps[4]], 1.0 / D)
                        for c in range(KC):
                            kb.op("dve", lambda e: e.scalar_tensor_tensor(out=hT[:, c, tb * 512:(tb + 1) * 512], in0=xt[:, c, :],
                                                                         scalar=gains[:, which, layer, c:c + 1], in1=r_[:],
                                                                         op0=ALU.mult, op1=ALU.mult),
                                  r=[b_xt, b_r, b_c], w=[b_hT[c]])

            def lin_fm(wt_rot, w_ap, hT, b_hT, kcn, ntb=4, q="pool", pbase=0):
                wt, b_wt = wt_rot.next()
                kb.dma(wt[:], w_ap, w=[b_wt], q=q)
                for tb in range(ntb):
                    for kc in range(kcn):
                        mm(bank(pbase + tb), wt[:, kc, :], hT[:, kc, tb * 512:(tb + 1) * 512], kc == 0, kc == kcn - 1,
                           r=[b_wt, b_hT[kc]], w=[b_ps[pbase + tb]])

            def residual_add(st_rot, s, oc, pbase=0):
                xr, b_xr = st_rot.next()
                kb.dma(xr[:], xT[s, oc * 128:(oc + 1) * 128, :], r=xall(s, oc), w=[b_xr])
                kb.op("dve", lambda e: e.tensor_tensor(out=xr[:].rearrange("p (b t) -> p b t", b=4), in0=xr[:].rearrange("p (b t) -> p b t", b=4),
                                                       in1=(psA4 if pbase == 0 else psB4), op=ALU.add), r=[b_xr] + (bA if pbase == 0 else bB), w=[b_xr])
                kb.dma(xT[s, oc * 128:(oc + 1) * 128, :], xr[:], r=[b_xr], w=xall(s, oc))

            def rmsnorm_fm(st, src_f, b_src, ones_l, inv_n, gcol, out_bf, b_out, extra=None):
                sqb, b_sqb = st["sqb"].next()
                rs, b_rs = st["rs"].next()
                kb.op("act", lambda e: e.activation(out=sqb[:], in_=src_f, func=AF.Square), r=[b_src], w=[b_sqb])
                for tb in range(4):
                    mm(bank(4 + tb), ones_l, sqb[:, tb * 512:(tb + 1) * 512], True, True, r=[b_sqb, b_c], w=[b_ps[4 + tb]])
                rstd_inplace(rs[:].rearrange("p (b t) -> p b t", b=4), b_rs, psB4, bB, inv_n)
                return rs, b_rs

            if "pro" in stages:
                with Stage(kb) as st:
                    xin = st.rot("xin", [128, D], F32, 2)
                    stg = st.rot("stg", [128, KC, 512], F32, 2)
                    ev = 0
                    for s in range(SPC):
                        for tb in range(4):
                            sg, b_sg = stg.next()
                            for tt in range(4):
                                xi, b_xi = xin.next()
                                t0 = tb * 512 + tt * 128
                                kb.dma(xi[:], x[s, t0:t0 + 128, :], w=[b_xi])
                                for j in range(4):
                                    for c4 in range(4):
                                        c = j * 4 + c4
                                        kb.op("pe", lambda e: e.transpose(out=bank(j)[:, c4 * 128:(c4 + 1) * 128],
                                                                          in_=xi[:, c * 128:(c + 1) * 128], identity=identf[:]),
                                              r=[b_xi, b_c], w=[b_ps[j]], inc=(c4 == 3))
                                    src = bank(j).rearrange("p (c t) -> p c t", c=4)
                                    dst = sg[:, j * 4:(j + 1) * 4, tt * 128:(tt + 1) * 128]
                                    evac(ev, dst, src, [b_ps[j]], [b_sg])
                                    ev += 1
                            kb.dma(xTv[s][:, :, tb * 512:(tb + 1) * 512], sg[:], r=[b_sg], w=xblk(s, tb))

            def sin_layer(st, dst, src_ps, b_src, col, n):
                u, b_u = st["u"].next()
                qf, b_qf = st["qf"].next()
                qi, b_qi = st["qi"].next()
                kb.op("dve", lambda e: e.tensor_scalar(out=u[:], in0=src_ps, scalar1=st["fcols"][:, 3:4], scalar2=st["ffb"][:, col:col + 1],
                                                       op0=ALU.mult, op1=ALU.add), r=b_src + [st["b_f"]], w=[b_u])
                kb.op("dve", lambda e: e.tensor_scalar(out=u[:], in0=u[:], scalar1=17.0 * math.pi, scalar2=None, op0=ALU.add), r=[b_u], w=[b_u])
                kb.op("dve", lambda e: e.tensor_scalar(out=qf[:], in0=u[:], scalar1=1.0 / TWO_PI, scalar2=None, op0=ALU.mult), r=[b_u], w=[b_qf])
                kb.op("dve", lambda e: e.tensor_copy(out=qi[:], in_=qf[:]), r=[b_qf], w=[b_qi])
                kb.op("dve", lambda e: e.tensor_copy(out=qf[:], in_=qi[:]), r=[b_qi], w=[b_qf])
                kb.op("dve", lambda e: e.scalar_tensor_tensor(out=u[:], in0=qf[:], scalar=-TWO_PI, in1=u[:], op0=ALU.mult, op1=ALU.add), r=[b_qf, b_u], w=[b_u])
                kb.op("dve", lambda e: e.tensor_scalar(out=qf[:], in0=u[:], scalar1=0.0, scalar2=TWO_PI, op0=ALU.is_lt, op1=ALU.mult), r=[b_u], w=[b_qf])
                kb.op("dve", lambda e: e.tensor_tensor(out=u[:], in0=u[:], in1=qf[:], op=ALU.add), r=[b_u, b_qf], w=[b_u])
                kb.op("act", lambda e: e.activation(out=dst, in_=u[:], func=AF.Sin, bias=st["mpi"][:], scale=1.0), r=[b_u, st["b_f"]], w=[st["b_h"]])

            if "filt" in stages:
                with Stage(kb) as st0:
                    zT, b_f = st0.sb("zT", [33, L], F32)
                    w1, _ = st0.sb("w1", [33, 64], F32)
                    w2, _ = st0.sb("w2", [64, 64], F32)
                    w3, _ = st0.sb("w3", [64, 64], F32)
                    w4, _ = st0.sb("w4", [64, 2048], F32)
                    fcols, _ = st0.sb("fcols", [64, 4], F32)
                    ffb, _ = st0.sb("ffb", [64, 3], F32)
                    mpi, _ = st0.sb("mpi", [64, 1], F32)
                    for t_, d_ in ((zT, c_zT), (w1, hy_fw1_d), (w2, hy_fw2_d), (w3, hy_fw3_d), (w4, hy_fw4_d), (fcols, hy_fcols_d)):
                        kb.dma(t_[:], d_[:, :], w=[b_f])
                    kb.op("dve", lambda e: e.memset(mpi[:], -math.pi), w=[b_f])
                    kb.op("dve", lambda e: e.tensor_scalar(out=ffb[:], in0=fcols[:, 0:3], scalar1=fcols[:, 3:4], scalar2=None, op0=ALU.mult), r=[b_f], w=[b_f])
                    hA, b_hA = st0.sb("hA", [64, L], F32)
                    hB, b_hB = st0.sb("hB", [64, L], F32)
                    sl = {"u": st0.rot("u", [64, 512], F32, 2), "qf": st0.rot("qf", [64, 512], F32, 2),
                          "qi": st0.rot("qi", [64, 512], mybir.dt.int32, 2), "fcols": fcols, "ffb": ffb, "mpi": mpi, "b_f": b_f}
                    for li, (wl, kdim, src, b_s, dst, b_d) in enumerate(((w1, 33, zT, b_f, hA, b_hA), (w2, 64, hA, b_hA, hB, b_hB), (w3, 64, hB, b_hB, hA, b_hA))):
                        sl["b_h"] = b_d
                        for tb in range(4):
                            mm(bank(tb)[0:64, :], wl[0:kdim, :], src[0:kdim, tb * 512:(tb + 1) * 512], True, True, r=[b_f, b_s], w=[b_ps[tb]])
                            sin_layer(sl, dst[:, tb * 512:(tb + 1) * 512], bank(tb)[0:64, :], [b_ps[tb]], li, 512)
                    h3 = hA
                    b_h3 = b_hA
                    for cb in range(2):
                        with Stage(kb) as st:
                            kp, b_kp = st.sb("kp", [128, 16, 512], BF16)
                            km, b_km = st.sb("km", [128, 16, 512], BF16)
                            wn = st.rot("wn", [128, 512], F32, 2)
                            ta = st.rot("ta", [128, 512], F32, 2)
                            tbb = st.rot("tbb", [128, 512], F32, 2)
                            for tc in range(16):
                                mm(bank(0), h3[:, tc * 128:(tc + 1) * 128], w4[:, cb * 512:(cb + 1) * 512], True, True, r=[b_h3, b_f], w=[b_ps[0]])
                                mm(bank(1), h3[:, tc * 128:(tc + 1) * 128], w4[:, 1024 + cb * 512:1024 + (cb + 1) * 512], True, True, r=[b_h3, b_f], w=[b_ps[1]])
                                wt_, b_w = wn.next()
                                kb.dma(wt_[:], c_win[tc, :, cb * 512:(cb + 1) * 512], w=[b_w])
                                a_, b_a = ta.next()
                                b__, b_b = tbb.next()
                                kb.op("dve", lambda e: e.tensor_tensor(out=a_[:], in0=bank(0), in1=wt_[:], op=ALU.mult), r=[b_ps[0], b_w], w=[b_a])
                                kb.op("dve", lambda e: e.tensor_tensor(out=b__[:], in0=bank(1), in1=wt_[:], op=ALU.mult), r=[b_ps[1], b_w], w=[b_b])
                                if tc == 0:
                                    kb.op("dve", lambda e: e.tensor_scalar(out=b__[:], in0=b__[:], scalar1=m0[:, 0:1], scalar2=None, op0=ALU.mult), r=[b_b, b_c], w=[b_b])
                                kb.op("pool", lambda e: e.tensor_tensor(out=kp[:, tc, :], in0=a_[:], in1=b__[:], op=ALU.add), r=[b_a, b_b], w=[b_kp])
                                kb.op("pool", lambda e: e.tensor_tensor(out=km[:, tc, :], in0=a_[:], in1=b__[:], op=ALU.subtract), r=[b_a, b_b], w=[b_km])
                            ftr = st.rot("ft", [128, 16, 128], BF16, 3)
                            ko = st.rot("ko", [128, 512], F32, 3)
                            for j in range(16):
                                ftc, b_fc = ftr.next()
                                kb.dma(ftc[:], c_FT[j], w=[b_fc])
                                fts, b_fs = ftr.next()
                                kb.dma(fts[:], c_FT[16 + j], w=[b_fs])
                                for tc in range(16):
                                    mm(bank(2), ftc[:, tc, :], kp[:, tc, :], tc == 0, tc == 15, r=[b_fc, b_kp], w=[b_ps[2]])
                                for tc in range(16):
                                    mm(bank(3), fts[:, tc, :], km[:, tc, :], tc == 0, tc == 15, r=[b_fs, b_km], w=[b_ps[3]])
                                ka_, b_ka = ko.next()
                                kb.op("act", lambda e: e.copy(out=ka_[:], in_=bank(2)), r=[b_ps[2]], w=[b_ka])
                                kb.dma(KA[j, :, cb * 512:(cb + 1) * 512], ka_[:], r=[b_ka], w=[b_K])
                                kb_, b_kb = ko.next()
                                if j == 0:
                                    kb.op("dve", lambda e: e.tensor_scalar(out=kb_[:], in0=bank(3), scalar1=m0[:, 0:1], scalar2=None, op0=ALU.mult), r=[b_ps[3], b_c], w=[b_kb])
                                else:
                                    kb.op("dve", lambda e: e.tensor_copy(out=kb_[:], in_=bank(3)), r=[b_ps[3]], w=[b_kb])
                                kb.dma(KBt[j, :, cb * 512:(cb + 1) * 512], kb_[:], r=[b_kb], w=[b_K])
                                if j == 0:
                                    for tc in range(16):
                                        mm(bank(4), fts[:, tc, :], kp[:, tc, :], tc == 0, tc == 15, r=[b_fs, b_kp], w=[b_ps[4]])
                                    kd_, b_kd = ko.next()
                                    kb.op("dve", lambda e: e.tensor_scalar(out=kd_[:], in0=bank(4), scalar1=m0[:, 1:2], scalar2=None, op0=ALU.mult), r=[b_ps[4], b_c], w=[b_kd])
                                    kb.op("dve", lambda e: e.scalar_tensor_tensor(out=kd_[:], in0=ka_[:], scalar=m0[:, 0:1], in1=kd_[:], op0=ALU.mult, op1=ALU.add),
                                          r=[b_ka, b_kd, b_c], w=[b_kd])
                                    kb.dma(KD0[:, cb * 512:(cb + 1) * 512], kd_[:], r=[b_kd], w=[b_K])

            def out_proj(s, w_t):
                with Stage(kb) as st:
                    ot = st.sb("oTs", [128, KC, L], BF16)[0]
                    b_ot = [kb.buf("oTs") for _ in range(KC)]
                    for c in range(KC):
                        kb.dma(ot[:, c, :], oT[s, c * 128:(c + 1) * 128, :], r=[b_oT[s][c]], w=[b_ot[c]])
                    wr = st.rot("wo", [128, KC, 128], BF16, 2)
                    xr = st.rot("xr", [128, L], F32, 2)
                    for oc in range(16):
                        lin_fm(wr, w_t[oc], ot, b_ot, KC, pbase=4 * (oc % 2))
                        residual_add(xr, s, oc, pbase=4 * (oc % 2))

            def mlp(layer, s):
                with Stage(kb) as sh:
                    hT = sh.sb("hT", [128, KC, L], BF16)[0]
                    b_hT = [kb.buf("hT") for _ in range(KC)]
                    norm_to_hT(s, 3, layer, hT, b_hT)
                    with Stage(kb) as st:
                        hid = st.sb("hid", [128, 32, 1024], BF16)[0]
                        b_hid = [kb.buf("hid") for _ in range(32)]
                        w1r = st.rot("w1t", [128, KC, 128], BF16, 4)
                        w2r = st.rot("w2t", [128, 32, 128], BF16, 3)
                        tmp = st.rot("mt", [128, 512], F32, 3)
                        xr = st.rot("mxr", [128, 512], F32, 4)
                        nb1 = 0
                        nb2 = 0
                        for tt in range(2):
                            for half in range(2):
                                for hcl in range(32):
                                    hc = half * 32 + hcl
                                    w1_, b_w1 = w1r.next()
                                    kb.dma(w1_[:], mlp_w1_t[layer, hc], w=[b_w1], q="pool")
                                    pbs = [(nb1 + t2) % 4 for t2 in range(2)]
                                    nb1 += 2
                                    for kc in range(KC):
                                        for t2 in range(2):
                                            tok = tt * 1024 + t2 * 512
                                            mm(bank(pbs[t2]), w1_[:, kc, :], hT[:, kc, tok:tok + 512], kc == 0, kc == KC - 1, r=[b_w1, b_hT[kc]], w=[b_ps[pbs[t2]]])
                                    for t2 in range(2):
                                        t_, b_t = tmp.next()
                                        kb.op("dve", lambda e: e.tensor_scalar_max(out=t_[:], in0=bank(pbs[t2]), scalar1=0.0), r=[b_ps[pbs[t2]]], w=[b_t])
                                        kb.op("act", lambda e: e.activation(out=hid[:, hcl, t2 * 512:(t2 + 1) * 512], in_=t_[:], func=AF.Square), r=[b_t], w=[b_hid[hcl]])
                                for oc in range(16):
                                    w2_, b_w2 = w2r.next()
                                    for q2 in range(2):
                                        kb.dma(w2_[:, q2 * 16:(q2 + 1) * 16, :], mlp_w2_t[layer, oc, :, half * 32 + q2 * 16:half * 32 + (q2 + 1) * 16, :], w=[b_w2], q="pool")
                                    pbs = [4 + (nb2 + t2) % 4 for t2 in range(2)]
                                    nb2 += 2
                                    for k_ in range(32):
                                        for t2 in range(2):
                                            mm(bank(pbs[t2]), w2_[:, k_, :], hid[:, k_, t2 * 512:(t2 + 1) * 512], k_ == 0, k_ == 31, r=[b_w2, b_hid[k_]], w=[b_ps[pbs[t2]]])
                                    for t2 in range(2):
                                        tb = tt * 2 + t2
                                        x_, b_x = xr.next()
                                        kb.dma(x_[:], xT[s, oc * 128:(oc + 1) * 128, tb * 512:(tb + 1) * 512], r=[b_xT[s][tb][oc]], w=[b_x])
                                        kb.op("dve", lambda e: e.tensor_tensor(out=x_[:], in0=x_[:], in1=bank(pbs[t2]), op=ALU.add), r=[b_x, b_ps[pbs[t2]]], w=[b_x])
                                        kb.dma(xT[s, oc * 128:(oc + 1) * 128, tb * 512:(tb + 1) * 512], x_[:], r=[b_x], w=[b_xT[s][tb][oc]])

            def cross_attn(layer, s):
                with Stage(kb) as so:
                    kTm, b_kTm = so.sb("kTm", [128, 4, MEM], BF16)
                    vm, b_vm = so.sb("vm", [128, 2, 512], BF16)
                    ocaT = so.sb("ocaT", [128, 4, L], BF16)[0]
                    b_oca = [kb.buf("oca") for _ in range(4)]
                    with Stage(kb) as st:
                        memT = st.sb("memT", [128, KC, MEM], BF16)[0]
                        b_memT = [kb.buf("memT") for _ in range(KC)]
                        mrow = st.rot("mrow", [128, D], F32, 2)
                        msq, b_msq = st.sb("msq", [128, D], F32)
                        mb = st.rot("mb", [128, D], BF16, 2)
                        col = st.rot("mcol", [128, 1], F32, 2)
                        for mt in range(2):
                            mr, b_mr = mrow.next()
                            kb.dma(mr[:], mem[s, mt * 128:(mt + 1) * 128, :], w=[b_mr])
                            kb.op("dve", lambda e: e.tensor_tensor(out=msq[:], in0=mr[:], in1=mr[:], op=ALU.mult), r=[b_mr], w=[b_msq])
                            cl, b_cl = col.next()
                            kb.op("dve", lambda e: e.reduce_sum(out=cl[:], in_=msq[:], axis=AX.X), r=[b_msq], w=[b_cl])
                            rstd_inplace(cl[:], b_cl, cl[:], [b_cl], 1.0 / D)
                            mb_, b_mb = mb.next()
                            kb.op("dve", lambda e: e.tensor_scalar(out=mb_[:], in0=mr[:], scalar1=cl[:, 0:1], scalar2=None, op0=ALU.mult), r=[b_mr, b_cl], w=[b_mb])
                            for hf in range(2):
                                pb = 4 + hf
                                for j in range(8):
                                    c = hf * 8 + j
                                    kb.op("pe", lambda e: e.transpose(out=bank_bf(pb)[:, j * 128:(j + 1) * 128], in_=mb_[:, c * 128:(c + 1) * 128], identity=identb[:]),
                                          r=[b_mb, b_c], w=[b_ps[pb]], inc=(j == 7))
                                for j in range(8):
                                    c = hf * 8 + j
                                    kb.op("dve", lambda e: e.tensor_scalar(out=memT[:, c, mt * 128:(mt + 1) * 128], in0=bank_bf(pb)[:, j * 128:(j + 1) * 128],
                                                                           scalar1=gains[:, 2, layer, c:c + 1], scalar2=None, op0=ALU.mult), r=[b_ps[pb], b_c], w=[b_memT[c]])
                        wr = st.rot("cwk", [128, KC, 128], BF16, 2)
                        kf, b_kf = st.sb("kf", [128, MEM], F32)
                        ksq, b_ksq = st.sb("ksq", [128, MEM], BF16)
                        krs, b_krs = st.sb("krs", [128, MEM], F32)
                        for h in range(4):
                            wt, b_wt = wr.next()
                            kb.dma(wt[:], ca_wk_t[layer, h], w=[b_wt], q="pool")
                            for kc in range(KC):
                                mm(bank(0)[:, 0:MEM], wt[:, kc, :], memT[:, kc, :], kc == 0, kc == KC - 1, r=[b_wt, b_memT[kc]], w=[b_ps[0]])
                            kb.op("act", lambda e: e.copy(out=kf[:], in_=bank(0)[:, 0:MEM]), r=[b_ps[0]], w=[b_kf])
                            kb.op("act", lambda e: e.activation(out=ksq[:], in_=bank(0)[:, 0:MEM], func=AF.Square), r=[b_ps[0]], w=[b_ksq])
                            mm(bank(1)[:, 0:MEM], onesb[:], ksq[:], True, True, r=[b_ksq, b_c], w=[b_ps[1]])
                            rstd_inplace(krs[:], b_krs, bank(1)[:, 0:MEM], [b_ps[1]], 1.0 / 128)
                            kb.op("dve", lambda e: e.scalar_tensor_tensor(out=kTm[:, h, :], in0=kf[:], scalar=cac[:, 1, layer:layer + 1], in1=krs[:], op0=ALU.mult, op1=ALU.mult),
                                  r=[b_kf, b_krs, b_c], w=[b_kTm])
                        wv, b_wv = st.sb("cwv", [128, KC, 512], BF16)
                        kb.dma(wv[:], ca_wv_t[layer], w=[b_wv], q="pool")
                        for mt in range(2):
                            for kc in range(KC):
                                mm(bank(2 + mt), memT[:, kc, mt * 128:(mt + 1) * 128], wv[:, kc, :], kc == 0, kc == KC - 1, r=[b_wv, b_memT[kc]], w=[b_ps[2 + mt]])
                            evac(mt, vm[:, mt, :], bank(2 + mt), [b_ps[2 + mt]], [b_vm])
                    scale = 128 ** -0.5
                    with Stage(kb) as sh:
                        hT = sh.sb("hT", [128, KC, L], BF16)[0]
                        b_hT = [kb.buf("hT") for _ in range(KC)]
                        norm_to_hT(s, 1, layer, hT, b_hT)
                        with Stage(kb) as st:
                            wr = st.rot("cwq", [128, KC, 128], BF16, 2)
                            qf = st.rot("cqf", [128, L], F32, 1)
                            pool = {"sqb": st.rot("csqb", [128, L], BF16, 1), "rs": st.rot("crs", [128, L], F32, 1)}
                            qn = st.rot("cqn", [128, L], BF16, 2)
                            pt = st.rot("cpt", [128, 512], BF16, 3)
                            rc = st.rot("crc", [128, 512], F32, 2)
                            for h in range(4):
                                lin_fm(wr, ca_wq_t[layer, h], hT, b_hT, KC)
                                qf_, b_qf = qf.next()
                                kb.op("act", lambda e: e.copy(out=qf_[:].rearrange("p (b t) -> p b t", b=4), in_=psA4), r=bA, w=[b_qf])
                                rs, b_rs = rmsnorm_fm(pool, qf_[:], b_qf, onesb[:], 1.0 / 128, None, None, None)
                                qn_, b_qn = qn.next()
                                kb.op("dve", lambda e: e.scalar_tensor_tensor(out=qn_[:], in0=qf_[:], scalar=cac[:, 0, layer:layer + 1], in1=rs[:], op0=ALU.mult, op1=ALU.mult),
                                      r=[b_qf, b_rs, b_c], w=[b_qn])
                                for qb in range(4):
                                    for mc in range(2):
                                        sbk = 4 + mc
                                        mm(bank(sbk), kTm[:, h, mc * 128:(mc + 1) * 128], qn_[:, qb * 512:(qb + 1) * 512], True, True, r=[b_kTm, b_qn], w=[b_ps[sbk]])
                                        p_, b_p = pt.next()
                                        kb.op("act", lambda e: e.activation(out=p_[:], in_=bank(sbk), func=AF.Exp, scale=scale), r=[b_ps[sbk]], w=[b_p])
                                        mm(bank(6), vm[:, mc, h * 128:(h + 1) * 128], p_[:], mc == 0, mc == 1, r=[b_vm, b_p], w=[b_ps[6]])
                                        mm(bank(7), onesb[:], p_[:], mc == 0, mc == 1, r=[b_c, b_p], w=[b_ps[7]])
                                    rc_, b_rc = rc.next()
                                    kb.op("dve", lambda e: e.reciprocal(out=rc_[:], in_=bank(7)), r=[b_ps[7]], w=[b_rc])
                                    kb.op("dve", lambda e: e.tensor_tensor(out=ocaT[:, h, qb * 512:(qb + 1) * 512], in0=bank(6), in1=rc_[:], op=ALU.mult),
                                          r=[b_ps[6], b_rc], w=[b_oca[h]])
                    with Stage(kb) as st:
                        wr = st.rot("cwo", [128, 4, 128], BF16, 2)
                        xr = st.rot("xr", [128, L], F32, 2)
                        for oc in range(16):
                            lin_fm(wr, ca_wo_t[layer, oc], ocaT, b_oca, 4, pbase=4 * (oc % 2))
                            residual_add(xr, s, oc, pbase=4 * (oc % 2))

            def view4(ap):
                return ap.rearrange("p (b t) -> p b t", b=4)


            def even_mixer(s):
                with Stage(kb) as so:
                    vx2T = [so.sb("vx2T%d" % cb, [128, 16, 512], BF16) for cb in range(2)]
                    with Stage(kb) as sh:
                        hT = sh.sb("hT", [128, KC, L], BF16)[0]
                        b_hT = [kb.buf("hT") for _ in range(KC)]
                        norm_to_hT(s, 0, 0, hT, b_hT)
                        with Stage(kb) as st:
                            wr = st.rot("hw", [128, KC, 128], BF16, 2)
                            upr = st.rot("up", [128, L + 2], F32, 2)
                            for u_, b_u in upr.items:
                                kb.op("pool", lambda e: e.memset(u_[:, 0:1], 0.0), w=[b_u])
                                kb.op("pool", lambda e: e.memset(u_[:, L + 1:L + 2], 0.0), w=[b_u])
                            accr = st.rot("acc", [128, L], F32, 4)
                            vbr = st.rot("vx2b", [128, L], BF16, 2)
                            for cb in range(2):
                                for cc in range(4):
                                    ch8 = cb * 4 + cc
                                    res = {}
                                    for gi in (1, 2, 0):
                                        ti = gi * 8 + ch8
                                        lin_fm(wr, ev_w_in_t[ti], hT, b_hT, KC)
                                        up, b_up = upr.next()
                                        kb.op("act", lambda e: e.copy(out=view4(up[:, 1:L + 1]), in_=psA4), r=bA, w=[b_up])
                                        acc, b_acc = accr.next()
                                        kb.op("dve", lambda e: e.tensor_scalar(out=acc[:], in0=up[:, 1:L + 1], scalar1=hycw[:, 1, ti:ti + 1], scalar2=hycb[:, ti:ti + 1],
                                                                               op0=ALU.mult, op1=ALU.add), r=[b_up, b_c], w=[b_acc])
                                        kb.op("dve", lambda e: e.scalar_tensor_tensor(out=acc[:], in0=up[:, 0:L], scalar=hycw[:, 0, ti:ti + 1], in1=acc[:],
                                                                                       op0=ALU.mult, op1=ALU.add), r=[b_up, b_c, b_acc], w=[b_acc])
                                        kb.op("dve", lambda e: e.scalar_tensor_tensor(out=acc[:], in0=up[:, 2:L + 2], scalar=hycw[:, 2, ti:ti + 1], in1=acc[:],
                                                                                      op0=ALU.mult, op1=ALU.add), r=[b_up, b_c, b_acc], w=[b_acc])
                                        res[gi] = (acc, b_acc)
                                    x1c, b_x1 = res[0]
                                    x2c, b_x2 = res[1]
                                    vc, b_v = res[2]
                                    kb.dma(hx1[s, ch8 * 128:(ch8 + 1) * 128, :], x1c[:], r=[b_x1], w=[b_scr["hx1"][s]])
                                    kb.op("pool", lambda e: e.tensor_tensor(out=vc[:], in0=vc[:], in1=x2c[:], op=ALU.mult), r=[b_v, b_x2], w=[b_v])
                                    kb.dma(hvx[s, ch8 * 128:(ch8 + 1) * 128, :], vc[:], r=[b_v], w=[b_scr["hvx"][s]])
                                    vb_, b_vb = vbr.next()
                                    kb.op("act", lambda e: e.copy(out=vb_[:], in_=vc[:]), r=[b_v], w=[b_vb])
                                    for hf in range(2):
                                        pb = 4 + hf
                                        for j in range(8):
                                            tc = hf * 8 + j
                                            kb.op("pe", lambda e: e.transpose(out=bank_bf(pb)[:, j * 128:(j + 1) * 128], in_=vb_[:, tc * 128:(tc + 1) * 128], identity=identb[:]),
                                                  r=[b_vb, b_c], w=[b_ps[pb]], inc=(j == 7))
                                        evac(hf, vx2T[cb][0][:, hf * 8:(hf + 1) * 8, cc * 128:(cc + 1) * 128],
                                             bank_bf(pb).rearrange("p (j c) -> p j c", j=8), [b_ps[pb]], [vx2T[cb][1]])
                        with Stage(kb) as st:
                            wr = st.rot("dw", [128, KC, 128], BF16, 2)
                            qf = st.rot("dqf", [128, L], F32, 2)
                            pool_ = {"sqb": st.rot("dsqb", [128, L], BF16, 1), "rs": st.rot("drs", [128, L], F32, 1)}
                            qn = st.rot("dqn", [128, L], BF16, 2)
                            for which, dst, nm in ((0, dq, "dq"), (1, dk, "dk")):
                                for h in range(8):
                                    lin_fm(wr, ev_w_in_t[24 + which * 8 + h], hT, b_hT, KC)
                                    qf_, b_qf = qf.next()
                                    kb.op("act", lambda e: e.copy(out=view4(qf_[:]), in_=psA4), r=bA, w=[b_qf])
                                    rs, b_rs = rmsnorm_fm(pool_, qf_[:], b_qf, bd64b[:], 1.0 / 64, None, None, None)
                                    qn_, b_qn = qn.next()
                                    kb.op("dve", lambda e: e.scalar_tensor_tensor(out=qn_[:], in0=qf_[:], scalar=dfc[:, which:which + 1], in1=rs[:], op0=ALU.mult, op1=ALU.mult),
                                          r=[b_qf, b_rs, b_c], w=[b_qn])
                                    kb.dma(dst[s, h * 128:(h + 1) * 128, :], qn_[:], r=[b_qn], w=[b_scr[nm][s]])
                            wv = st.rot("dwv", [128, KC, 512], BF16, 1)
                            vrow = st.rot("dvrow", [128, 512], BF16, 3)
                            for vb in range(2):
                                wv_, b_wv = wv.next()
                                for j4 in range(4):
                                    kb.dma(wv_[:, :, j4 * 128:(j4 + 1) * 128], ev_w_in_t[40 + vb * 4 + j4], w=[b_wv], q="pool")
                                for tt in range(16):
                                    pb = 4 + tt % 4
                                    for kc in range(KC):
                                        mm(bank(pb), hT[:, kc, tt * 128:(tt + 1) * 128], wv_[:, kc, :], kc == 0, kc == KC - 1, r=[b_wv, b_hT[kc]], w=[b_ps[pb]])
                                    vr, b_vr = vrow.next()
                                    evac(tt, vr[:], bank(pb), [b_ps[pb]], [b_vr])
                                    kb.dma(dv[s, tt * 128:(tt + 1) * 128, vb * 512:(vb + 1) * 512], vr[:], r=[b_vr], w=[b_scr["dv"][s]])
                    with Stage(kb) as st:
                        ftr = st.rot("ft", [128, 16, 128], BF16, 4)
                        kt = st.rot("kt", [128, 512], F32, 6)
                        tmpr = st.rot("ht", [128, 512], F32, 4)
                        Yf, b_Yf = st.sb("Yf", [128, 32, 512], BF16)
                        gtr = st.rot("gt", [128, 16, 512], BF16, 2)
                        epr = st.rot("ep", [128, 512], F32, 4)
                        obr = st.rot("ob", [128, 512], BF16, 2)
                        for cb in range(2):
                            vt, b_vt = vx2T[cb]
                            for j in range(16):
                                ftc, b_fc = ftr.next()
                                kb.dma(ftc[:], c_FT[j], w=[b_fc])
                                fts, b_fs = ftr.next()
                                kb.dma(fts[:], c_FT[16 + j], w=[b_fs])
                                pa = (j % 2) * 2
                                pbk = pa + 1
                                for tc in range(16):
                                    mm(bank(pa), ftc[:, tc, :], vt[:, tc, :], tc == 0, tc == 15, r=[b_fc, b_vt], w=[b_ps[pa]])
                                for tc in range(16):
                                    mm(bank(pbk), fts[:, tc, :], vt[:, tc, :], tc == 0, tc == 15, r=[b_fs, b_vt], w=[b_ps[pbk]])
                                ka_, b_ka = kt.next()
                                kb.dma(ka_[:], KA[j, :, cb * 512:(cb + 1) * 512], r=[b_K], w=[b_ka])
                                kb_, b_kb = kt.next()
                                kb.dma(kb_[:], KBt[j, :, cb * 512:(cb + 1) * 512], r=[b_K], w=[b_kb])
                                if j == 0:
                                    kd_, b_kd = kt.next()
                                    kb.dma(kd_[:], KD0[:, cb * 512:(cb + 1) * 512], r=[b_K], w=[b_kd])
                                else:
                                    kd_, b_kd = ka_, b_ka
                                t1, b_t1 = tmpr.next()
                                t2, b_t2 = tmpr.next()
                                kb.op("dve", lambda e: e.tensor_tensor(out=t1[:], in0=bank(pa), in1=ka_[:], op=ALU.mult), r=[b_ps[pa], b_ka], w=[b_t1])
                                kb.op("dve", lambda e: e.tensor_tensor(out=t2[:], in0=bank(pbk), in1=kb_[:], op=ALU.mult), r=[b_ps[pbk], b_kb], w=[b_t2])
                                kb.op("pool", lambda e: e.tensor_tensor(out=Yf[:, j, :], in0=t1[:], in1=t2[:], op=ALU.subtract), r=[b_t1, b_t2], w=[b_Yf])
                                t3, b_t3 = tmpr.next()
                                t4, b_t4 = tmpr.next()
                                kb.op("dve", lambda e: e.tensor_tensor(out=t3[:], in0=bank(pa), in1=kb_[:], op=ALU.mult), r=[b_ps[pa], b_kb], w=[b_t3])
                                kb.op("dve", lambda e: e.tensor_tensor(out=t4[:], in0=bank(pbk), in1=kd_[:], op=ALU.mult), r=[b_ps[pbk], b_kd], w=[b_t4])
                                kb.op("pool", lambda e: e.tensor_tensor(out=Yf[:, 16 + j, :], in0=t3[:], in1=t4[:], op=ALU.add), r=[b_t3, b_t4], w=[b_Yf])
                            for tb in range(4):
                                for hf in range(2):
                                    gt, b_gt = gtr.next()
                                    kb.dma(gt[:], c_G[tb, hf], w=[b_gt])
                                    for cc in range(4):
                                        for fj in range(16):
                                            mm(bank(4 + cc), Yf[:, hf * 16 + fj, cc * 128:(cc + 1) * 128], gt[:, fj, :], hf == 0 and fj == 0, hf == 1 and fj == 15,
                                               r=[b_Yf, b_gt], w=[b_ps[4 + cc]], inc=(fj == 15))
                                for cc in range(4):
                                    ch8 = cb * 4 + cc
                                    vxb, b_vxb = epr.next()
                                    x1b, b_x1b = epr.next()
                                    kb.dma(vxb[:], hvx[s, ch8 * 128:(ch8 + 1) * 128, tb * 512:(tb + 1) * 512], r=[b_scr["hvx"][s]], w=[b_vxb])
                                    kb.dma(x1b[:], hx1[s, ch8 * 128:(ch8 + 1) * 128, tb * 512:(tb + 1) * 512], r=[b_scr["hx1"][s]], w=[b_x1b])
                                    kb.op("dve", lambda e: e.scalar_tensor_tensor(out=vxb[:], in0=vxb[:], scalar=hyb[:, ch8:ch8 + 1], in1=bank(4 + cc), op0=ALU.mult, op1=ALU.add),
                                          r=[b_vxb, b_c, b_ps[4 + cc]], w=[b_vxb])
                                    ob, b_ob = obr.next()
                                    kb.op("pool", lambda e: e.tensor_tensor(out=ob[:], in0=vxb[:], in1=x1b[:], op=ALU.mult), r=[b_vxb, b_x1b], w=[b_ob])
                                    kb.dma(oT[s, ch8 * 128:(ch8 + 1) * 128, tb * 512:(tb + 1) * 512], ob[:], r=[b_ob], w=[b_oT[s][ch8]])
                scale = 64 ** -0.5
                with Stage(kb) as st:
                    base, b_base = st.sb("alibi", [128, 3968], F32)
                    kb.dma(base[:], c_alibi[:, :], w=[b_base])
                    qzr = Rot([(st.sb("aqz0%d" % i, [128, L], BF16), st.sb("aqz1%d" % i, [128, L], BF16)) for i in range(2)])
                    for (z0, b_z0), (z1, b_z1) in qzr.items:
                        kb.op("pool", lambda e: e.memset(z0[64:128, :], 0.0), w=[b_z0])
                        kb.op("pool", lambda e: e.memset(z1[0:64, :], 0.0), w=[b_z1])
                    kr = st.rot("ak", [128, L], BF16, 2)
                    vr = st.rot("av", [128, 16, 128], BF16, 2)
                    tmpr = st.rot("as", [128, 512], BF16, 3)
                    ptr = st.rot("ap", [128, 512], BF16, 3)
                    etr = st.rot("aet", [128, 3968], BF16, 2)
                    rcr = st.rot("arc", [128, 512], F32, 2)
                    rr = st.rot("ar", [128, 512], F32, 4)
                    sqr = st.rot("asq", [128, 512], BF16, 2)
                    rsr = st.rot("ars", [128, 512], F32, 2)
                    obr = st.rot("aob", [128, 512], BF16, 2)
                    LA = 2
                    iters = [(h, qb, j, kc) for h in range(8) for qb in range(4) for j in range(2) for kc in range(16)]
                    heads = {}
                    etabs = {}
                    rj = {}

                    def emit_S(i):
                        h, qb, j, kc = iters[i]
                        if h not in heads:
                            (z0, b_z0), (z1, b_z1) = qzr.next()
                            kT_, b_k = kr.next()
                            v_, b_v = vr.next()
                            kb.dma(z0[0:64, :], dq[s, h * 128:h * 128 + 64, :], r=[b_scr["dq"][s]], w=[b_z0])
                            kb.dma(z1[64:128, :], dq[s, h * 128 + 64:(h + 1) * 128, :], r=[b_scr["dq"][s]], w=[b_z1])
                            qT_, b_q = (z0, z1), (b_z0, b_z1)
                            et_, b_et = etr.next()
                            kb.op("act", lambda e: e.activation(out=et_[:], in_=base[:], func=AF.Exp, scale=-(2.0 ** (-(h + 1)))), r=[b_base], w=[b_et])
                            etabs[h] = (et_, b_et)
                            kb.dma(kT_[:], dk[s, h * 128:(h + 1) * 128, :], r=[b_scr["dk"][s]], w=[b_k])
                            kb.dma(v_[:], dv[s].rearrange("(tc p) c -> p tc c", p=128)[:, :, h * 128:(h + 1) * 128], r=[b_scr["dv"][s]], w=[b_v])
                            heads[h] = (qT_, b_q, kT_, b_k, v_, b_v)
                        qT_, b_q, kT_, b_k, v_, b_v = heads[h]
                        sb_ = i % 3
                        mm(bank(sb_), kT_[:, kc * 128:(kc + 1) * 128], qT_[j][:, qb * 512:(qb + 1) * 512], True, True,
                           r=[b_k, b_q[j]], w=[b_ps[sb_]])

                    def emit_rest(i):
                        h, qb, j, kc = iters[i]
                        slope = 2.0 ** (-(h + 1))
                        qT_, b_q, kT_, b_k, v_, b_v = heads[h]
                        sb_ = i % 3
                        off = qb * 512 - kc * 128 + 1920
                        et_, b_et = etabs[h]
                        t_, b_t = tmpr.next()
                        kb.op("act", lambda e: e.activation(out=t_[:], in_=bank(sb_), func=AF.Exp, scale=scale), r=[b_ps[sb_]], w=[b_t])
                        p_, b_p = ptr.next()
                        kb.op("dve", lambda e: e.tensor_tensor(out=p_[:], in0=t_[:], in1=et_[:, off:off + 512], op=ALU.mult), r=[b_t, b_et], w=[b_p])
                        mm(bank(4 + j), v_[:, kc, :], p_[:], kc == 0, kc == 15, r=[b_v, b_p], w=[b_ps[4 + j]])
                        mm(bank(6 + j), onesb[:], p_[:], kc == 0, kc == 15, r=[b_c, b_p], w=[b_ps[6 + j]])
                        if kc != 15:
                            return
                        rc_, b_rc = rcr.next()
                        kb.op("dve", lambda e: e.reciprocal(out=rc_[:], in_=bank(6 + j)), r=[b_ps[6 + j]], w=[b_rc])
                        r_, b_r = rr.next()
                        kb.op("dve", lambda e: e.tensor_tensor(out=r_[:], in0=bank(4 + j), in1=rc_[:], op=ALU.mult), r=[b_ps[4 + j], b_rc], w=[b_r])
                        rj[j] = (r_, b_r)
                        if j != 1:
                            return
                        (r0, b_r0), (r1, b_r1) = rj[0], rj[1]
                        kb.op("dve", lambda e: e.scalar_tensor_tensor(out=r0[:], in0=r1[:], scalar=neglam[:, 0:1], in1=r0[:], op0=ALU.mult, op1=ALU.add),
                              r=[b_r0, b_r1, b_c], w=[b_r0])
                        sq_, b_sq = sqr.next()
                        kb.op("pool", lambda e: e.tensor_tensor(out=sq_[:], in0=r0[:], in1=r0[:], op=ALU.mult), r=[b_r0], w=[b_sq])
                        mm(bank(3), onesb[:], sq_[:], True, True, r=[b_c, b_sq], w=[b_ps[3]])
                        rs_, b_rs = rsr.next()
                        kb.op("dve", lambda e: e.tensor_scalar(out=rs_[:], in0=bank(3), scalar1=1.0 / 128, scalar2=EPS, op0=ALU.mult, op1=ALU.add), r=[b_ps[3]], w=[b_rs])
                        kb.op("act", lambda e: e.activation(out=rs_[:], in_=rs_[:], func=AF.Ln), r=[b_rs], w=[b_rs])
                        kb.op("act", lambda e: e.activation(out=rs_[:], in_=rs_[:], func=AF.Exp, scale=-0.5), r=[b_rs], w=[b_rs])
                        ob, b_ob = obr.next()
                        kb.op("dve", lambda e: e.scalar_tensor_tensor(out=ob[:], in0=r0[:], scalar=sg2[:, 0:1], in1=rs_[:], op0=ALU.mult, op1=ALU.mult),
                              r=[b_r0, b_rs, b_c], w=[b_ob])
                        kb.dma(oT[s, 1024 + h * 128:1024 + (h + 1) * 128, qb * 512:(qb + 1) * 512], ob[:], r=[b_ob], w=[b_oT[s][8 + h]])

                    n_it = len(iters)
                    for i in range(n_it + LA):
                        if i < n_it:
                            emit_S(i)
                        if i - LA >= 0:
                            emit_rest(i - LA)


            C.even_mixer = even_mixer


            def odd_mixer(s):
                with Stage(kb) as sh:
                    hT = sh.sb("hT", [128, KC, L], BF16)[0]
                    b_hT = [kb.buf("hT") for _ in range(KC)]
                    norm_to_hT(s, 0, 1, hT, b_hT)
                    with Stage(kb) as st:
                        cosT, b_cs = st.sb("cosT", [128, L], F32)
                        sinT, _ = st.sb("sinT", [128, L], F32)
                        kb.dma(cosT[:], c_cos[:, :], w=[b_cs])
                        kb.dma(sinT[:], c_sin[:, :], w=[b_cs])
                        wr = st.rot("gw", [128, KC, 128], BF16, 2)
                        qf = st.rot("gqf", [128, L], F32, 1)
                        qb16 = st.rot("gqb", [128, L], BF16, 1)
                        pool_ = {"sqb": st.rot("gsqb", [128, L], BF16, 1), "rs": st.rot("grs", [128, L], F32, 1)}
                        ta = st.rot("gta", [128, L], F32, 1)
                        tb_ = st.rot("gtb", [128, L], F32, 1)
                        qn = st.rot("gqn", [128, L], BF16, 2)
                        for hh in range(10):
                            isk = hh >= 8
                            ti = 40 + hh
                            lin_fm(wr, od_w_in_t[ti], hT, b_hT, KC)
                            qf_, b_qf = qf.next()
                            kb.op("act", lambda e: e.copy(out=view4(qf_[:]), in_=psA4), r=bA, w=[b_qf])
                            q16, b_q16 = qb16.next()
                            kb.op("act", lambda e: e.copy(out=view4(q16[:]), in_=psA4), r=bA, w=[b_q16])
                            rs, b_rs = rmsnorm_fm(pool_, qf_[:], b_qf, onesb[:], 1.0 / 128, None, None, None)
                            for tb in range(4):
                                mm(bank(4 + tb), RgT[:, 1 if isk else 0, :], q16[:, tb * 512:(tb + 1) * 512], True, True, r=[b_c, b_q16], w=[b_ps[4 + tb]])
                            a_, b_a = ta.next()
                            b__, b_b = tb_.next()
                            gcol = odc[:, 2:3] if isk else odc[:, 1:2]
                            kb.op("dve", lambda e: e.scalar_tensor_tensor(out=a_[:], in0=qf_[:], scalar=gcol, in1=cosT[:], op0=ALU.mult, op1=ALU.mult), r=[b_qf, b_c, b_cs], w=[b_a])
                            kb.op("dve", lambda e: e.tensor_tensor(out=view4(b__[:]), in0=psB4, in1=view4(sinT[:]), op=ALU.mult), r=bB + [b_cs], w=[b_b])
                            kb.op("pool", lambda e: e.tensor_tensor(out=a_[:], in0=a_[:], in1=b__[:], op=ALU.add), r=[b_a, b_b], w=[b_a])
                            qn_, b_qn = qn.next()
                            kb.op("dve", lambda e: e.tensor_tensor(out=qn_[:], in0=a_[:], in1=rs[:], op=ALU.mult), r=[b_a, b_rs], w=[b_qn])
                            if isk:
                                kb.dma(gk[s, (hh - 8) * 128:(hh - 7) * 128, :], qn_[:], r=[b_qn], w=[b_scr["gk"][s]])
                            else:
                                kb.dma(dq[s, hh * 128:(hh + 1) * 128, :], qn_[:], r=[b_qn], w=[b_scr["dq"][s]])
                        wv, b_wv = st.sb("gwv", [128, KC, 256], BF16)
                        for j4 in range(2):
                            kb.dma(wv[:, :, j4 * 128:(j4 + 1) * 128], od_w_in_t[50 + j4], w=[b_wv], q="pool")
                        vrow = st.rot("gvrow", [128, 256], BF16, 3)
                        for tt in range(16):
                            pb = 4 + tt % 4
                            for kc in range(KC):
                                mm(bank(pb)[:, 0:256], hT[:, kc, tt * 128:(tt + 1) * 128], wv[:, kc, :], kc == 0, kc == KC - 1, r=[b_wv, b_hT[kc]], w=[b_ps[pb]])
                            vr, b_vr = vrow.next()
                            evac(tt, vr[:], bank(pb)[:, 0:256], [b_ps[pb]], [b_vr])
                            kb.dma(gv[s, tt * 128:(tt + 1) * 128, :], vr[:], r=[b_vr], w=[b_scr["gv"][s]])
                        wi = st.rot("gwi", [128, KC, 512], BF16, 1)
                        irow = st.rot("girow", [128, 512], BF16, 3)
                        for vb in range(2):
                            wi_, b_wi = wi.next()
                            for j4 in range(4):
                                kb.dma(wi_[:, :, j4 * 128:(j4 + 1) * 128], od_w_in_t[24 + vb * 4 + j4], w=[b_wi], q="pool")
                            for tt in range(16):
                                pb = 4 + tt % 4
                                for kc in range(KC):
                                    mm(bank(pb), hT[:, kc, tt * 128:(tt + 1) * 128], wi_[:, kc, :], kc == 0, kc == KC - 1, r=[b_wi, b_hT[kc]], w=[b_ps[pb]])
                                ir, b_ir = irow.next()
                                evac(tt, ir[:], bank(pb), [b_ps[pb]], [b_ir])
                                kb.dma(dv[s, tt * 128:(tt + 1) * 128, vb * 512:(vb + 1) * 512], ir[:], r=[b_ir], w=[b_scr["dv"][s]])
                    with Stage(kb) as st:
                        msk, b_msk = st.sb("scanmask", [128, L], F32)
                        kb.op("pool", lambda e: e.memset(msk[:], 1.0), w=[b_msk])
                        kb.op("pool", lambda e: e.memset(msk[:].rearrange("p (c j) -> p c j", j=64)[:, :, 0:1], 0.0), w=[b_msk])
                        cm, b_cm = st.sb("cmask", [64, 2, 512], F32)
                        kb.dma(cm[:], c_mask[:, :, :], w=[b_cm])
                        wr = st.rot("hgw", [128, KC, 128], BF16, 2)
                        qs, b_qs = st.sb("hqs", [128, L], F32)
                        vtok, b_vtok = st.sb("hvtok", [64, 32, 128], BF16)
                        fa, b_fa = st.sb("hfa", [128, L], F32)
                        ka, b_kk = st.sb("hka", [128, L], F32)
                        Bc, b_B = st.sb("hB", [128, L], F32)
                        eb, b_eb = st.sb("heb", [128, L], F32)
                        enb, b_enb = st.sb("henb", [128, L], F32)
                        ebl, b_ebl = st.sb("hebl", [128, 32], F32)
                        qe, b_qe = st.sb("hqe", [128, L], BF16)
                        ke, b_ke = st.sb("hke", [128, L], BF16)
                        klT, b_klT = st.sb("hklT", [128, L], BF16)
                        kltok, b_kltok = st.sb("hkltok", [64, 32, 128], BF16)
                        attT, b_att = st.sb("hatt", [64, 32, 64], BF16)
                        sgt, b_sgt = enb, b_enb
                        Sbr = st.rot("hSb", [128, 128], BF16, 6)
                        oacc, b_oacc = st.sb("hoacc", [128, L], F32)
                        oaccb, b_oaccb = st.sb("hoaccb", [128, L], F32)
                        pool_ = {"sqb": st.rot("hsqb", [128, L], BF16, 1), "rs": st.rot("hrs", [128, L], F32, 1)}
                        ob, b_ob = st.sb("hob", [128, L], BF16)
                        c3 = lambda ap: ap.rearrange("p (c j) -> p c j", j=64)
                        for h in range(8):
                            kb.dma(vtok[:], dv[s].rearrange("(ch p) c -> p ch c", p=64)[:, :, h * 128:(h + 1) * 128], r=[b_scr["dv"][s]], w=[b_vtok])
                            lin_fm(wr, od_w_in_t[h], hT, b_hT, KC)
                            kb.op("act", lambda e: e.mul(out=view4(qs[:]), in_=psA4, mul=128 ** -0.5), r=bA, w=[b_qs])
                            for di in range(2):
                                lin_fm(wr, od_w_in_t[8 + di * 8 + h], hT, b_hT, KC)
                                kb.op("act", lambda e: e.activation(out=view4(fa[:]), in_=psA4, func=AF.Sigmoid), r=bA, w=[b_fa])
                                kb.op("dve", lambda e: e.tensor_scalar(out=ka[:], in0=fa[:], scalar1=noml[:, di, h:h + 1], scalar2=oml[:, di, h:h + 1], op0=ALU.mult, op1=ALU.add),
                                      r=[b_fa, b_c], w=[b_kk])
                                kb.op("act", lambda e: e.activation(out=fa[:], in_=fa[:], func=AF.Ln, scale=oml[:, di, h:h + 1], bias=lbs[:, di, h:h + 1]), r=[b_fa, b_c], w=[b_fa])
                                kb.op("dve", lambda e: e.tensor_tensor_scan(out=Bc[:], data0=msk[:], data1=fa[:], initial=0.0, op0=ALU.mult, op1=ALU.add), r=[b_msk, b_fa], w=[b_B])
                                if di == 1:
                                    kb.op("dve", lambda e: e.tensor_tensor(out=c3(eb[:]), in0=c3(Bc[:]), in1=c3(Bc[:])[:, :, 63:64].to_broadcast([128, 32, 64]), op=ALU.subtract),
                                          r=[b_B], w=[b_eb])
                                    kb.op("dve", lambda e: e.tensor_tensor(out=eb[:], in0=eb[:], in1=fa[:], op=ALU.subtract), r=[b_eb, b_fa], w=[b_eb])
                                    kb.op("act", lambda e: e.activation(out=enb[:], in_=eb[:], func=AF.Exp, scale=1.0), r=[b_eb], w=[b_enb])
                                    kb.op("act", lambda e: e.activation(out=eb[:], in_=eb[:], func=AF.Exp, scale=-1.0), r=[b_eb], w=[b_eb])
                                    last = 0
                                else:
                                    kb.op("act", lambda e: e.activation(out=eb[:], in_=Bc[:], func=AF.Exp, scale=1.0), r=[b_B], w=[b_eb])
                                    kb.op("act", lambda e: e.activation(out=enb[:], in_=Bc[:], func=AF.Exp, scale=-1.0), r=[b_B], w=[b_enb])
                                    last = 63
                                kb.op("pool", lambda e: e.tensor_copy(out=ebl[:], in_=c3(eb[:])[:, :, last]), r=[b_eb], w=[b_ebl])
                                kb.op("dve", lambda e: e.tensor_tensor(out=qe[:], in0=qs[:], in1=eb[:], op=ALU.mult), r=[b_qs, b_eb], w=[b_qe])
                                kb.op("dve", lambda e: e.tensor_tensor(out=ke[:], in0=ka[:], in1=enb[:], op=ALU.mult), r=[b_kk, b_enb], w=[b_ke])
                                kb.op("dve", lambda e: e.tensor_tensor(out=c3(klT[:]), in0=c3(ke[:]), in1=ebl[:].unsqueeze(2).to_broadcast([128, 32, 64]), op=ALU.mult),
                                      r=[b_ke, b_ebl], w=[b_klT])
                                for g4 in range(4):
                                    pb = 4 + g4
                                    for j in range(8):
                                        ch = g4 * 8 + j
                                        kb.op("pe", lambda e: e.transpose(out=bank_bf(pb)[0:64, j * 128:(j + 1) * 128], in_=klT[:, ch * 64:(ch + 1) * 64], identity=identb[:]),
                                              r=[b_klT, b_c], w=[b_ps[pb]], inc=(j == 7))
                                    evac(g4, kltok[:, g4 * 8:(g4 + 1) * 8, :], bank_bf(pb)[0:64, :].rearrange("p (j c) -> p j c", j=8), [b_ps[pb]], [b_kltok])
                                for g4 in range(4):
                                    pb = 4 + g4
                                    for j in range(8):
                                        ch = g4 * 8 + j
                                        mm(bank(pb)[0:64, j * 64:(j + 1) * 64], ke[:, ch * 64:(ch + 1) * 64], qe[:, ch * 64:(ch + 1) * 64], True, True,
                                           r=[b_ke, b_qe], w=[b_ps[pb]], inc=(j == 7))
                                    kb.op("dve", lambda e: e.tensor_tensor(out=attT[:, g4 * 8:(g4 + 1) * 8, :], in0=bank(pb)[0:64, :].rearrange("p (j c) -> p j c", j=8),
                                                                           in1=cm[:, di, :].rearrange("p (j c) -> p j c", j=8), op=ALU.mult), r=[b_ps[pb], b_cm], w=[b_att])
                                order = list(range(32)) if di == 0 else list(range(31, -1, -1))
                                Sb, b_Sb = Sbr.next()
                                kb.op("pool", lambda e: e.memset(Sb[:], 0.0), w=[b_Sb])
                                odst = oacc if di == 0 else oaccb
                                b_odst = b_oacc if di == 0 else b_oaccb
                                for gi in range(8):
                                    pb = 2 + gi % 2
                                    for j in range(4):
                                        ch = order[gi * 4 + j]
                                        mm(bank(pb)[:, j * 128:(j + 1) * 128], kltok[:, ch, :], vtok[:, ch, :], True, True, r=[b_kltok, b_vtok], w=[b_ps[pb]], inc=(j == 3))
                                    for j in range(4):
                                        i = gi * 4 + j
                                        ch = order[i]
                                        po = (i // 8) % 2
                                        sl = ch % 8
                                        mm(bank(po)[:, sl * 64:(sl + 1) * 64], Sb[:], qe[:, ch * 64:(ch + 1) * 64], True, False, r=[b_Sb, b_qe], w=[b_ps[po]], inc=False)
                                        mm(bank(po)[:, sl * 64:(sl + 1) * 64], vtok[:, ch, :], attT[:, ch, :], False, True, r=[b_vtok, b_att], w=[b_ps[po]])
                                        Sn, b_Sn = Sbr.next()
                                        kb.op("dve", lambda e: e.scalar_tensor_tensor(out=Sn[:], in0=Sb[:], scalar=ebl[:, ch:ch + 1], in1=bank(pb)[:, j * 128:(j + 1) * 128], op0=ALU.mult, op1=ALU.add),
                                              r=[b_Sb, b_ebl, b_ps[pb]], w=[b_Sn])
                                        Sb, b_Sb = Sn, b_Sn
                                        if i % 8 == 7:
                                            g8 = ch // 8
                                            kb.op("act", lambda e: e.copy(out=odst[:, g8 * 512:(g8 + 1) * 512], in_=bank(po)), r=[b_ps[po]], w=[b_odst])
                            kb.op("pool", lambda e: e.tensor_tensor(out=oacc[:], in0=oacc[:], in1=oaccb[:], op=ALU.add), r=[b_oacc, b_oaccb], w=[b_oacc])
                            lin_fm(wr, od_w_in_t[32 + h], hT, b_hT, KC)
                            kb.op("act", lambda e: e.activation(out=view4(sgt[:]), in_=psA4, func=AF.Silu), r=bA, w=[b_sgt])
                            rs, b_rs = rmsnorm_fm(pool_, oacc[:], b_oacc, onesb[:], 1.0 / 128, None, None, None)
                            kb.op("dve", lambda e: e.scalar_tensor_tensor(out=oacc[:], in0=oacc[:], scalar=odc[:, 0:1], in1=rs[:], op0=ALU.mult, op1=ALU.mult), r=[b_oacc, b_rs, b_c], w=[b_oacc])
                            kb.op("pool", lambda e: e.tensor_tensor(out=ob[:], in0=oacc[:], in1=sgt[:], op=ALU.mult), r=[b_oacc, b_sgt], w=[b_ob])
                            kb.dma(oT[s, h * 128:(h + 1) * 128, :], ob[:], r=[b_ob], w=[b_oT[s][h]])
                scale = 128 ** -0.5
                with Stage(kb) as st:
                    qr = st.rot("bq", [128, L], BF16, 2)
                    kr = st.rot("bk", [128, L], BF16, 2)
                    vr = st.rot("bv", [128, 16, 128], BF16, 2)
                    ptr = st.rot("bp", [128, 512], BF16, 3)
                    rcr = st.rot("brc", [128, 512], F32, 2)
                    obr = st.rot("bob", [128, 512], BF16, 2)
                    LA = 2
                    iters = [(hq, qb, kc) for hq in range(8) for qb in range(4) for kc in range(16)]
                    heads = {}
                    kvs = {}

                    def emit_S(i):
                        hq, qb, kc = iters[i]
                        g_ = hq // 4
                        if g_ not in kvs:
                            kT_, b_k = kr.next()
                            v_, b_v = vr.next()
                            kb.dma(kT_[:], gk[s, g_ * 128:(g_ + 1) * 128, :], r=[b_scr["gk"][s]], w=[b_k])
                            kb.dma(v_[:], gv[s].rearrange("(tc p) c -> p tc c", p=128)[:, :, g_ * 128:(g_ + 1) * 128], r=[b_scr["gv"][s]], w=[b_v])
                            kvs[g_] = (kT_, b_k, v_, b_v)
                        if hq not in heads:
                            qT_, b_q = qr.next()
                            kb.dma(qT_[:], dq[s, hq * 128:(hq + 1) * 128, :], r=[b_scr["dq"][s]], w=[b_q])
                            heads[hq] = (qT_, b_q)
                        kT_, b_k, v_, b_v = kvs[g_]
                        qT_, b_q = heads[hq]
                        sb_ = i % 3
                        mm(bank(sb_), kT_[:, kc * 128:(kc + 1) * 128], qT_[:, qb * 512:(qb + 1) * 512], True, True, r=[b_k, b_q], w=[b_ps[sb_]])

                    def emit_rest(i):
                        hq, qb, kc = iters[i]
                        g_ = hq // 4
                        kT_, b_k, v_, b_v = kvs[g_]
                        grp = hq * 4 + qb
                        po = 4 + grp % 2
                        pn = 6 + grp % 2
                        sb_ = i % 3
                        p_, b_p = ptr.next()
                        kb.op("act", lambda e: e.activation(out=p_[:], in_=bank(sb_), func=AF.Exp, scale=scale), r=[b_ps[sb_]], w=[b_p])
                        mm(bank(po), v_[:, kc, :], p_[:], kc == 0, kc == 15, r=[b_v, b_p], w=[b_ps[po]])
                        mm(bank(pn), onesb[:], p_[:], kc == 0, kc == 15, r=[b_c, b_p], w=[b_ps[pn]])
                        if kc != 15:
                            return
                        rc_, b_rc = rcr.next()
                        kb.op("dve", lambda e: e.reciprocal(out=rc_[:], in_=bank(pn)), r=[b_ps[pn]], w=[b_rc])
                        ob, b_ob = obr.next()
                        kb.op("dve", lambda e: e.tensor_tensor(out=ob[:], in0=bank(po), in1=rc_[:], op=ALU.mult), r=[b_ps[po], b_rc], w=[b_ob])
                        kb.dma(oT[s, 1024 + hq * 128:1024 + (hq + 1) * 128, qb * 512:(qb + 1) * 512], ob[:], r=[b_ob], w=[b_oT[s][8 + hq]])

                    n_it = len(iters)
                    for i in range(n_it + LA):
                        if i < n_it:
                            emit_S(i)
                        if i - LA >= 0:
                            emit_rest(i - LA)


            C.odd_mixer = odd_mixer

            for layer in range(2):
                for s in range(SPC):
                    if ("mix%d" % layer) in stages:
                        if layer == 0:
                            C.even_mixer(s)
                            out_proj(s, ev_w_out_t)
                        else:
                            C.odd_mixer(s)
                            out_proj(s, od_w_out_t)
                    if ("ca%d" % layer) in stages:
                        cross_attn(layer, s)
                    if ("mlp%d" % layer) in stages:
                        mlp(layer, s)

            if "epi" in stages:
                with Stage(kb) as st:
                    blk = st.rot("blk", [128, KC, 512], F32, 2)
                    yo = st.rot("yo", [128, D], F32, 2)
                    ev = 0
                    for s in range(SPC):
                        for tb in range(4):
                            bl, b_bl = blk.next()
                            kb.dma(bl[:], xTv[s][:, :, tb * 512:(tb + 1) * 512], r=xblk(s, tb), w=[b_bl])
                            for tt in range(4):
                                yt, b_yt = yo.next()
                                for j in range(4):
                                    for c4 in range(4):
                                        c = j * 4 + c4
                                        kb.op("pe", lambda e: e.transpose(out=bank(j)[:, c4 * 128:(c4 + 1) * 128],
                                                                          in_=bl[:, c, tt * 128:(tt + 1) * 128], identity=identf[:]),
                                              r=[b_bl, b_c], w=[b_ps[j]], inc=(c4 == 3))
                                    evac(ev, yt[:, j * 512:(j + 1) * 512], bank(j), [b_ps[j]], [b_yt])
                                    ev += 1
                                t0 = tb * 512 + tt * 128
                                kb.dma(y[s, t0:t0 + 128, :], yt[:], r=[b_yt], w=[])
        kb.final_wait()
    return C


def _tile_w(W, ncol):
    K, N = W.shape
    return np.ascontiguousarray(W.reshape(K // 128, 128, N // ncol, ncol).transpose(2, 1, 0, 3))


def _col(v, n=128):
    return np.ascontiguousarray(np.asarray(v).reshape(-1, n).T)


def host_consts():
    c = {}
    c["c_identf"] = np.eye(128, dtype=np.float32)
    bd = np.zeros((128, 128), np.float32)
    bd[:64, :64] = 1.0
    bd[64:, 64:] = 1.0
    c["c_bd64"] = bd
    p = np.arange(128)[:, None]
    j = np.arange(3968)[None, :]
    c["c_alibi"] = np.abs(j - p - 1920).astype(np.float32)
    t = np.linspace(0.0, 1.0, L, dtype=np.float32)[:, None]
    w = (np.float32(2.0 * math.pi / L) * np.arange(L, dtype=np.float32))[:, None]
    f = np.linspace(1e-4, 15, 16, dtype=np.float32)[None, :]
    z = np.concatenate([t, np.cos(f * w), -np.sin(f * w)], axis=-1).astype(np.float32)
    c["c_zT"] = np.ascontiguousarray(z.T)
    max_decay = math.log(1e-2) / 0.3
    min_decay = math.log(1e-2) / 1.5
    deltas = np.linspace(min_decay, max_decay, 1024, dtype=np.float32)
    win = np.exp(-t * np.abs(deltas)[None, :]).astype(np.float32)
    c["c_win"] = np.ascontiguousarray(win.reshape(16, 128, 1024))
    m0 = np.ones((128, 2), np.float32)
    m0[0, 0] = 0.0
    m0[:, 1] = 1.0 - m0[:, 0]
    c["c_m0"] = m0
    N = 2 * L
    tt = np.arange(L, dtype=np.float64)[:, None]
    ff = np.arange(L, dtype=np.float64)[None, :]
    ang = 2.0 * np.pi * tt * ff / N
    FT = np.concatenate([np.cos(ang), np.sin(ang)], axis=1)
    FT[:, L] = np.where(np.arange(L) % 2 == 0, 1.0, -1.0)
    FTt = FT.reshape(16, 128, 32, 128).transpose(2, 1, 0, 3)
    c["c_FT"] = np.ascontiguousarray(FTt).astype(ml_dtypes.bfloat16)
    wf = np.full((L,), 2.0 / N)
    wf[0] = 1.0 / N
    G = np.concatenate([(np.cos(ang) * wf[None, :]).T, (np.sin(ang) * (2.0 / N)).T], axis=0)
    G[L, :] = np.where(np.arange(L) % 2 == 0, 1.0, -1.0) / N
    Gt = G.reshape(2, 16, 128, 4, 512).transpose(3, 0, 2, 1, 4)
    c["c_G"] = np.ascontiguousarray(Gt).astype(ml_dtypes.bfloat16)
    rows = L // 64
    r = np.repeat(np.arange(rows, dtype=np.float32), 64)
    cc = np.tile(np.arange(64, dtype=np.float32), rows)
    half = 64
    inv = (np.float32(10000.0) ** (-np.arange(0, half, 2, dtype=np.float32) / half)).astype(np.float32)
    ang_r = r[:, None] * inv[None, :]
    ang_c = cc[:, None] * inv[None, :]
    a = np.concatenate([ang_r, ang_r, ang_c, ang_c], axis=-1).astype(np.float32)
    c["c_cos"] = np.ascontiguousarray(np.cos(a).T.astype(np.float32))
    c["c_sin"] = np.ascontiguousarray(np.sin(a).T.astype(np.float32))
    Pm = np.zeros((128, 128), np.float32)
    for m in range(32):
        Pm[m, m + 32] = -1.0
        Pm[m + 32, m] = 1.0
        Pm[m + 64, m + 96] = -1.0
        Pm[m + 96, m + 64] = 1.0
    c["c_PT"] = np.ascontiguousarray(Pm.T)
    s_ = np.arange(64)[:, None]
    t_ = np.arange(64)[None, :]
    mk = np.stack([np.tile((s_ <= t_).astype(np.float32), (1, 8)), np.tile((s_ >= t_).astype(np.float32), (1, 8))], axis=1)
    c["c_mask"] = np.ascontiguousarray(mk)
    return c


def host_layout(inp):
    o = {}
    g = np.stack([inp["norm_mix_g"], inp["norm_mem_q_g"], inp["norm_mem_kv_g"], inp["norm_ffn_g"]], axis=0)
    o["gains"] = np.ascontiguousarray(g.reshape(4, 2, KC, 128).transpose(3, 0, 1, 2))
    ev = inp["ev_w_in"][0]
    o["ev_w_in_t"] = _tile_w(ev, 128)
    o["ev_w_out_t"] = _tile_w(inp["ev_w_out"][0], 128)
    od = inp["od_w_in"][0]
    o["od_w_in_t"] = _tile_w(od, 128)
    o["od_w_out_t"] = _tile_w(inp["od_w_out"][0], 128)
    o["ca_wq_t"] = np.stack([_tile_w(inp["ca_w_q"][l], 128) for l in range(2)])
    o["ca_wk_t"] = np.stack([_tile_w(inp["ca_w_kv"][l][:, :512], 128) for l in range(2)])
    o["ca_wv_t"] = np.stack([_tile_w(inp["ca_w_kv"][l][:, 512:], 512)[0] for l in range(2)])
    o["ca_wo_t"] = np.stack([_tile_w(inp["ca_w_out"][l], 128) for l in range(2)])
    o["mlp_w1_t"] = np.stack([_tile_w(inp["mlp_w1"][l], 128) for l in range(2)])
    o["mlp_w2_t"] = np.stack([_tile_w(inp["mlp_w2"][l], 128) for l in range(2)])
    cw = inp["hy_conv_w"][0]
    o["hy_cw"] = np.ascontiguousarray(cw.reshape(3, 24, 128).transpose(2, 0, 1))
    o["hy_cb"] = _col(inp["hy_conv_b"][0])
    o["hy_bias"] = _col(inp["hy_bias"][0])
    o["hy_fw1"] = np.ascontiguousarray(inp["hy_fw1"][0])
    o["hy_fw2"] = np.ascontiguousarray(inp["hy_fw2"][0])
    o["hy_fw3"] = np.ascontiguousarray(inp["hy_fw3"][0])
    o["hy_fw4"] = np.ascontiguousarray(inp["hy_fw4"][0])
    o["hy_fcols"] = np.ascontiguousarray(np.stack([inp["hy_fb1"][0], inp["hy_fb2"][0], inp["hy_fb3"][0], inp["hy_sin_freq"][0]], axis=1))
    qg = np.concatenate([inp["df_q_g"][0], inp["df_q_g"][0]])
    kg = np.concatenate([inp["df_k_g"][0], inp["df_k_g"][0]])
    o["df_cols"] = np.ascontiguousarray(np.stack([qg, kg, inp["df_subln_g"][0]], axis=1))
    o["df_lam"] = np.ascontiguousarray(np.stack([inp["df_lam_q1"][0], inp["df_lam_k1"][0], inp["df_lam_q2"][0], inp["df_lam_k2"][0]])[None])
    lb = inp["hg_lb_logits"]
    o["hg_lb"] = np.ascontiguousarray(lb.reshape(2, 2, 8, 128).transpose(3, 0, 1, 2))
    o["od_cols"] = np.ascontiguousarray(np.stack([inp["hg_norm_g"][0], inp["gq_q_g"][0], inp["gq_k_g"][0]], axis=1))
    o["ca_cols"] = np.ascontiguousarray(np.stack([inp["ca_q_g"], inp["ca_k_g"]], axis=0).transpose(2, 0, 1))
    return {k: np.ascontiguousarray(v, dtype=np.float32) for k, v in o.items()}


_CACHE = {}


def run(inputs, ncores=NCORES, stages=ALL_STAGES, debug=()):
    key = (tuple(stages), tuple(debug))
    if key not in _CACHE:
        _CACHE[key] = build(stages=stages, debug=debug)
    if "consts" not in _CACHE:
        _CACHE["consts"] = host_consts()
    C = _CACHE[key]
    shared = dict(_CACHE["consts"])
    shared.update(host_layout(inputs))
    x = np.ascontiguousarray(inputs["x"], dtype=np.float32)
    mem = np.ascontiguousarray(inputs["mem"], dtype=np.float32)
    in_maps = []
    for c in range(ncores):
        m = {"x": x[c * SPC:(c + 1) * SPC], "mem": mem[c * SPC:(c + 1) * SPC]}
        m.update(shared)
        in_maps.append({k: v for k, v in m.items() if k in C.din})
    res = run_bass_kernel_spmd(C.nc, in_maps, core_ids=list(range(ncores)))
    return res.results


def kernel(**inputs):
    results = run(inputs)
    out = np.concatenate([r["y"] for r in results], axis=0)
    return out.astype(np.float32)
```

```python
import contextlib
import math
import os
import numpy as np
import ml_dtypes
import concourse.bass as bass
import concourse.mybir as mybir
from concourse.bass_utils import run_bass_kernel_spmd

F32 = mybir.dt.float32
BF16 = mybir.dt.bfloat16
AF = mybir.ActivationFunctionType
ALU = mybir.AluOpType
AX = mybir.AxisListType

NCORES = 8
SPC = 2
L = 2048
D = 2048
KC = D // 128
MEM = 256
EPS = 1e-6
FFN = 8192
EVEN_IN = 6144
ODD_IN = 6656


class Buf:
    __slots__ = ("name", "w", "r")

    def __init__(self, name):
        self.name = name
        self.w = {}
        self.r = {}


class Eng:
    def __init__(self, name, h, sem):
        self.name, self.h, self.sem = name, h, sem
        self.n = 0
        self.waited = {}


class KB:
    def __init__(self, nc, es):
        self.nc, self.es = nc, es
        self.sems = {}
        self.eng = {}
        for name, h in (("pe", nc.tensor), ("act", nc.scalar), ("dve", nc.vector),
                        ("pool", nc.gpsimd), ("sp", nc.sync)):
            sem = es.enter_context(nc.semaphore("s_" + name))
            self.eng[name] = Eng(name, h, sem)
            self.sems[name] = sem
        self.dq = {}
        for q, nslots in (("sp", 8), ("pool", 8), ("act", 4)):
            slots = []
            for i in range(nslots):
                key = "d_%s%d" % (q, i)
                self.sems[key] = es.enter_context(nc.semaphore(key))
                slots.append([key, 0])
            self.dq[q] = [slots, 0]
        self.nbuf = 0

    def buf(self, name="b"):
        self.nbuf += 1
        return Buf("%s%d" % (name, self.nbuf))

    def _wait(self, E, key, val):
        if E.waited.get(key, 0) >= val:
            return
        if key == "pe" and E.name == "pe":
            return
        E.h.wait_ge(self.sems[key], val)
        E.waited[key] = val

    def _deps(self, E, r, w):
        need = {}
        for b in r:
            for k_, v in b.w.items():
                if need.get(k_, 0) < v:
                    need[k_] = v
        for b in w:
            for k_, v in b.w.items():
                if need.get(k_, 0) < v:
                    need[k_] = v
            for k_, v in b.r.items():
                if need.get(k_, 0) < v:
                    need[k_] = v
        for k_, v in need.items():
            self._wait(E, k_, v)

    def _reg(self, key, val, r, w):
        for b in r:
            if b.r.get(key, 0) < val:
                b.r[key] = val
        for b in w:
            b.w = {key: val}
            b.r = {}

    def op(self, eng, fn, r=(), w=(), inc=True):
        E = self.eng[eng]
        self._deps(E, r, w)
        ins = fn(E.h)
        val = E.n + 1
        if inc:
            ins.then_inc(E.sem, 1)
            E.n = val
        self._reg(eng, val, r, w)
        return ins

    def dma(self, out, in_, r=(), w=(), q="sp", **kw):
        E = self.eng[q]
        slots, n = self.dq[q]
        slot = slots[n % len(slots)]
        self.dq[q][1] = n + 1
        self._deps(E, r, w)
        if slot[1] > 0:
            self._wait(E, slot[0], slot[1])
        ins = E.h.dma_start(out=out, in_=in_, **kw)
        slot[1] += 16
        ins.then_inc(self.sems[slot[0]], 16)
        self._reg(slot[0], slot[1], r, w)

    def barrier(self, engines=("pe", "act", "dve", "pool", "sp")):
        cur = {}
        for name, E in self.eng.items():
            if E.n > 0:
                cur[name] = E.n
        for q, (slots, n) in self.dq.items():
            for key, v in slots:
                if v > 0:
                    cur[key] = v
        for name in engines:
            E = self.eng[name]
            for key, v in cur.items():
                if key == name:
                    continue
                self._wait(E, key, v)

    def final_wait(self):
        self.barrier(engines=("sp",))


class Stage:
    def __init__(self, kb):
        self.kb = kb
        self.es = contextlib.ExitStack()
        self.cnt = 0

    def __enter__(self):
        self.es.__enter__()
        return self

    def __exit__(self, *a):
        self.kb.barrier()
        return self.es.__exit__(*a)

    def sb(self, name, shape, dt):
        self.kb.nbuf += 1
        t = self.es.enter_context(self.kb.nc.sbuf_tensor("%s_%d" % (name, self.kb.nbuf), list(shape), dt))
        return t, self.kb.buf(name)

    def rot(self, name, shape, dt, n):
        return Rot([self.sb(name + str(i), shape, dt) for i in range(n)])


class Rot:
    def __init__(self, items):
        self.items = items
        self.i = 0

    def next(self):
        it = self.items[self.i % len(self.items)]
        self.i += 1
        return it


TWO_PI = 2.0 * math.pi
LAM_INIT0 = 0.8 - 0.6 * math.exp(-0.3 * 0)
ALL_STAGES = ("pro", "filt", "mix0", "ca0", "mlp0", "mix1", "ca1", "mlp1", "epi")


class Ctx:
    pass


def build(stages=ALL_STAGES, debug=()):
    C = Ctx()
    nc = bass.Bass("TRN2", target_bir_lowering=False)
    C.nc = nc
    C.din = {}
    debug = set(debug)

    def inp(name, shape, dt=F32):
        t = nc.dram_tensor(name, list(shape), dt, kind="ExternalInput").ap()
        C.din[name] = t
        return t

    def scratch(name, shape, dt):
        kind = "ExternalOutput" if name in debug else "Internal"
        return nc.dram_tensor(name, list(shape), dt, kind=kind).ap()

    x = inp("x", [SPC, L, D])
    mem = inp("mem", [SPC, MEM, D])
    y = nc.dram_tensor("y", [SPC, L, D], F32, kind="ExternalOutput").ap()
    gains_d = inp("gains", [128, 4, 2, KC])
    ev_w_in_t = inp("ev_w_in_t", [48, 128, KC, 128])
    ev_w_out_t = inp("ev_w_out_t", [16, 128, KC, 128])
    od_w_in_t = inp("od_w_in_t", [52, 128, KC, 128])
    od_w_out_t = inp("od_w_out_t", [16, 128, KC, 128])
    ca_wq_t = inp("ca_wq_t", [2, 4, 128, KC, 128])
    ca_wk_t = inp("ca_wk_t", [2, 4, 128, KC, 128])
    ca_wv_t = inp("ca_wv_t", [2, 128, KC, 512])
    ca_wo_t = inp("ca_wo_t", [2, 16, 128, 4, 128])
    mlp_w1_t = inp("mlp_w1_t", [2, 64, 128, KC, 128])
    mlp_w2_t = inp("mlp_w2_t", [2, 16, 128, 64, 128])
    hy_cw_d = inp("hy_cw", [128, 3, 24])
    hy_cb_d = inp("hy_cb", [128, 24])
    hy_bias_d = inp("hy_bias", [128, 8])
    hy_fw1_d = inp("hy_fw1", [33, 64])
    hy_fw2_d = inp("hy_fw2", [64, 64])
    hy_fw3_d = inp("hy_fw3", [64, 64])
    hy_fw4_d = inp("hy_fw4", [64, 2048])
    hy_fcols_d = inp("hy_fcols", [64, 4])
    df_cols_d = inp("df_cols", [128, 3])
    df_lam_d = inp("df_lam", [1, 4, 64])
    hg_lb_d = inp("hg_lb", [128, 2, 2, 8])
    od_cols_d = inp("od_cols", [128, 3])
    ca_cols_d = inp("ca_cols", [128, 2, 2])
    c_identf = inp("c_identf", [128, 128])
    c_bd64 = inp("c_bd64", [128, 128])
    c_alibi = inp("c_alibi", [128, 3968])
    c_zT = inp("c_zT", [33, L])
    c_win = inp("c_win", [16, 128, 1024])
    c_m0 = inp("c_m0", [128, 2])
    c_FT = inp("c_FT", [32, 128, 16, 128], BF16)
    c_G = inp("c_G", [4, 2, 128, 16, 512], BF16)
    c_cos = inp("c_cos", [128, L])
    c_sin = inp("c_sin", [128, L])
    c_PT = inp("c_PT", [128, 128])
    c_mask = inp("c_mask", [64, 2, 512])

    xT = scratch("xT", [SPC, D, L], F32)
    oT = scratch("oT", [SPC, D, L], BF16)
    hx1 = scratch("hx1", [SPC, 1024, L], F32)
    hvx = scratch("hvx", [SPC, 1024, L], F32)
    dq = scratch("dq", [SPC, 1024, L], BF16)
    dk = scratch("dk", [SPC, 1024, L], BF16)
    dv = scratch("dv", [SPC, L, 1024], BF16)
    gk = scratch("gk", [SPC, 256, L], BF16)
    gv = scratch("gv", [SPC, L, 256], BF16)
    KA = scratch("KA", [16, 128, 1024], F32)
    KBt = scratch("KB", [16, 128, 1024], F32)
    KD0 = scratch("KD0", [128, 1024], F32)
    xTv = [xT[s].rearrange("(c p) t -> p c t", p=128) for s in range(SPC)]
    oTv = [oT[s].rearrange("(c p) t -> p c t", p=128) for s in range(SPC)]

    es = contextlib.ExitStack()
    with es:
        kb = KB(nc, es)
        C.kb = kb
        b_xT = [[[kb.buf("xT") for _ in range(KC)] for _ in range(4)] for _ in range(SPC)]
        b_oT = [[kb.buf("oT") for _ in range(KC)] for _ in range(SPC)]
        b_scr = {n: [kb.buf(n) for _ in range(SPC)] for n in ("hx1", "hvx", "dq", "dk", "dv", "gk", "gv")}
        b_K = kb.buf("K")
        psA = es.enter_context(nc.psum_tensor("psA", [128, 4, 512], F32))
        psB = es.enter_context(nc.psum_tensor("psB", [128, 4, 512], F32))
        pst = [psA, psA, psA, psA, psB, psB, psB, psB]
        b_ps = [kb.buf("ps") for _ in range(8)]

        def bank(i):
            return pst[i][:, i % 4, :]

        def bank_bf(i):
            return pst[i][:, i % 4, :].bitcast(BF16)

        def xall(s, oc):
            return [b_xT[s][tb][oc] for tb in range(4)]

        def xblk(s, tb):
            return list(b_xT[s][tb])

        def evac(i, out, in_, r, w):
            if i % 2 == 0:
                kb.op("act", lambda e: e.copy(out=out, in_=in_), r=r, w=w)
            else:
                kb.op("dve", lambda e: e.tensor_copy(out=out, in_=in_), r=r, w=w)

        def mm(out, lhsT, rhs, start, stop, r, w, inc=None):
            kb.op("pe", lambda e: e.matmul(out, lhsT=lhsT, rhs=rhs, start=start, stop=stop), r=r, w=w,
                  inc=(stop if inc is None else inc))

        def rstd_inplace(ap, b, src, src_b, inv_n):
            kb.op("dve", lambda e: e.tensor_scalar(out=ap, in0=src, scalar1=inv_n, scalar2=EPS, op0=ALU.mult, op1=ALU.add),
                  r=src_b, w=[b])
            kb.op("act", lambda e: e.activation(out=ap, in_=ap, func=AF.Ln), r=[b], w=[b])
            kb.op("act", lambda e: e.activation(out=ap, in_=ap, func=AF.Exp, scale=-0.5), r=[b], w=[b])

        psA4 = psA[:, :, :]
        psB4 = psB[:, :, :]
        bA = b_ps[0:4]
        bB = b_ps[4:8]

        with Stage(kb) as g:
            identf, b_c = g.sb("identf", [128, 128], F32)
            b_c = kb.buf("const")
            kb.dma(identf[:], c_identf[:, :], w=[b_c])
            identb, _ = g.sb("identb", [128, 128], BF16)
            onesb, _ = g.sb("onesb", [128, 128], BF16)
            onesf, _ = g.sb("onesf", [128, 128], F32)
            bd64f, _ = g.sb("bd64f", [128, 128], F32)
            bd64b, _ = g.sb("bd64b", [128, 128], BF16)
            gains, _ = g.sb("gains", [128, 4, 2, KC], F32)
            kb.dma(bd64f[:], c_bd64[:, :], w=[b_c])
            kb.dma(gains[:], gains_d[:, :, :, :], w=[b_c])
            kb.op("dve", lambda e: e.tensor_copy(out=identb[:], in_=identf[:]), r=[b_c], w=[b_c])
            kb.op("dve", lambda e: e.tensor_copy(out=bd64b[:], in_=bd64f[:]), r=[b_c], w=[b_c])
            kb.op("dve", lambda e: e.memset(onesb[:], 1.0), w=[b_c])
            kb.op("dve", lambda e: e.memset(onesf[:], 1.0), w=[b_c])
            m0, _ = g.sb("m0", [128, 2], F32)
            kb.dma(m0[:], c_m0[:, :], w=[b_c])
            hycw, _ = g.sb("hycw", [128, 3, 24], F32)
            hycb, _ = g.sb("hycb", [128, 24], F32)
            hyb, _ = g.sb("hyb", [128, 8], F32)
            dfc, _ = g.sb("dfc", [128, 3], F32)
            odc, _ = g.sb("odc", [128, 3], F32)
            cac, _ = g.sb("cac", [128, 2, 2], F32)
            kb.dma(hycw[:], hy_cw_d[:, :, :], w=[b_c])
            kb.dma(hycb[:], hy_cb_d[:, :], w=[b_c])
            kb.dma(hyb[:], hy_bias_d[:, :], w=[b_c])
            kb.dma(dfc[:], df_cols_d[:, :], w=[b_c])
            kb.dma(odc[:], od_cols_d[:, :], w=[b_c])
            kb.dma(cac[:], ca_cols_d[:, :, :], w=[b_c])
            neglam, _ = g.sb("neglam", [128, 1], F32)
            sg2, _ = g.sb("sg2", [128, 1], F32)
            lbs, _ = g.sb("lbs", [128, 2, 8], F32)
            oml, _ = g.sb("oml", [128, 2, 8], F32)
            noml, _ = g.sb("noml", [128, 2, 8], F32)
            RgT, _ = g.sb("RgT", [128, 2, 128], BF16)

            with Stage(kb) as st:
                lamv, b_l = st.sb("lamv", [1, 4, 64], F32)
                kb.dma(lamv[:], df_lam_d[:, :, :], w=[b_l])
                pr, b_pr = st.sb("pr", [1, 2, 64], F32)
                sm, b_sm = st.sb("sm", [1, 2], F32)
                kb.op("dve", lambda e: e.tensor_tensor(out=pr[:, 0, :], in0=lamv[:, 0, :], in1=lamv[:, 1, :], op=ALU.mult), r=[b_l], w=[b_pr])
                kb.op("dve", lambda e: e.tensor_tensor(out=pr[:, 1, :], in0=lamv[:, 2, :], in1=lamv[:, 3, :], op=ALU.mult), r=[b_l], w=[b_pr])
                kb.op("dve", lambda e: e.reduce_sum(out=sm[:], in_=pr[:], axis=AX.X), r=[b_pr], w=[b_sm])
                kb.op("act", lambda e: e.activation(out=sm[:], in_=sm[:], func=AF.Exp), r=[b_sm], w=[b_sm])
                nl, b_nl = st.sb("nl", [1, 1], F32)
                kb.op("dve", lambda e: e.tensor_tensor(out=nl[:], in0=sm[:, 1:2], in1=sm[:, 0:1], op=ALU.subtract), r=[b_sm], w=[b_nl])
                kb.op("dve", lambda e: e.tensor_scalar_add(out=nl[:], in0=nl[:], scalar1=-LAM_INIT0), r=[b_nl], w=[b_nl])
                mm(bank(0)[:, 0:1], onesf[0:1, :], nl[0:1, 0:1], True, True, r=[b_nl, b_c], w=[b_ps[0]])
                kb.op("dve", lambda e: e.tensor_copy(out=neglam[:], in_=bank(0)[:, 0:1]), r=[b_ps[0]], w=[b_c])
                kb.op("dve", lambda e: e.tensor_scalar(out=sg2[:], in0=dfc[:, 2:3], scalar1=(1.0 - LAM_INIT0), scalar2=None, op0=ALU.mult), r=[b_c], w=[b_c])
                lg, b_lg = st.sb("lg", [128, 2, 2, 8], F32)
                kb.dma(lg[:], hg_lb_d[:, :, :, :], w=[b_lg])
                kb.op("dve", lambda e: e.tensor_tensor(out=lbs[:], in0=lg[:, :, 1, :], in1=lg[:, :, 0, :], op=ALU.subtract), r=[b_lg], w=[b_c])
                kb.op("act", lambda e: e.activation(out=lbs[:], in_=lbs[:], func=AF.Sigmoid), r=[b_c], w=[b_c])
                kb.op("dve", lambda e: e.tensor_scalar(out=oml[:], in0=lbs[:], scalar1=-1.0, scalar2=1.0, op0=ALU.mult, op1=ALU.add), r=[b_c], w=[b_c])
                kb.op("dve", lambda e: e.tensor_scalar(out=noml[:], in0=oml[:], scalar1=-1.0, scalar2=None, op0=ALU.mult), r=[b_c], w=[b_c])
                ptf, b_pt = st.sb("ptf", [128, 128], F32)
                kb.dma(ptf[:], c_PT[:, :], w=[b_pt])
                kb.op("dve", lambda e: e.tensor_scalar(out=RgT[:, 0, :], in0=ptf[:], scalar1=odc[:, 1:2], scalar2=None, op0=ALU.mult), r=[b_pt, b_c], w=[b_c])
                kb.op("dve", lambda e: e.tensor_scalar(out=RgT[:, 1, :], in0=ptf[:], scalar1=odc[:, 2:3], scalar2=None, op0=ALU.mult), r=[b_pt, b_c], w=[b_c])

            def norm_to_hT(s, which, layer, hT, b_hT):
                with Stage(kb) as st:
                    xb = st.rot("nx", [128, KC, 512], F32, 2)
                    sq, b_sq = st.sb("nsq", [128, KC, 512], BF16)
                    rs = st.rot("nrs", [128, 512], F32, 2)
                    for tb in range(4):
                        xt, b_xt = xb.next()
                        kb.dma(xt[:], xTv[s][:, :, tb * 512:(tb + 1) * 512], r=xblk(s, tb), w=[b_xt])
                        kb.op("act", lambda e: e.activation(out=sq[:], in_=xt[:], func=AF.Square), r=[b_xt], w=[b_sq])
                        for c in range(KC):
                            mm(bank(4), onesb[:], sq[:, c, :], c == 0, c == KC - 1, r=[b_sq, b_c], w=[b_ps[4]])
                        r_, b_r = rs.next()
                        rstd_inplace(r_[:], b_r, bank(4), [b_ps[4]], 1.0 / D)
                        for c in range(KC):
                            kb.op("dve", lambda e: e.scalar_tensor_tensor(out=hT[:, c, tb * 512:(tb + 1) * 512], in0=xt[:, c, :],
                                                                         scalar=gains[:, which, layer, c:c + 1], in1=r_[:],
                                                                         op0=ALU.mult, op1=ALU.mult),
                                  r=[b_xt, b_r, b_c], w=[b_hT[c]])

            def lin_fm(wt_rot, w_ap, hT, b_hT, kcn, ntb=4, q="pool", pbase=0):
                wt, b_wt = wt_rot.next()
                kb.dma(wt[:], w_ap, w=[b_wt], q=q)
                for tb in range(ntb):
                    for kc in range(kcn):
                        mm(bank(pbase + tb), wt[:, kc, :], hT[:, kc, tb * 512:(tb + 1) * 512], kc == 0, kc == kcn - 1,
                           r=[b_wt, b_hT[kc]], w=[b_ps[pbase + tb]])

            def residual_add(st_rot, s, oc, pbase=0):
                xr, b_xr = st_rot.next()
                kb.dma(xr[:], xT[s, oc * 128:(oc + 1) * 128, :], r=xall(s, oc), w=[b_xr])
                kb.op("dve", lambda e: e.tensor_tensor(out=xr[:].rearrange("p (b t) -> p b t", b=4), in0=xr[:].rearrange("p (b t) -> p b t", b=4),
                                                       in1=(psA4 if pbase == 0 else psB4), op=ALU.add), r=[b_xr] + (bA if pbase == 0 else bB), w=[b_xr])
                kb.dma(xT[s, oc * 128:(oc + 1) * 128, :], xr[:], r=[b_xr], w=xall(s, oc))

            def rmsnorm_fm(st, src_f, b_src, ones_l, inv_n, gcol, out_bf, b_out, extra=None):
                sqb, b_sqb = st["sqb"].next()
                rs, b_rs = st["rs"].next()
                kb.op("act", lambda e: e.activation(out=sqb[:], in_=src_f, func=AF.Square), r=[b_src], w=[b_sqb])
                for tb in range(4):
                    mm(bank(4 + tb), ones_l, sqb[:, tb * 512:(tb + 1) * 512], True, True, r=[b_sqb, b_c], w=[b_ps[4 + tb]])
                rstd_inplace(rs[:].rearrange("p (b t) -> p b t", b=4), b_rs, psB4, bB, inv_n)
                return rs, b_rs

            if "pro" in stages:
                with Stage(kb) as st:
                    xin = st.rot("xin", [128, D], F32, 2)
                    stg = st.rot("stg", [128, KC, 512], F32, 2)
                    ev = 0
                    for s in range(SPC):
                        for tb in range(4):
                            sg, b_sg = stg.next()
                            for tt in range(4):
                                xi, b_xi = xin.next()
                                t0 = tb * 512 + tt * 128
                                kb.dma(xi[:], x[s, t0:t0 + 128, :], w=[b_xi])
                                for j in range(4):
                                    for c4 in range(4):
                                        c = j * 4 + c4
                                        kb.op("pe", lambda e: e.transpose(out=bank(j)[:, c4 * 128:(c4 + 1) * 128],
                                                                          in_=xi[:, c * 128:(c + 1) * 128], identity=identf[:]),
                                              r=[b_xi, b_c], w=[b_ps[j]], inc=(c4 == 3))
                                    src = bank(j).rearrange("p (c t) -> p c t", c=4)
                                    dst = sg[:, j * 4:(j + 1) * 4, tt * 128:(tt + 1) * 128]
                                    evac(ev, dst, src, [b_ps[j]], [b_sg])
                                    ev += 1
                            kb.dma(xTv[s][:, :, tb * 512:(tb + 1) * 512], sg[:], r=[b_sg], w=xblk(s, tb))

            def sin_layer(st, dst, src_ps, b_src, col, n):
                u, b_u = st["u"].next()
                qf, b_qf = st["qf"].next()
                qi, b_qi = st["qi"].next()
                kb.op("dve", lambda e: e.tensor_scalar(out=u[:], in0=src_ps, scalar1=st["fcols"][:, 3:4], scalar2=st["ffb"][:, col:col + 1],
                                                       op0=ALU.mult, op1=ALU.add), r=b_src + [st["b_f"]], w=[b_u])
                kb.op("dve", lambda e: e.tensor_scalar(out=u[:], in0=u[:], scalar1=17.0 * math.pi, scalar2=None, op0=ALU.add), r=[b_u], w=[b_u])
                kb.op("dve", lambda e: e.tensor_scalar(out=qf[:], in0=u[:], scalar1=1.0 / TWO_PI, scalar2=None, op0=ALU.mult), r=[b_u], w=[b_qf])
                kb.op("dve", lambda e: e.tensor_copy(out=qi[:], in_=qf[:]), r=[b_qf], w=[b_qi])
                kb.op("dve", lambda e: e.tensor_copy(out=qf[:], in_=qi[:]), r=[b_qi], w=[b_qf])
                kb.op("dve", lambda e: e.scalar_tensor_tensor(out=u[:], in0=qf[:], scalar=-TWO_PI, in1=u[:], op0=ALU.mult, op1=ALU.add), r=[b_qf, b_u], w=[b_u])
                kb.op("dve", lambda e: e.tensor_scalar(out=qf[:], in0=u[:], scalar1=0.0, scalar2=TWO_PI, op0=ALU.is_lt, op1=ALU.mult), r=[b_u], w=[b_qf])
                kb.op("dve", lambda e: e.tensor_tensor(out=u[:], in0=u[:], in1=qf[:], op=ALU.add), r=[b_u, b_qf], w=[b_u])
                kb.op("act", lambda e: e.activation(out=dst, in_=u[:], func=AF.Sin, bias=st["mpi"][:], scale=1.0), r=[b_u, st["b_f"]], w=[st["b_h"]])

            if "filt" in stages:
                with Stage(kb) as st0:
                    zT, b_f = st0.sb("zT", [33, L], F32)
                    w1, _ = st0.sb("w1", [33, 64], F32)
                    w2, _ = st0.sb("w2", [64, 64], F32)
                    w3, _ = st0.sb("w3", [64, 64], F32)
                    w4, _ = st0.sb("w4", [64, 2048], F32)
                    fcols, _ = st0.sb("fcols", [64, 4], F32)
                    ffb, _ = st0.sb("ffb", [64, 3], F32)
                    mpi, _ = st0.sb("mpi", [64, 1], F32)
                    for t_, d_ in ((zT, c_zT), (w1, hy_fw1_d), (w2, hy_fw2_d), (w3, hy_fw3_d), (w4, hy_fw4_d), (fcols, hy_fcols_d)):
                        kb.dma(t_[:], d_[:, :], w=[b_f])
                    kb.op("dve", lambda e: e.memset(mpi[:], -math.pi), w=[b_f])
                    kb.op("dve", lambda e: e.tensor_scalar(out=ffb[:], in0=fcols[:, 0:3], scalar1=fcols[:, 3:4], scalar2=None, op0=ALU.mult), r=[b_f], w=[b_f])
                    hA, b_hA = st0.sb("hA", [64, L], F32)
                    hB, b_hB = st0.sb("hB", [64, L], F32)
                    sl = {"u": st0.rot("u", [64, 512], F32, 2), "qf": st0.rot("qf", [64, 512], F32, 2),
                          "qi": st0.rot("qi", [64, 512], mybir.dt.int32, 2), "fcols": fcols, "ffb": ffb, "mpi": mpi, "b_f": b_f}
                    for li, (wl, kdim, src, b_s, dst, b_d) in enumerate(((w1, 33, zT, b_f, hA, b_hA), (w2, 64, hA, b_hA, hB, b_hB), (w3, 64, hB, b_hB, hA, b_hA))):
                        sl["b_h"] = b_d
                        for tb in range(4):
                            mm(bank(tb)[0:64, :], wl[0:kdim, :], src[0:kdim, tb * 512:(tb + 1) * 512], True, True, r=[b_f, b_s], w=[b_ps[tb]])
                            sin_layer(sl, dst[:, tb * 512:(tb + 1) * 512], bank(tb)[0:64, :], [b_ps[tb]], li, 512)
                    h3 = hA
                    b_h3 = b_hA
                    for cb in range(2):
                        with Stage(kb) as st:
                            kp, b_kp = st.sb("kp", [128, 16, 512], BF16)
                            km, b_km = st.sb("km", [128, 16, 512], BF16)
                            wn = st.rot("wn", [128, 512], F32, 2)
                            ta = st.rot("ta", [128, 512], F32, 2)
                            tbb = st.rot("tbb", [128, 512], F32, 2)
                            for tc in range(16):
                                mm(bank(0), h3[:, tc * 128:(tc + 1) * 128], w4[:, cb * 512:(cb + 1) * 512], True, True, r=[b_h3, b_f], w=[b_ps[0]])
                                mm(bank(1), h3[:, tc * 128:(tc + 1) * 128], w4[:, 1024 + cb * 512:1024 + (cb + 1) * 512], True, True, r=[b_h3, b_f], w=[b_ps[1]])
                                wt_, b_w = wn.next()
                                kb.dma(wt_[:], c_win[tc, :, cb * 512:(cb + 1) * 512], w=[b_w])
                                a_, b_a = ta.next()
                                b__, b_b = tbb.next()
                                kb.op("dve", lambda e: e.tensor_tensor(out=a_[:], in0=bank(0), in1=wt_[:], op=ALU.mult), r=[b_ps[0], b_w], w=[b_a])
                                kb.op("dve", lambda e: e.tensor_tensor(out=b__[:], in0=bank(1), in1=wt_[:], op=ALU.mult), r=[b_ps[1], b_w], w=[b_b])
                                if tc == 0:
                                    kb.op("dve", lambda e: e.tensor_scalar(out=b__[:], in0=b__[:], scalar1=m0[:, 0:1], scalar2=None, op0=ALU.mult), r=[b_b, b_c], w=[b_b])
                                kb.op("pool", lambda e: e.tensor_tensor(out=kp[:, tc, :], in0=a_[:], in1=b__[:], op=ALU.add), r=[b_a, b_b], w=[b_kp])
                                kb.op("pool", lambda e: e.tensor_tensor(out=km[:, tc, :], in0=a_[:], in1=b__[:], op=ALU.subtract), r=[b_a, b_b], w=[b_km])
                            ftr = st.rot("ft", [128, 16, 128], BF16, 3)
                            ko = st.rot("ko", [128, 512], F32, 3)
                            for j in range(16):
                                ftc, b_fc = ftr.next()
                                kb.dma(ftc[:], c_FT[j], w=[b_fc])
                                fts, b_fs = ftr.next()
                                kb.dma(fts[:], c_FT[16 + j], w=[b_fs])
                                for tc in range(16):
                                    mm(bank(2), ftc[:, tc, :], kp[:, tc, :], tc == 0, tc == 15, r=[b_fc, b_kp], w=[b_ps[2]])
                                for tc in range(16):
                                    mm(bank(3), fts[:, tc, :], km[:, tc, :], tc == 0, tc == 15, r=[b_fs, b_km], w=[b_ps[3]])
                                ka_, b_ka = ko.next()
                                kb.op("act", lambda e: e.copy(out=ka_[:], in_=bank(2)), r=[b_ps[2]], w=[b_ka])
                                kb.dma(KA[j, :, cb * 512:(cb + 1) * 512], ka_[:], r=[b_ka], w=[b_K])
                                kb_, b_kb = ko.next()
                                if j == 0:
                                    kb.op("dve", lambda e: e.tensor_scalar(out=kb_[:], in0=bank(3), scalar1=m0[:, 0:1], scalar2=None, op0=ALU.mult), r=[b_ps[3], b_c], w=[b_kb])
                                else:
                                    kb.op("dve", lambda e: e.tensor_copy(out=kb_[:], in_=bank(3)), r=[b_ps[3]], w=[b_kb])
                                kb.dma(KBt[j, :, cb * 512:(cb + 1) * 512], kb_[:], r=[b_kb], w=[b_K])
                                if j == 0:
                                    for tc in range(16):
                                        mm(bank(4), fts[:, tc, :], kp[:, tc, :], tc == 0, tc == 15, r=[b_fs, b_kp], w=[b_ps[4]])
                                    kd_, b_kd = ko.next()
                                    kb.op("dve", lambda e: e.tensor_scalar(out=kd_[:], in0=bank(4), scalar1=m0[:, 1:2], scalar2=None, op0=ALU.mult), r=[b_ps[4], b_c], w=[b_kd])
                                    kb.op("dve", lambda e: e.scalar_tensor_tensor(out=kd_[:], in0=ka_[:], scalar=m0[:, 0:1], in1=kd_[:], op0=ALU.mult, op1=ALU.add),
                                          r=[b_ka, b_kd, b_c], w=[b_kd])
                                    kb.dma(KD0[:, cb * 512:(cb + 1) * 512], kd_[:], r=[b_kd], w=[b_K])

            def out_proj(s, w_t):
                with Stage(kb) as st:
                    ot = st.sb("oTs", [128, KC, L], BF16)[0]
                    b_ot = [kb.buf("oTs") for _ in range(KC)]
                    for c in range(KC):
                        kb.dma(ot[:, c, :], oT[s, c * 128:(c + 1) * 128, :], r=[b_oT[s][c]], w=[b_ot[c]])
                    wr = st.rot("wo", [128, KC, 128], BF16, 2)
                    xr = st.rot("xr", [128, L], F32, 2)
                    for oc in range(16):
                        lin_fm(wr, w_t[oc], ot, b_ot, KC, pbase=4 * (oc % 2))
                        residual_add(xr, s, oc, pbase=4 * (oc % 2))

            def mlp(layer, s):
                with Stage(kb) as sh:
                    hT = sh.sb("hT", [128, KC, L], BF16)[0]
                    b_hT = [kb.buf("hT") for _ in range(KC)]
                    norm_to_hT(s, 3, layer, hT, b_hT)
                    with Stage(kb) as st:
                        hid = st.sb("hid", [128, 32, 1024], BF16)[0]
                        b_hid = [kb.buf("hid") for _ in range(32)]
                        w1r = st.rot("w1t", [128, KC, 128], BF16, 4)
                        w2r = st.rot("w2t", [128, 32, 128], BF16, 3)
                        tmp = st.rot("mt", [128, 512], F32, 3)
                        xr = st.rot("mxr", [128, 512], F32, 4)
                        nb1 = 0
                        nb2 = 0
                        for tt in range(2):
                            for half in range(2):
                                for hcl in range(32):
                                    hc = half * 32 + hcl
                                    w1_, b_w1 = w1r.next()
                                    kb.dma(w1_[:], mlp_w1_t[layer, hc], w=[b_w1], q="pool")
                                    pbs = [(nb1 + t2) % 4 for t2 in range(2)]
                                    nb1 += 2
                                    for kc in range(KC):
                                        for t2 in range(2):
                                            tok = tt * 1024 + t2 * 512
                                            mm(bank(pbs[t2]), w1_[:, kc, :], hT[:, kc, tok:tok + 512], kc == 0, kc == KC - 1, r=[b_w1, b_hT[kc]], w=[b_ps[pbs[t2]]])
                                    for t2 in range(2):
                                        t_, b_t = tmp.next()
                                        kb.op("dve", lambda e: e.tensor_scalar_max(out=t_[:], in0=bank(pbs[t2]), scalar1=0.0), r=[b_ps[pbs[t2]]], w=[b_t])
                                        kb.op("act", lambda e: e.activation(out=hid[:, hcl, t2 * 512:(t2 + 1) * 512], in_=t_[:], func=AF.Square), r=[b_t], w=[b_hid[hcl]])
                                for oc in range(16):
                                    w2_, b_w2 = w2r.next()
                                    for q2 in range(2):
                                        kb.dma(w2_[:, q2 * 16:(q2 + 1) * 16, :], mlp_w2_t[layer, oc, :, half * 32 + q2 * 16:half * 32 + (q2 + 1) * 16, :], w=[b_w2], q="pool")
                                    pbs = [4 + (nb2 + t2) % 4 for t2 in range(2)]
                                    nb2 += 2
                                    for k_ in range(32):
                                        for t2 in range(2):
                                            mm(bank(pbs[t2]), w2_[:, k_, :], hid[:, k_, t2 * 512:(t2 + 1) * 512], k_ == 0, k_ == 31, r=[b_w2, b_hid[k_]], w=[b_ps[pbs[t2]]])
                                    for t2 in range(2):
                                        tb = tt * 2 + t2
                                        x_, b_x = xr.next()
                                        kb.dma(x_[:], xT[s, oc * 128:(oc + 1) * 128, tb * 512:(tb + 1) * 512], r=[b_xT[s][tb][oc]], w=[b_x])
                                        kb.op("dve", lambda e: e.tensor_tensor(out=x_[:], in0=x_[:], in1=bank(pbs[t2]), op=ALU.add), r=[b_x, b_ps[pbs[t2]]], w=[b_x])
                                        kb.dma(xT[s, oc * 128:(oc + 1) * 128, tb * 512:(tb + 1) * 512], x_[:], r=[b_x], w=[b_xT[s][tb][oc]])

            def cross_attn(layer, s):
                with Stage(kb) as so:
                    kTm, b_kTm = so.sb("kTm", [128, 4, MEM], BF16)
                    vm, b_vm = so.sb("vm", [128, 2, 512], BF16)
                    ocaT = so.sb("ocaT", [128, 4, L], BF16)[0]
                    b_oca = [kb.buf("oca") for _ in range(4)]
                    with Stage(kb) as st:
                        memT = st.sb("memT", [128, KC, MEM], BF16)[0]
                        b_memT = [kb.buf("memT") for _ in range(KC)]
                        mrow = st.rot("mrow", [128, D], F32, 2)
                        msq, b_msq = st.sb("msq", [128, D], F32)
                        mb = st.rot("mb", [128, D], BF16, 2)
                        col = st.rot("mcol", [128, 1], F32, 2)
                        for mt in range(2):
                            mr, b_mr = mrow.next()
                            kb.dma(mr[:], mem[s, mt * 128:(mt + 1) * 128, :], w=[b_mr])
                            kb.op("dve", lambda e: e.tensor_tensor(out=msq[:], in0=mr[:], in1=mr[:], op=ALU.mult), r=[b_mr], w=[b_msq])
                            cl, b_cl = col.next()
                            kb.op("dve", lambda e: e.reduce_sum(out=cl[:], in_=msq[:], axis=AX.X), r=[b_msq], w=[b_cl])
                            rstd_inplace(cl[:], b_cl, cl[:], [b_cl], 1.0 / D)
                            mb_, b_mb = mb.next()
                            kb.op("dve", lambda e: e.tensor_scalar(out=mb_[:], in0=mr[:], scalar1=cl[:, 0:1], scalar2=None, op0=ALU.mult), r=[b_mr, b_cl], w=[b_mb])
                            for hf in range(2):
                                pb = 4 + hf
                                for j in range(8):
                                    c = hf * 8 + j
                                    kb.op("pe", lambda e: e.transpose(out=bank_bf(pb)[:, j * 128:(j + 1) * 128], in_=mb_[:, c * 128:(c + 1) * 128], identity=identb[:]),
                                          r=[b_mb, b_c], w=[b_ps[pb]], inc=(j == 7))
                                for j in range(8):
                                    c = hf * 8 + j
                                    kb.op("dve", lambda e: e.tensor_scalar(out=memT[:, c, mt * 128:(mt + 1) * 128], in0=bank_bf(pb)[:, j * 128:(j + 1) * 128],
                                                                           scalar1=gains[:, 2, layer, c:c + 1], scalar2=None, op0=ALU.mult), r=[b_ps[pb], b_c], w=[b_memT[c]])
                        wr = st.rot("cwk", [128, KC, 128], BF16, 2)
                        kf, b_kf = st.sb("kf", [128, MEM], F32)
                        ksq, b_ksq = st.sb("ksq", [128, MEM], BF16)
                        krs, b_krs = st.sb("krs", [128, MEM], F32)
                        for h in range(4):
                            wt, b_wt = wr.next()
                            kb.dma(wt[:], ca_wk_t[layer, h], w=[b_wt], q="pool")
                            for kc in range(KC):
                                mm(bank(0)[:, 0:MEM], wt[:, kc, :], memT[:, kc, :], kc == 0, kc == KC - 1, r=[b_wt, b_memT[kc]], w=[b_ps[0]])
                            kb.op("act", lambda e: e.copy(out=kf[:], in_=bank(0)[:, 0:MEM]), r=[b_ps[0]], w=[b_kf])
                            kb.op("act", lambda e: e.activation(out=ksq[:], in_=bank(0)[:, 0:MEM], func=AF.Square), r=[b_ps[0]], w=[b_ksq])
                            mm(bank(1)[:, 0:MEM], onesb[:], ksq[:], True, True, r=[b_ksq, b_c], w=[b_ps[1]])
                            rstd_inplace(krs[:], b_krs, bank(1)[:, 0:MEM], [b_ps[1]], 1.0 / 128)
                            kb.op("dve", lambda e: e.scalar_tensor_tensor(out=kTm[:, h, :], in0=kf[:], scalar=cac[:, 1, layer:layer + 1], in1=krs[:], op0=ALU.mult, op1=ALU.mult),
                                  r=[b_kf, b_krs, b_c], w=[b_kTm])
                        wv, b_wv = st.sb("cwv", [128, KC, 512], BF16)
                        kb.dma(wv[:], ca_wv_t[layer], w=[b_wv], q="pool")
                        for mt in range(2):
                            for kc in range(KC):
                                mm(bank(2 + mt), memT[:, kc, mt * 128:(mt + 1) * 128], wv[:, kc, :], kc == 0, kc == KC - 1, r=[b_wv, b_memT[kc]], w=[b_ps[2 + mt]])
                            evac(mt, vm[:, mt, :], bank(2 + mt), [b_ps[2 + mt]], [b_vm])
                    scale = 128 ** -0.5
                    with Stage(kb) as sh:
                        hT = sh.sb("hT", [128, KC, L], BF16)[0]
                        b_hT = [kb.buf("hT") for _ in range(KC)]
                        norm_to_hT(s, 1, layer, hT, b_hT)
                        with Stage(kb) as st:
                            wr = st.rot("cwq", [128, KC, 128], BF16, 2)
                            qf = st.rot("cqf", [128, L], F32, 1)
                            pool = {"sqb": st.rot("csqb", [128, L], BF16, 1), "rs": st.rot("crs", [128, L], F32, 1)}
                            qn = st.rot("cqn", [128, L], BF16, 2)
                            pt = st.rot("cpt", [128, 512], BF16, 3)
                            rc = st.rot("crc", [128, 512], F32, 2)
                            for h in range(4):
                                lin_fm(wr, ca_wq_t[layer, h], hT, b_hT, KC)
                                qf_, b_qf = qf.next()
                                kb.op("act", lambda e: e.copy(out=qf_[:].rearrange("p (b t) -> p b t", b=4), in_=psA4), r=bA, w=[b_qf])
                                rs, b_rs = rmsnorm_fm(pool, qf_[:], b_qf, onesb[:], 1.0 / 128, None, None, None)
                                qn_, b_qn = qn.next()
                                kb.op("dve", lambda e: e.scalar_tensor_tensor(out=qn_[:], in0=qf_[:], scalar=cac[:, 0, layer:layer + 1], in1=rs[:], op0=ALU.mult, op1=ALU.mult),
                                      r=[b_qf, b_rs, b_c], w=[b_qn])
                                for qb in range(4):
                                    for mc in range(2):
                                        sbk = 4 + mc
                                        mm(bank(sbk), kTm[:, h, mc * 128:(mc + 1) * 128], qn_[:, qb * 512:(qb + 1) * 512], True, True, r=[b_kTm, b_qn], w=[b_ps[sbk]])
                                        p_, b_p = pt.next()
                                        kb.op("act", lambda e: e.activation(out=p_[:], in_=bank(sbk), func=AF.Exp, scale=scale), r=[b_ps[sbk]], w=[b_p])
                                        mm(bank(6), vm[:, mc, h * 128:(h + 1) * 128], p_[:], mc == 0, mc == 1, r=[b_vm, b_p], w=[b_ps[6]])
                                        mm(bank(7), onesb[:], p_[:], mc == 0, mc == 1, r=[b_c, b_p], w=[b_ps[7]])
                                    rc_, b_rc = rc.next()
                                    kb.op("dve", lambda e: e.reciprocal(out=rc_[:], in_=bank(7)), r=[b_ps[7]], w=[b_rc])
                                    kb.op("dve", lambda e: e.tensor_tensor(out=ocaT[:, h, qb * 512:(qb + 1) * 512], in0=bank(6), in1=rc_[:], op=ALU.mult),
                                          r=[b_ps[6], b_rc], w=[b_oca[h]])
                    with Stage(kb) as st:
                        wr = st.rot("cwo", [128, 4, 128], BF16, 2)
                        xr = st.rot("xr", [128, L], F32, 2)
                        for oc in range(16):
                            lin_fm(wr, ca_wo_t[layer, oc], ocaT, b_oca, 4, pbase=4 * (oc % 2))
                            residual_add(xr, s, oc, pbase=4 * (oc % 2))

            def view4(ap):
                return ap.rearrange("p (b t) -> p b t", b=4)


            def even_mixer(s):
                with Stage(kb) as so:
                    vx2T = [so.sb("vx2T%d" % cb, [128, 16, 512], BF16) for cb in range(2)]
                    with Stage(kb) as sh:
                        hT = sh.sb("hT", [128, KC, L], BF16)[0]
                        b_hT = [kb.buf("hT") for _ in range(KC)]
                        norm_to_hT(s, 0, 0, hT, b_hT)
                        with Stage(kb) as st:
                            wr = st.rot("hw", [128, KC, 128], BF16, 2)
                            upr = st.rot("up", [128, L + 2], F32, 2)
                            for u_, b_u in upr.items:
                                kb.op("pool", lambda e: e.memset(u_[:, 0:1], 0.0), w=[b_u])
                                kb.op("pool", lambda e: e.memset(u_[:, L + 1:L + 2], 0.0), w=[b_u])
                            accr = st.rot("acc", [128, L], F32, 4)
                            vbr = st.rot("vx2b", [128, L], BF16, 2)
                            for cb in range(2):
                                for cc in range(4):
                                    ch8 = cb * 4 + cc
                                    res = {}
                                    for gi in (1, 2, 0):
                                        ti = gi * 8 + ch8
                                        lin_fm(wr, ev_w_in_t[ti], hT, b_hT, KC)
                                        up, b_up = upr.next()
                                        kb.op("act", lambda e: e.copy(out=view4(up[:, 1:L + 1]), in_=psA4), r=bA, w=[b_up])
                                        acc, b_acc = accr.next()
                                        kb.op("dve", lambda e: e.tensor_scalar(out=acc[:], in0=up[:, 1:L + 1], scalar1=hycw[:, 1, ti:ti + 1], scalar2=hycb[:, ti:ti + 1],
                                                                               op0=ALU.mult, op1=ALU.add), r=[b_up, b_c], w=[b_acc])
                                        kb.op("dve", lambda e: e.scalar_tensor_tensor(out=acc[:], in0=up[:, 0:L], scalar=hycw[:, 0, ti:ti + 1], in1=acc[:],
                                                                                       op0=ALU.mult, op1=ALU.add), r=[b_up, b_c, b_acc], w=[b_acc])
                                        kb.op("dve", lambda e: e.scalar_tensor_tensor(out=acc[:], in0=up[:, 2:L + 2], scalar=hycw[:, 2, ti:ti + 1], in1=acc[:],
                                                                                      op0=ALU.mult, op1=ALU.add), r=[b_up, b_c, b_acc], w=[b_acc])
                                        res[gi] = (acc, b_acc)
                                    x1c, b_x1 = res[0]
                                    x2c, b_x2 = res[1]
                                    vc, b_v = res[2]
                                    kb.dma(hx1[s, ch8 * 128:(ch8 + 1) * 128, :], x1c[:], r=[b_x1], w=[b_scr["hx1"][s]])
                                    kb.op("pool", lambda e: e.tensor_tensor(out=vc[:], in0=vc[:], in1=x2c[:], op=ALU.mult), r=[b_v, b_x2], w=[b_v])
                                    kb.dma(hvx[s, ch8 * 128:(ch8 + 1) * 128, :], vc[:], r=[b_v], w=[b_scr["hvx"][s]])
                                    vb_, b_vb = vbr.next()
                                    kb.op("act", lambda e: e.copy(out=vb_[:], in_=vc[:]), r=[b_v], w=[b_vb])
                                    for hf in range(2):
                                        pb = 4 + hf
                                        for j in range(8):
                                            tc = hf * 8 + j
                                            kb.op("pe", lambda e: e.transpose(out=bank_bf(pb)[:, j * 128:(j + 1) * 128], in_=vb_[:, tc * 128:(tc + 1) * 128], identity=identb[:]),
                                                  r=[b_vb, b_c], w=[b_ps[pb]], inc=(j == 7))
                                        evac(hf, vx2T[cb][0][:, hf * 8:(hf + 1) * 8, cc * 128:(cc + 1) * 128],
                                             bank_bf(pb).rearrange("p (j c) -> p j c", j=8), [b_ps[pb]], [vx2T[cb][1]])
                        with Stage(kb) as st:
                            wr = st.rot("dw", [128, KC, 128], BF16, 2)
                            qf = st.rot("dqf", [128, L], F32, 2)
                            pool_ = {"sqb": st.rot("dsqb", [128, L], BF16, 1), "rs": st.rot("drs", [128, L], F32, 1)}
                            qn = st.rot("dqn", [128, L], BF16, 2)
                            for which, dst, nm in ((0, dq, "dq"), (1, dk, "dk")):
                                for h in range(8):
                                    lin_fm(wr, ev_w_in_t[24 + which * 8 + h], hT, b_hT, KC)
                                    qf_, b_qf = qf.next()
                                    kb.op("act", lambda e: e.copy(out=view4(qf_[:]), in_=psA4), r=bA, w=[b_qf])
                                    rs, b_rs = rmsnorm_fm(pool_, qf_[:], b_qf, bd64b[:], 1.0 / 64, None, None, None)
                                    qn_, b_qn = qn.next()
                                    kb.op("dve", lambda e: e.scalar_tensor_tensor(out=qn_[:], in0=qf_[:], scalar=dfc[:, which:which + 1], in1=rs[:], op0=ALU.mult, op1=ALU.mult),
                                          r=[b_qf, b_rs, b_c], w=[b_qn])
                                    kb.dma(dst[s, h * 128:(h + 1) * 128, :], qn_[:], r=[b_qn], w=[b_scr[nm][s]])
                            wv = st.rot("dwv", [128, KC, 512], BF16, 1)
                            vrow = st.rot("dvrow", [128, 512], BF16, 3)
                            for vb in range(2):
                                wv_, b_wv = wv.next()
                                for j4 in range(4):
                                    kb.dma(wv_[:, :, j4 * 128:(j4 + 1) * 128], ev_w_in_t[40 + vb * 4 + j4], w=[b_wv], q="pool")
                                for tt in range(16):
                                    pb = 4 + tt % 4
                                    for kc in range(KC):
                                        mm(bank(pb), hT[:, kc, tt * 128:(tt + 1) * 128], wv_[:, kc, :], kc == 0, kc == KC - 1, r=[b_wv, b_hT[kc]], w=[b_ps[pb]])
                                    vr, b_vr = vrow.next()
                                    evac(tt, vr[:], bank(pb), [b_ps[pb]], [b_vr])
                                    kb.dma(dv[s, tt * 128:(tt + 1) * 128, vb * 512:(vb + 1) * 512], vr[:], r=[b_vr], w=[b_scr["dv"][s]])
                    with Stage(kb) as st:
                        ftr = st.rot("ft", [128, 16, 128], BF16, 4)
                        kt = st.rot("kt", [128, 512], F32, 6)
                        tmpr = st.rot("ht", [128, 512], F32, 4)
                        Yf, b_Yf = st.sb("Yf", [128, 32, 512], BF16)
                        gtr = st.rot("gt", [128, 16, 512], BF16, 2)
                        epr = st.rot("ep", [128, 512], F32, 4)
                        obr = st.rot("ob", [128, 512], BF16, 2)
                        for cb in range(2):
                            vt, b_vt = vx2T[cb]
                            for j in range(16):
                                ftc, b_fc = ftr.next()
                                kb.dma(ftc[:], c_FT[j], w=[b_fc])
                                fts, b_fs = ftr.next()
                                kb.dma(fts[:], c_FT[16 + j], w=[b_fs])
                                pa = (j % 2) * 2
                                pbk = pa + 1
                                for tc in range(16):
                                    mm(bank(pa), ftc[:, tc, :], vt[:, tc, :], tc == 0, tc == 15, r=[b_fc, b_vt], w=[b_ps[pa]])
                                for tc in range(16):
                                    mm(bank(pbk), fts[:, tc, :], vt[:, tc, :], tc == 0, tc == 15, r=[b_fs, b_vt], w=[b_ps[pbk]])
                                ka_, b_ka = kt.next()
                                kb.dma(ka_[:], KA[j, :, cb * 512:(cb + 1) * 512], r=[b_K], w=[b_ka])
                                kb_, b_kb = kt.next()
                                kb.dma(kb_[:], KBt[j, :, cb * 512:(cb + 1) * 512], r=[b_K], w=[b_kb])
                                if j == 0:
                                    kd_, b_kd = kt.next()
                                    kb.dma(kd_[:], KD0[:, cb * 512:(cb + 1) * 512], r=[b_K], w=[b_kd])
                                else:
                                    kd_, b_kd = ka_, b_ka
                                t1, b_t1 = tmpr.next()
                                t2, b_t2 = tmpr.next()
                                kb.op("dve", lambda e: e.tensor_tensor(out=t1[:], in0=bank(pa), in1=ka_[:], op=ALU.mult), r=[b_ps[pa], b_ka], w=[b_t1])
                                kb.op("dve", lambda e: e.tensor_tensor(out=t2[:], in0=bank(pbk), in1=kb_[:], op=ALU.mult), r=[b_ps[pbk], b_kb], w=[b_t2])
                                kb.op("pool", lambda e: e.tensor_tensor(out=Yf[:, j, :], in0=t1[:], in1=t2[:], op=ALU.subtract), r=[b_t1, b_t2], w=[b_Yf])
                                t3, b_t3 = tmpr.next()
                                t4, b_t4 = tmpr.next()
                                kb.op("dve", lambda e: e.tensor_tensor(out=t3[:], in0=bank(pa), in1=kb_[:], op=ALU.mult), r=[b_ps[pa], b_kb], w=[b_t3])
                                kb.op("dve", lambda e: e.tensor_tensor(out=t4[:], in0=bank(pbk), in1=kd_[:], op=ALU.mult), r=[b_ps[pbk], b_kd], w=[b_t4])
                                kb.op("pool", lambda e: e.tensor_tensor(out=Yf[:, 16 + j, :], in0=t3[:], in1=t4[:], op=ALU.add), r=[b_t3, b_t4], w=[b_Yf])
                            for tb in range(4):
                                for hf in range(2):
                                    gt, b_gt = gtr.next()
                                    kb.dma(gt[:], c_G[tb, hf], w=[b_gt])
                                    for cc in range(4):
                                        for fj in range(16):
                                            mm(bank(4 + cc), Yf[:, hf * 16 + fj, cc * 128:(cc + 1) * 128], gt[:, fj, :], hf == 0 and fj == 0, hf == 1 and fj == 15,
                                               r=[b_Yf, b_gt], w=[b_ps[4 + cc]], inc=(fj == 15))
                                for cc in range(4):
                                    ch8 = cb * 4 + cc
                                    vxb, b_vxb = epr.next()
                                    x1b, b_x1b = epr.next()
                                    kb.dma(vxb[:], hvx[s, ch8 * 128:(ch8 + 1) * 128, tb * 512:(tb + 1) * 512], r=[b_scr["hvx"][s]], w=[b_vxb])
                                    kb.dma(x1b[:], hx1[s, ch8 * 128:(ch8 + 1) * 128, tb * 512:(tb + 1) * 512], r=[b_scr["hx1"][s]], w=[b_x1b])
                                    kb.op("dve", lambda e: e.scalar_tensor_tensor(out=vxb[:], in0=vxb[:], scalar=hyb[:, ch8:ch8 + 1], in1=bank(4 + cc), op0=ALU.mult, op1=ALU.add),
                                          r=[b_vxb, b_c, b_ps[4 + cc]], w=[b_vxb])
                                    ob, b_ob = obr.next()
                                    kb.op("pool", lambda e: e.tensor_tensor(out=ob[:], in0=vxb[:], in1=x1b[:], op=ALU.mult), r=[b_vxb, b_x1b], w=[b_ob])
                                    kb.dma(oT[s, ch8 * 128:(ch8 + 1) * 128, tb * 512:(tb + 1) * 512], ob[:], r=[b_ob], w=[b_oT[s][ch8]])
                scale = 64 ** -0.5
                with Stage(kb) as st:
                    base, b_base = st.sb("alibi", [128, 3968], F32)
                    kb.dma(base[:], c_alibi[:, :], w=[b_base])
                    qzr = Rot([(st.sb("aqz0%d" % i, [128, L], BF16), st.sb("aqz1%d" % i, [128, L], BF16)) for i in range(2)])
                    for (z0, b_z0), (z1, b_z1) in qzr.items:
                        kb.op("pool", lambda e: e.memset(z0[64:128, :], 0.0), w=[b_z0])
                        kb.op("pool", lambda e: e.memset(z1[0:64, :], 0.0), w=[b_z1])
                    kr = st.rot("ak", [128, L], BF16, 2)
                    vr = st.rot("av", [128, 16, 128], BF16, 2)
                    tmpr = st.rot("as", [128, 512], BF16, 3)
                    ptr = st.rot("ap", [128, 512], BF16, 3)
                    etr = st.rot("aet", [128, 3968], BF16, 2)
                    rcr = st.rot("arc", [128, 512], F32, 2)
                    rr = st.rot("ar", [128, 512], F32, 4)
                    osr = st.rot("aos", [128, 512], F32, 2)
                    sqr = st.rot("asq", [128, 512], BF16, 2)
                    rsr = st.rot("ars", [128, 512], F32, 2)
                    obr = st.rot("aob", [128, 512], BF16, 2)
                    LA = 2
                    iters = [(h, qb, j, kc) for h in range(8) for qb in range(4) for j in range(2) for kc in range(16)]
                    heads = {}
                    etabs = {}
                    rj = {}

                    def emit_S(i):
                        h, qb, j, kc = iters[i]
                        if h not in heads:
                            (z0, b_z0), (z1, b_z1) = qzr.next()
                            kT_, b_k = kr.next()
                            v_, b_v = vr.next()
                            kb.dma(z0[0:64, :], dq[s, h * 128:h * 128 + 64, :], r=[b_scr["dq"][s]], w=[b_z0])
                            kb.dma(z1[64:128, :], dq[s, h * 128 + 64:(h + 1) * 128, :], r=[b_scr["dq"][s]], w=[b_z1])
                            qT_, b_q = (z0, z1), (b_z0, b_z1)
                            et_, b_et = etr.next()
                            kb.op("act", lambda e: e.activation(out=et_[:], in_=base[:], func=AF.Exp, scale=-(2.0 ** (-(h + 1)))), r=[b_base], w=[b_et])
                            etabs[h] = (et_, b_et)
                            kb.dma(kT_[:], dk[s, h * 128:(h + 1) * 128, :], r=[b_scr["dk"][s]], w=[b_k])
                            kb.dma(v_[:], dv[s].rearrange("(tc p) c -> p tc c", p=128)[:, :, h * 128:(h + 1) * 128], r=[b_scr["dv"][s]], w=[b_v])
                            heads[h] = (qT_, b_q, kT_, b_k, v_, b_v)
                        qT_, b_q, kT_, b_k, v_, b_v = heads[h]
                        sb_ = i % 3
                        mm(bank(sb_), kT_[:, kc * 128:(kc + 1) * 128], qT_[j][:, qb * 512:(qb + 1) * 512], True, True,
                           r=[b_k, b_q[j]], w=[b_ps[sb_]])

                    def emit_rest(i):
                        h, qb, j, kc = iters[i]
                        slope = 2.0 ** (-(h + 1))
                        qT_, b_q, kT_, b_k, v_, b_v = heads[h]
                        sb_ = i % 3
                        off = qb * 512 - kc * 128 + 1920
                        et_, b_et = etabs[h]
                        t_, b_t = tmpr.next()
                        kb.op("act", lambda e: e.activation(out=t_[:], in_=bank(sb_), func=AF.Exp, scale=scale), r=[b_ps[sb_]], w=[b_t])
                        p_, b_p = ptr.next()
                        kb.op("dve", lambda e: e.tensor_tensor(out=p_[:], in0=t_[:], in1=et_[:, off:off + 512], op=ALU.mult), r=[b_t, b_et], w=[b_p])
                        mm(bank(4 + j), v_[:, kc, :], p_[:], kc == 0, kc == 15, r=[b_v, b_p], w=[b_ps[4 + j]])
                        mm(bank(6 + j), onesb[:], p_[:], kc == 0, kc == 15, r=[b_c, b_p], w=[b_ps[6 + j]])
                        if kc != 15:
                            return
                        rc_, b_rc = rcr.next()
                        kb.op("act", lambda e: e.activation(out=rc_[:], in_=bank(6 + j), func=AF.Ln), r=[b_ps[6 + j]], w=[b_rc])
                        kb.op("act", lambda e: e.activation(out=rc_[:], in_=rc_[:], func=AF.Exp, scale=-1.0), r=[b_rc], w=[b_rc])
                        os_, b_os = osr.next()
                        kb.op("act", lambda e: e.copy(out=os_[:], in_=bank(4 + j)), r=[b_ps[4 + j]], w=[b_os])
                        r_, b_r = rr.next()
                        kb.op("pool", lambda e: e.tensor_tensor(out=r_[:], in0=os_[:], in1=rc_[:], op=ALU.mult), r=[b_os, b_rc], w=[b_r])
                        rj[j] = (r_, b_r)
                        if j != 1:
                            return
                        (r0, b_r0), (r1, b_r1) = rj[0], rj[1]
                        kb.op("dve", lambda e: e.scalar_tensor_tensor(out=r0[:], in0=r1[:], scalar=neglam[:, 0:1], in1=r0[:], op0=ALU.mult, op1=ALU.add),
                              r=[b_r0, b_r1, b_c], w=[b_r0])
                        sq_, b_sq = sqr.next()
                        kb.op("pool", lambda e: e.tensor_tensor(out=sq_[:], in0=r0[:], in1=r0[:], op=ALU.mult), r=[b_r0], w=[b_sq])
                        mm(bank(3), onesb[:], sq_[:], True, True, r=[b_c, b_sq], w=[b_ps[3]])
                        rs_, b_rs = rsr.next()
                        kb.op("dve", lambda e: e.tensor_scalar(out=rs_[:], in0=bank(3), scalar1=1.0 / 128, scalar2=EPS, op0=ALU.mult, op1=ALU.add), r=[b_ps[3]], w=[b_rs])
                        kb.op("act", lambda e: e.activation(out=rs_[:], in_=rs_[:], func=AF.Ln), r=[b_rs], w=[b_rs])
                        kb.op("act", lambda e: e.activation(out=rs_[:], in_=rs_[:], func=AF.Exp, scale=-0.5), r=[b_rs], w=[b_rs])
                        ob, b_ob = obr.next()
                        kb.op("dve", lambda e: e.scalar_tensor_tensor(out=ob[:], in0=r0[:], scalar=sg2[:, 0:1], in1=rs_[:], op0=ALU.mult, op1=ALU.mult),
                              r=[b_r0, b_rs, b_c], w=[b_ob])
                        kb.dma(oT[s, 1024 + h * 128:1024 + (h + 1) * 128, qb * 512:(qb + 1) * 512], ob[:], r=[b_ob], w=[b_oT[s][8 + h]])

                    n_it = len(iters)
                    for i in range(n_it + LA):
                        if i < n_it:
                            emit_S(i)
                        if i - LA >= 0:
                            emit_rest(i - LA)


            C.even_mixer = even_mixer


            def odd_mixer(s):
                with Stage(kb) as sh:
                    hT = sh.sb("hT", [128, KC, L], BF16)[0]
                    b_hT = [kb.buf("hT") for _ in range(KC)]
                    norm_to_hT(s, 0, 1, hT, b_hT)
                    with Stage(kb) as st:
                        cosT, b_cs = st.sb("cosT", [128, L], F32)
                        sinT, _ = st.sb("sinT", [128, L], F32)
                        kb.dma(cosT[:], c_cos[:, :], w=[b_cs])
                        kb.dma(sinT[:], c_sin[:, :], w=[b_cs])
                        wr = st.rot("gw", [128, KC, 128], BF16, 2)
                        qf = st.rot("gqf", [128, L], F32, 1)
                        qb16 = st.rot("gqb", [128, L], BF16, 1)
                        pool_ = {"sqb": st.rot("gsqb", [128, L], BF16, 1), "rs": st.rot("grs", [128, L], F32, 1)}
                        ta = st.rot("gta", [128, L], F32, 1)
                        tb_ = st.rot("gtb", [128, L], F32, 1)
                        qn = st.rot("gqn", [128, L], BF16, 2)
                        for hh in range(10):
                            isk = hh >= 8
                            ti = 40 + hh
                            lin_fm(wr, od_w_in_t[ti], hT, b_hT, KC)
                            qf_, b_qf = qf.next()
                            kb.op("act", lambda e: e.copy(out=view4(qf_[:]), in_=psA4), r=bA, w=[b_qf])
                            q16, b_q16 = qb16.next()
                            kb.op("act", lambda e: e.copy(out=view4(q16[:]), in_=psA4), r=bA, w=[b_q16])
                            rs, b_rs = rmsnorm_fm(pool_, qf_[:], b_qf, onesb[:], 1.0 / 128, None, None, None)
                            for tb in range(4):
                                mm(bank(4 + tb), RgT[:, 1 if isk else 0, :], q16[:, tb * 512:(tb + 1) * 512], True, True, r=[b_c, b_q16], w=[b_ps[4 + tb]])
                            a_, b_a = ta.next()
                            b__, b_b = tb_.next()
                            gcol = odc[:, 2:3] if isk else odc[:, 1:2]
                            kb.op("dve", lambda e: e.scalar_tensor_tensor(out=a_[:], in0=qf_[:], scalar=gcol, in1=cosT[:], op0=ALU.mult, op1=ALU.mult), r=[b_qf, b_c, b_cs], w=[b_a])
                            kb.op("dve", lambda e: e.tensor_tensor(out=view4(b__[:]), in0=psB4, in1=view4(sinT[:]), op=ALU.mult), r=bB + [b_cs], w=[b_b])
                            kb.op("pool", lambda e: e.tensor_tensor(out=a_[:], in0=a_[:], in1=b__[:], op=ALU.add), r=[b_a, b_b], w=[b_a])
                            qn_, b_qn = qn.next()
                            kb.op("dve", lambda e: e.tensor_tensor(out=qn_[:], in0=a_[:], in1=rs[:], op=ALU.mult), r=[b_a, b_rs], w=[b_qn])
                            if isk:
                                kb.dma(gk[s, (hh - 8) * 128:(hh - 7) * 128, :], qn_[:], r=[b_qn], w=[b_scr["gk"][s]])
                            else:
                                kb.dma(dq[s, hh * 128:(hh + 1) * 128, :], qn_[:], r=[b_qn], w=[b_scr["dq"][s]])
                        wv, b_wv = st.sb("gwv", [128, KC, 256], BF16)
                        for j4 in range(2):
                            kb.dma(wv[:, :, j4 * 128:(j4 + 1) * 128], od_w_in_t[50 + j4], w=[b_wv], q="pool")
                        vrow = st.rot("gvrow", [128, 256], BF16, 3)
                        for tt in range(16):
                            pb = 4 + tt % 4
                            for kc in range(KC):
                                mm(bank(pb)[:, 0:256], hT[:, kc, tt * 128:(tt + 1) * 128], wv[:, kc, :], kc == 0, kc == KC - 1, r=[b_wv, b_hT[kc]], w=[b_ps[pb]])
                            vr, b_vr = vrow.next()
                            evac(tt, vr[:], bank(pb)[:, 0:256], [b_ps[pb]], [b_vr])
                            kb.dma(gv[s, tt * 128:(tt + 1) * 128, :], vr[:], r=[b_vr], w=[b_scr["gv"][s]])
                        wi = st.rot("gwi", [128, KC, 512], BF16, 1)
                        irow = st.rot("girow", [128, 512], BF16, 3)
                        for vb in range(2):
                            wi_, b_wi = wi.next()
                            for j4 in range(4):
                                kb.dma(wi_[:, :, j4 * 128:(j4 + 1) * 128], od_w_in_t[24 + vb * 4 + j4], w=[b_wi], q="pool")
                            for tt in range(16):
                                pb = 4 + tt % 4
                                for kc in range(KC):
                                    mm(bank(pb), hT[:, kc, tt * 128:(tt + 1) * 128], wi_[:, kc, :], kc == 0, kc == KC - 1, r=[b_wi, b_hT[kc]], w=[b_ps[pb]])
                                ir, b_ir = irow.next()
                                evac(tt, ir[:], bank(pb), [b_ps[pb]], [b_ir])
                                kb.dma(dv[s, tt * 128:(tt + 1) * 128, vb * 512:(vb + 1) * 512], ir[:], r=[b_ir], w=[b_scr["dv"][s]])
                    with Stage(kb) as st:
                        msk, b_msk = st.sb("scanmask", [128, L], F32)
                        kb.op("pool", lambda e: e.memset(msk[:], 1.0), w=[b_msk])
                        kb.op("pool", lambda e: e.memset(msk[:].rearrange("p (c j) -> p c j", j=64)[:, :, 0:1], 0.0), w=[b_msk])
                        cm, b_cm = st.sb("cmask", [64, 2, 512], F32)
                        kb.dma(cm[:], c_mask[:, :, :], w=[b_cm])
                        wr = st.rot("hgw", [128, KC, 128], BF16, 2)
                        qs, b_qs = st.sb("hqs", [128, L], F32)
                        vtok, b_vtok = st.sb("hvtok", [64, 32, 128], BF16)
                        fa, b_fa = st.sb("hfa", [128, L], F32)
                        ka, b_kk = st.sb("hka", [128, L], F32)
                        Bc, b_B = st.sb("hB", [128, L], F32)
                        eb, b_eb = st.sb("heb", [128, L], F32)
                        enb, b_enb = st.sb("henb", [128, L], F32)
                        ebl, b_ebl = st.sb("hebl", [128, 32], F32)
                        qe, b_qe = st.sb("hqe", [128, L], BF16)
                        ke, b_ke = st.sb("hke", [128, L], BF16)
                        klT, b_klT = st.sb("hklT", [128, L], BF16)
                        kltok, b_kltok = st.sb("hkltok", [64, 32, 128], BF16)
                        attT, b_att = st.sb("hatt", [64, 32, 64], BF16)
                        sgt, b_sgt = enb, b_enb
                        Sbr = st.rot("hSb", [128, 128], BF16, 6)
                        oacc, b_oacc = st.sb("hoacc", [128, L], F32)
                        oaccb, b_oaccb = st.sb("hoaccb", [128, L], F32)
                        pool_ = {"sqb": st.rot("hsqb", [128, L], BF16, 1), "rs": st.rot("hrs", [128, L], F32, 1)}
                        ob, b_ob = st.sb("hob", [128, L], BF16)
                        c3 = lambda ap: ap.rearrange("p (c j) -> p c j", j=64)
                        for h in range(8):
                            kb.dma(vtok[:], dv[s].rearrange("(ch p) c -> p ch c", p=64)[:, :, h * 128:(h + 1) * 128], r=[b_scr["dv"][s]], w=[b_vtok])
                            lin_fm(wr, od_w_in_t[h], hT, b_hT, KC)
                            kb.op("act", lambda e: e.mul(out=view4(qs[:]), in_=psA4, mul=128 ** -0.5), r=bA, w=[b_qs])
                            for di in range(2):
                                lin_fm(wr, od_w_in_t[8 + di * 8 + h], hT, b_hT, KC)
                                kb.op("act", lambda e: e.activation(out=view4(fa[:]), in_=psA4, func=AF.Sigmoid), r=bA, w=[b_fa])
                                kb.op("dve", lambda e: e.tensor_scalar(out=ka[:], in0=fa[:], scalar1=noml[:, di, h:h + 1], scalar2=oml[:, di, h:h + 1], op0=ALU.mult, op1=ALU.add),
                                      r=[b_fa, b_c], w=[b_kk])
                                kb.op("act", lambda e: e.activation(out=fa[:], in_=fa[:], func=AF.Ln, scale=oml[:, di, h:h + 1], bias=lbs[:, di, h:h + 1]), r=[b_fa, b_c], w=[b_fa])
                                kb.op("dve", lambda e: e.tensor_tensor_scan(out=Bc[:], data0=msk[:], data1=fa[:], initial=0.0, op0=ALU.mult, op1=ALU.add), r=[b_msk, b_fa], w=[b_B])
                                if di == 1:
                                    kb.op("dve", lambda e: e.tensor_tensor(out=c3(eb[:]), in0=c3(Bc[:]), in1=c3(Bc[:])[:, :, 63:64].to_broadcast([128, 32, 64]), op=ALU.subtract),
                                          r=[b_B], w=[b_eb])
                                    kb.op("dve", lambda e: e.tensor_tensor(out=eb[:], in0=eb[:], in1=fa[:], op=ALU.subtract), r=[b_eb, b_fa], w=[b_eb])
                                    kb.op("act", lambda e: e.activation(out=enb[:], in_=eb[:], func=AF.Exp, scale=1.0), r=[b_eb], w=[b_enb])
                                    kb.op("act", lambda e: e.activation(out=eb[:], in_=eb[:], func=AF.Exp, scale=-1.0), r=[b_eb], w=[b_eb])
                                    last = 0
                                else:
                                    kb.op("act", lambda e: e.activation(out=eb[:], in_=Bc[:], func=AF.Exp, scale=1.0), r=[b_B], w=[b_eb])
                                    kb.op("act", lambda e: e.activation(out=enb[:], in_=Bc[:], func=AF.Exp, scale=-1.0), r=[b_B], w=[b_enb])
                                    last = 63
                                kb.op("pool", lambda e: e.tensor_copy(out=ebl[:], in_=c3(eb[:])[:, :, last]), r=[b_eb], w=[b_ebl])
                                kb.op("dve", lambda e: e.tensor_tensor(out=qe[:], in0=qs[:], in1=eb[:], op=ALU.mult), r=[b_qs, b_eb], w=[b_qe])
                                kb.op("dve", lambda e: e.tensor_tensor(out=ke[:], in0=ka[:], in1=enb[:], op=ALU.mult), r=[b_kk, b_enb], w=[b_ke])
                                kb.op("dve", lambda e: e.tensor_tensor(out=c3(klT[:]), in0=c3(ke[:]), in1=ebl[:].unsqueeze(2).to_broadcast([128, 32, 64]), op=ALU.mult),
                                      r=[b_ke, b_ebl], w=[b_klT])
                                for g4 in range(4):
                                    pb = 4 + g4
                                    for j in range(8):
                                        ch = g4 * 8 + j
                                        kb.op("pe", lambda e: e.transpose(out=bank_bf(pb)[0:64, j * 128:(j + 1) * 128], in_=klT[:, ch * 64:(ch + 1) * 64], identity=identb[:]),
                                              r=[b_klT, b_c], w=[b_ps[pb]], inc=(j == 7))
                                    evac(g4, kltok[:, g4 * 8:(g4 + 1) * 8, :], bank_bf(pb)[0:64, :].rearrange("p (j c) -> p j c", j=8), [b_ps[pb]], [b_kltok])
                                for g4 in range(4):
                                    pb = 4 + g4
                                    for j in range(8):
                                        ch = g4 * 8 + j
                                        mm(bank(pb)[0:64, j * 64:(j + 1) * 64], ke[:, ch * 64:(ch + 1) * 64], qe[:, ch * 64:(ch + 1) * 64], True, True,
                                           r=[b_ke, b_qe], w=[b_ps[pb]], inc=(j == 7))
                                    kb.op("dve", lambda e: e.tensor_tensor(out=attT[:, g4 * 8:(g4 + 1) * 8, :], in0=bank(pb)[0:64, :].rearrange("p (j c) -> p j c", j=8),
                                                                           in1=cm[:, di, :].rearrange("p (j c) -> p j c", j=8), op=ALU.mult), r=[b_ps[pb], b_cm], w=[b_att])
                                order = list(range(32)) if di == 0 else list(range(31, -1, -1))
                                Sb, b_Sb = Sbr.next()
                                kb.op("pool", lambda e: e.memset(Sb[:], 0.0), w=[b_Sb])
                                odst = oacc if di == 0 else oaccb
                                b_odst = b_oacc if di == 0 else b_oaccb
                                for gi in range(8):
                                    pb = 2 + gi % 2
                                    for j in range(4):
                                        ch = order[gi * 4 + j]
                                        mm(bank(pb)[:, j * 128:(j + 1) * 128], kltok[:, ch, :], vtok[:, ch, :], True, True, r=[b_kltok, b_vtok], w=[b_ps[pb]], inc=(j == 3))
                                    for j in range(4):
                                        i = gi * 4 + j
                                        ch = order[i]
                                        po = (i // 8) % 2
                                        sl = ch % 8
                                        mm(bank(po)[:, sl * 64:(sl + 1) * 64], Sb[:], qe[:, ch * 64:(ch + 1) * 64], True, False, r=[b_Sb, b_qe], w=[b_ps[po]], inc=False)
                                        mm(bank(po)[:, sl * 64:(sl + 1) * 64], vtok[:, ch, :], attT[:, ch, :], False, True, r=[b_vtok, b_att], w=[b_ps[po]])
                                        Sn, b_Sn = Sbr.next()
                                        kb.op("dve", lambda e: e.scalar_tensor_tensor(out=Sn[:], in0=Sb[:], scalar=ebl[:, ch:ch + 1], in1=bank(pb)[:, j * 128:(j + 1) * 128], op0=ALU.mult, op1=ALU.add),
                                              r=[b_Sb, b_ebl, b_ps[pb]], w=[b_Sn])
                                        Sb, b_Sb = Sn, b_Sn
                                        if i % 8 == 7:
                                            g8 = ch // 8
                                            kb.op("act", lambda e: e.copy(out=odst[:, g8 * 512:(g8 + 1) * 512], in_=bank(po)), r=[b_ps[po]], w=[b_odst])
                            kb.op("pool", lambda e: e.tensor_tensor(out=oacc[:], in0=oacc[:], in1=oaccb[:], op=ALU.add), r=[b_oacc, b_oaccb], w=[b_oacc])
                            lin_fm(wr, od_w_in_t[32 + h], hT, b_hT, KC)
                            kb.op("act", lambda e: e.activation(out=view4(sgt[:]), in_=psA4, func=AF.Silu), r=bA, w=[b_sgt])
                            rs, b_rs = rmsnorm_fm(pool_, oacc[:], b_oacc, onesb[:], 1.0 / 128, None, None, None)
                            kb.op("dve", lambda e: e.scalar_tensor_tensor(out=oacc[:], in0=oacc[:], scalar=odc[:, 0:1], in1=rs[:], op0=ALU.mult, op1=ALU.mult), r=[b_oacc, b_rs, b_c], w=[b_oacc])
                            kb.op("pool", lambda e: e.tensor_tensor(out=ob[:], in0=oacc[:], in1=sgt[:], op=ALU.mult), r=[b_oacc, b_sgt], w=[b_ob])
                            kb.dma(oT[s, h * 128:(h + 1) * 128, :], ob[:], r=[b_ob], w=[b_oT[s][h]])
                scale = 128 ** -0.5
                with Stage(kb) as st:
                    qr = st.rot("bq", [128, L], BF16, 2)
                    kr = st.rot("bk", [128, L], BF16, 2)
                    vr = st.rot("bv", [128, 16, 128], BF16, 2)
                    ptr = st.rot("bp", [128, 512], BF16, 3)
                    rcr = st.rot("brc", [128, 512], F32, 2)
                    obr = st.rot("bob", [128, 512], BF16, 2)
                    LA = 2
                    iters = [(hq, qb, kc) for hq in range(8) for qb in range(4) for kc in range(16)]
                    heads = {}
                    kvs = {}

                    def emit_S(i):
                        hq, qb, kc = iters[i]
                        g_ = hq // 4
                        if g_ not in kvs:
                            kT_, b_k = kr.next()
                            v_, b_v = vr.next()
                            kb.dma(kT_[:], gk[s, g_ * 128:(g_ + 1) * 128, :], r=[b_scr["gk"][s]], w=[b_k])
                            kb.dma(v_[:], gv[s].rearrange("(tc p) c -> p tc c", p=128)[:, :, g_ * 128:(g_ + 1) * 128], r=[b_scr["gv"][s]], w=[b_v])
                            kvs[g_] = (kT_, b_k, v_, b_v)
                        if hq not in heads:
                            qT_, b_q = qr.next()
                            kb.dma(qT_[:], dq[s, hq * 128:(hq + 1) * 128, :], r=[b_scr["dq"][s]], w=[b_q])
                            heads[hq] = (qT_, b_q)
                        kT_, b_k, v_, b_v = kvs[g_]
                        qT_, b_q = heads[hq]
                        sb_ = i % 3
                        mm(bank(sb_), kT_[:, kc * 128:(kc + 1) * 128], qT_[:, qb * 512:(qb + 1) * 512], True, True, r=[b_k, b_q], w=[b_ps[sb_]])

                    def emit_rest(i):
                        hq, qb, kc = iters[i]
                        g_ = hq // 4
                        kT_, b_k, v_, b_v = kvs[g_]
                        grp = hq * 4 + qb
                        po = 4 + grp % 2
                        pn = 6 + grp % 2
                        sb_ = i % 3
                        p_, b_p = ptr.next()
                        kb.op("act", lambda e: e.activation(out=p_[:], in_=bank(sb_), func=AF.Exp, scale=scale), r=[b_ps[sb_]], w=[b_p])
                        mm(bank(po), v_[:, kc, :], p_[:], kc == 0, kc == 15, r=[b_v, b_p], w=[b_ps[po]])
                        mm(bank(pn), onesb[:], p_[:], kc == 0, kc == 15, r=[b_c, b_p], w=[b_ps[pn]])
                        if kc != 15:
                            return
                        rc_, b_rc = rcr.next()
                        kb.op("dve", lambda e: e.reciprocal(out=rc_[:], in_=bank(pn)), r=[b_ps[pn]], w=[b_rc])
                        ob, b_ob = obr.next()
                        kb.op("dve", lambda e: e.tensor_tensor(out=ob[:], in0=bank(po), in1=rc_[:], op=ALU.mult), r=[b_ps[po], b_rc], w=[b_ob])
                        kb.dma(oT[s, 1024 + hq * 128:1024 + (hq + 1) * 128, qb * 512:(qb + 1) * 512], ob[:], r=[b_ob], w=[b_oT[s][8 + hq]])

                    n_it = len(iters)
                    for i in range(n_it + LA):
                        if i < n_it:
                            emit_S(i)
                        if i - LA >= 0:
                            emit_rest(i - LA)


            C.odd_mixer = odd_mixer

            for layer in range(2):
                for s in range(SPC):
                    if ("mix%d" % layer) in stages:
                        if layer == 0:
                            C.even_mixer(s)
                            out_proj(s, ev_w_out_t)
                        else:
                            C.odd_mixer(s)
                            out_proj(s, od_w_out_t)
                    if ("ca%d" % layer) in stages:
                        cross_attn(layer, s)
                    if ("mlp%d" % layer) in stages:
                        mlp(layer, s)

            if "epi" in stages:
                with Stage(kb) as st:
                    blk = st.rot("blk", [128, KC, 512], F32, 2)
                    yo = st.rot("yo", [128, D], F32, 2)
                    ev = 0
                    for s in range(SPC):
                        for tb in range(4):
                            bl, b_bl = blk.next()
                            kb.dma(bl[:], xTv[s][:, :, tb * 512:(tb + 1) * 512], r=xblk(s, tb), w=[b_bl])
                            for tt in range(4):
                                yt, b_yt = yo.next()
                                for j in range(4):
                                    for c4 in range(4):
                                        c = j * 4 + c4
                                        kb.op("pe", lambda e: e.transpose(out=bank(j)[:, c4 * 128:(c4 + 1) * 128],
                                                                          in_=bl[:, c, tt * 128:(tt + 1) * 128], identity=identf[:]),
                                              r=[b_bl, b_c], w=[b_ps[j]], inc=(c4 == 3))
                                    evac(ev, yt[:, j * 512:(j + 1) * 512], bank(j), [b_ps[j]], [b_yt])
                                    ev += 1
                                t0 = tb * 512 + tt * 128
                                kb.dma(y[s, t0:t0 + 128, :], yt[:], r=[b_yt], w=[])
        kb.final_wait()
    return C


def _tile_w(W, ncol):
    K, N = W.shape
    return np.ascontiguousarray(W.reshape(K // 128, 128, N // ncol, ncol).transpose(2, 1, 0, 3))


def _col(v, n=128):
    return np.ascontiguousarray(np.asarray(v).reshape(-1, n).T)


def host_consts():
    c = {}
    c["c_identf"] = np.eye(128, dtype=np.float32)
    bd = np.zeros((128, 128), np.float32)
    bd[:64, :64] = 1.0
    bd[64:, 64:] = 1.0
    c["c_bd64"] = bd
    p = np.arange(128)[:, None]
    j = np.arange(3968)[None, :]
    c["c_alibi"] = np.abs(j - p - 1920).astype(np.float32)
    t = np.linspace(0.0, 1.0, L, dtype=np.float32)[:, None]
    w = (np.float32(2.0 * math.pi / L) * np.arange(L, dtype=np.float32))[:, None]
    f = np.linspace(1e-4, 15, 16, dtype=np.float32)[None, :]
    z = np.concatenate([t, np.cos(f * w), -np.sin(f * w)], axis=-1).astype(np.float32)
    c["c_zT"] = np.ascontiguousarray(z.T)
    max_decay = math.log(1e-2) / 0.3
    min_decay = math.log(1e-2) / 1.5
    deltas = np.linspace(min_decay, max_decay, 1024, dtype=np.float32)
    win = np.exp(-t * np.abs(deltas)[None, :]).astype(np.float32)
    c["c_win"] = np.ascontiguousarray(win.reshape(16, 128, 1024))
    m0 = np.ones((128, 2), np.float32)
    m0[0, 0] = 0.0
    m0[:, 1] = 1.0 - m0[:, 0]
    c["c_m0"] = m0
    N = 2 * L
    tt = np.arange(L, dtype=np.float64)[:, None]
    ff = np.arange(L, dtype=np.float64)[None, :]
    ang = 2.0 * np.pi * tt * ff / N
    FT = np.concatenate([np.cos(ang), np.sin(ang)], axis=1)
    FT[:, L] = np.where(np.arange(L) % 2 == 0, 1.0, -1.0)
    FTt = FT.reshape(16, 128, 32, 128).transpose(2, 1, 0, 3)
    c["c_FT"] = np.ascontiguousarray(FTt).astype(ml_dtypes.bfloat16)
    wf = np.full((L,), 2.0 / N)
    wf[0] = 1.0 / N
    G = np.concatenate([(np.cos(ang) * wf[None, :]).T, (np.sin(ang) * (2.0 / N)).T], axis=0)
    G[L, :] = np.where(np.arange(L) % 2 == 0, 1.0, -1.0) / N
    Gt = G.reshape(2, 16, 128, 4, 512).transpose(3, 0, 2, 1, 4)
    c["c_G"] = np.ascontiguousarray(Gt).astype(ml_dtypes.bfloat16)
    rows = L // 64
    r = np.repeat(np.arange(rows, dtype=np.float32), 64)
    cc = np.tile(np.arange(64, dtype=np.float32), rows)
    half = 64
    inv = (np.float32(10000.0) ** (-np.arange(0, half, 2, dtype=np.float32) / half)).astype(np.float32)
    ang_r = r[:, None] * inv[None, :]
    ang_c = cc[:, None] * inv[None, :]
    a = np.concatenate([ang_r, ang_r, ang_c, ang_c], axis=-1).astype(np.float32)
    c["c_cos"] = np.ascontiguousarray(np.cos(a).T.astype(np.float32))
    c["c_sin"] = np.ascontiguousarray(np.sin(a).T.astype(np.float32))
    Pm = np.zeros((128, 128), np.float32)
    for m in range(32):
        Pm[m, m + 32] = -1.0
        Pm[m + 32, m] = 1.0
        Pm[m + 64, m + 96] = -1.0
        Pm[m + 96, m + 64] = 1.0
    c["c_PT"] = np.ascontiguousarray(Pm.T)
    s_ = np.arange(64)[:, None]
    t_ = np.arange(64)[None, :]
    mk = np.stack([np.tile((s_ <= t_).astype(np.float32), (1, 8)), np.tile((s_ >= t_).astype(np.float32), (1, 8))], axis=1)
    c["c_mask"] = np.ascontiguousarray(mk)
    return c


def host_layout(inp):
    o = {}
    g = np.stack([inp["norm_mix_g"], inp["norm_mem_q_g"], inp["norm_mem_kv_g"], inp["norm_ffn_g"]], axis=0)
    o["gains"] = np.ascontiguousarray(g.reshape(4, 2, KC, 128).transpose(3, 0, 1, 2))
    ev = inp["ev_w_in"][0]
    o["ev_w_in_t"] = _tile_w(ev, 128)
    o["ev_w_out_t"] = _tile_w(inp["ev_w_out"][0], 128)
    od = inp["od_w_in"][0]
    o["od_w_in_t"] = _tile_w(od, 128)
    o["od_w_out_t"] = _tile_w(inp["od_w_out"][0], 128)
    o["ca_wq_t"] = np.stack([_tile_w(inp["ca_w_q"][l], 128) for l in range(2)])
    o["ca_wk_t"] = np.stack([_tile_w(inp["ca_w_kv"][l][:, :512], 128) for l in range(2)])
    o["ca_wv_t"] = np.stack([_tile_w(inp["ca_w_kv"][l][:, 512:], 512)[0] for l in range(2)])
    o["ca_wo_t"] = np.stack([_tile_w(inp["ca_w_out"][l], 128) for l in range(2)])
    o["mlp_w1_t"] = np.stack([_tile_w(inp["mlp_w1"][l], 128) for l in range(2)])
    o["mlp_w2_t"] = np.stack([_tile_w(inp["mlp_w2"][l], 128) for l in range(2)])
    cw = inp["hy_conv_w"][0]
    o["hy_cw"] = np.ascontiguousarray(cw.reshape(3, 24, 128).transpose(2, 0, 1))
    o["hy_cb"] = _col(inp["hy_conv_b"][0])
    o["hy_bias"] = _col(inp["hy_bias"][0])
    o["hy_fw1"] = np.ascontiguousarray(inp["hy_fw1"][0])
    o["hy_fw2"] = np.ascontiguousarray(inp["hy_fw2"][0])
    o["hy_fw3"] = np.ascontiguousarray(inp["hy_fw3"][0])
    o["hy_fw4"] = np.ascontiguousarray(inp["hy_fw4"][0])
    o["hy_fcols"] = np.ascontiguousarray(np.stack([inp["hy_fb1"][0], inp["hy_fb2"][0], inp["hy_fb3"][0], inp["hy_sin_freq"][0]], axis=1))
    qg = np.concatenate([inp["df_q_g"][0], inp["df_q_g"][0]])
    kg = np.concatenate([inp["df_k_g"][0], inp["df_k_g"][0]])
    o["df_cols"] = np.ascontiguousarray(np.stack([qg, kg, inp["df_subln_g"][0]], axis=1))
    o["df_lam"] = np.ascontiguousarray(np.stack([inp["df_lam_q1"][0], inp["df_lam_k1"][0], inp["df_lam_q2"][0], inp["df_lam_k2"][0]])[None])
    lb = inp["hg_lb_logits"]
    o["hg_lb"] = np.ascontiguousarray(lb.reshape(2, 2, 8, 128).transpose(3, 0, 1, 2))
    o["od_cols"] = np.ascontiguousarray(np.stack([inp["hg_norm_g"][0], inp["gq_q_g"][0], inp["gq_k_g"][0]], axis=1))
    o["ca_cols"] = np.ascontiguousarray(np.stack([inp["ca_q_g"], inp["ca_k_g"]], axis=0).transpose(2, 0, 1))
    return {k: np.ascontiguousarray(v, dtype=np.float32) for k, v in o.items()}


_CACHE = {}


def run(inputs, ncores=NCORES, stages=ALL_STAGES, debug=()):
    key = (tuple(stages), tuple(debug))
    if key not in _CACHE:
        _CACHE[key] = build(stages=stages, debug=debug)
    if "consts" not in _CACHE:
        _CACHE["consts"] = host_consts()
    C = _CACHE[key]
    shared = dict(_CACHE["consts"])
    shared.update(host_layout(inputs))
    x = np.ascontiguousarray(inputs["x"], dtype=np.float32)
    mem = np.ascontiguousarray(inputs["mem"], dtype=np.float32)
    in_maps = []
    for c in range(ncores):
        m = {"x": x[c * SPC:(c + 1) * SPC], "mem": mem[c * SPC:(c + 1) * SPC]}
        m.update(shared)
        in_maps.append({k: v for k, v in m.items() if k in C.din})
    res = run_bass_kernel_spmd(C.nc, in_maps, core_ids=list(range(ncores)))
    return res.results


def kernel(**inputs):
    results = run(inputs)
    out = np.concatenate([r["y"] for r in results], axis=0)
    return out.astype(np.float32)
```

```python
import contextlib
import math
import os
import numpy as np
import ml_dtypes
import concourse.bass as bass
import concourse.mybir as mybir
from concourse.bass_utils import run_bass_kernel_spmd

F32 = mybir.dt.float32
BF16 = mybir.dt.bfloat16
AF = mybir.ActivationFunctionType
ALU = mybir.AluOpType
AX = mybir.AxisListType

NCORES = 8
SPC = 2
L = 2048
D = 2048
KC = D // 128
MEM = 256
EPS = 1e-6
FFN = 8192
EVEN_IN = 6144
ODD_IN = 6656


class Buf:
    __slots__ = ("name", "w", "r")

    def __init__(self, name):
        self.name = name
        self.w = {}
        self.r = {}


class Eng:
    def __init__(self, name, h, sem):
        self.name, self.h, self.sem = name, h, sem
        self.n = 0
        self.waited = {}


class KB:
    def __init__(self, nc, es):
        self.nc, self.es = nc, es
        self.sems = {}
        self.eng = {}
        for name, h in (("pe", nc.tensor), ("act", nc.scalar), ("dve", nc.vector),
                        ("pool", nc.gpsimd), ("sp", nc.sync)):
            sem = es.enter_context(nc.semaphore("s_" + name))
            self.eng[name] = Eng(name, h, sem)
            self.sems[name] = sem
        self.dq = {}
        for q, nslots in (("sp", 8), ("pool", 8), ("act", 4)):
            slots = []
            for i in range(nslots):
                key = "d_%s%d" % (q, i)
                self.sems[key] = es.enter_context(nc.semaphore(key))
                slots.append([key, 0])
            self.dq[q] = [slots, 0]
        self.nbuf = 0

    def buf(self, name="b"):
        self.nbuf += 1
        return Buf("%s%d" % (name, self.nbuf))

    def _wait(self, E, key, val):
        if E.waited.get(key, 0) >= val:
            return
        if key == "pe" and E.name == "pe":
            return
        E.h.wait_ge(self.sems[key], val)
        E.waited[key] = val

    def _deps(self, E, r, w):
        need = {}
        for b in r:
            for k_, v in b.w.items():
                if need.get(k_, 0) < v:
                    need[k_] = v
        for b in w:
            for k_, v in b.w.items():
                if need.get(k_, 0) < v:
                    need[k_] = v
            for k_, v in b.r.items():
                if need.get(k_, 0) < v:
                    need[k_] = v
        for k_, v in need.items():
            self._wait(E, k_, v)

    def _reg(self, key, val, r, w):
        for b in r:
            if b.r.get(key, 0) < val:
                b.r[key] = val
        for b in w:
            b.w = {key: val}
            b.r = {}

    def op(self, eng, fn, r=(), w=(), inc=True):
        E = self.eng[eng]
        self._deps(E, r, w)
        ins = fn(E.h)
        val = E.n + 1
        if inc:
            ins.then_inc(E.sem, 1)
            E.n = val
        self._reg(eng, val, r, w)
        return ins

    def dma(self, out, in_, r=(), w=(), q="sp", **kw):
        E = self.eng[q]
        slots, n = self.dq[q]
        slot = slots[n % len(slots)]
        self.dq[q][1] = n + 1
        self._deps(E, r, w)
        if slot[1] > 0:
            self._wait(E, slot[0], slot[1])
        ins = E.h.dma_start(out=out, in_=in_, **kw)
        slot[1] += 16
        ins.then_inc(self.sems[slot[0]], 16)
        self._reg(slot[0], slot[1], r, w)

    def barrier(self, engines=("pe", "act", "dve", "pool", "sp")):
        cur = {}
        for name, E in self.eng.items():
            if E.n > 0:
                cur[name] = E.n
        for q, (slots, n) in self.dq.items():
            for key, v in slots:
                if v > 0:
                    cur[key] = v
        for name in engines:
            E = self.eng[name]
            for key, v in cur.items():
                if key == name:
                    continue
                self._wait(E, key, v)

    def final_wait(self):
        self.barrier(engines=("sp",))


class Stage:
    def __init__(self, kb):
        self.kb = kb
        self.es = contextlib.ExitStack()
        self.cnt = 0

    def __enter__(self):
        self.es.__enter__()
        return self

    def __exit__(self, *a):
        self.kb.barrier()
        return self.es.__exit__(*a)

    def sb(self, name, shape, dt):
        self.kb.nbuf += 1
        t = self.es.enter_context(self.kb.nc.sbuf_tensor("%s_%d" % (name, self.kb.nbuf), list(shape), dt))
        return t, self.kb.buf(name)

    def rot(self, name, shape, dt, n):
        return Rot([self.sb(name + str(i), shape, dt) for i in range(n)])


class Rot:
    def __init__(self, items):
        self.items = items
        self.i = 0

    def next(self):
        it = self.items[self.i % len(self.items)]
        self.i += 1
        return it


TWO_PI = 2.0 * math.pi
LAM_INIT0 = 0.8 - 0.6 * math.exp(-0.3 * 0)
ALL_STAGES = ("pro", "filt", "mix0", "ca0", "mlp0", "mix1", "ca1", "mlp1", "epi")


class Ctx:
    pass


def build(stages=ALL_STAGES, debug=()):
    C = Ctx()
    nc = bass.Bass("TRN2", target_bir_lowering=False)
    C.nc = nc
    C.din = {}
    debug = set(debug)

    def inp(name, shape, dt=F32):
        t = nc.dram_tensor(name, list(shape), dt, kind="ExternalInput").ap()
        C.din[name] = t
        return t

    def scratch(name, shape, dt):
        kind = "ExternalOutput" if name in debug else "Internal"
        return nc.dram_tensor(name, list(shape), dt, kind=kind).ap()

    x = inp("x", [SPC, L, D])
    mem = inp("mem", [SPC, MEM, D])
    y = nc.dram_tensor("y", [SPC, L, D], F32, kind="ExternalOutput").ap()
    gains_d = inp("gains", [128, 4, 2, KC])
    ev_w_in_t = inp("ev_w_in_t", [48, 128, KC, 128])
    ev_w_out_t = inp("ev_w_out_t", [16, 128, KC, 128])
    od_w_in_t = inp("od_w_in_t", [52, 128, KC, 128])
    od_w_out_t = inp("od_w_out_t", [16, 128, KC, 128])
    ca_wq_t = inp("ca_wq_t", [2, 4, 128, KC, 128])
    ca_wk_t = inp("ca_wk_t", [2, 4, 128, KC, 128])
    ca_wv_t = inp("ca_wv_t", [2, 128, KC, 512])
    ca_wo_t = inp("ca_wo_t", [2, 16, 128, 4, 128])
    mlp_w1_t = inp("mlp_w1_t", [2, 64, 128, KC, 128])
    mlp_w2_t = inp("mlp_w2_t", [2, 16, 128, 64, 128])
    hy_cw_d = inp("hy_cw", [128, 3, 24])
    hy_cb_d = inp("hy_cb", [128, 24])
    hy_bias_d = inp("hy_bias", [128, 8])
    hy_fw1_d = inp("hy_fw1", [33, 64])
    hy_fw2_d = inp("hy_fw2", [64, 64])
    hy_fw3_d = inp("hy_fw3", [64, 64])
    hy_fw4_d = inp("hy_fw4", [64, 2048])
    hy_fcols_d = inp("hy_fcols", [64, 4])
    df_cols_d = inp("df_cols", [128, 3])
    df_lam_d = inp("df_lam", [1, 4, 64])
    hg_lb_d = inp("hg_lb", [128, 2, 2, 8])
    od_cols_d = inp("od_cols", [128, 3])
    ca_cols_d = inp("ca_cols", [128, 2, 2])
    c_identf = inp("c_identf", [128, 128])
    c_bd64 = inp("c_bd64", [128, 128])
    c_alibi = inp("c_alibi", [128, 3968])
    c_zT = inp("c_zT", [33, L])
    c_win = inp("c_win", [16, 128, 1024])
    c_m0 = inp("c_m0", [128, 2])
    c_FT = inp("c_FT", [32, 128, 16, 128], BF16)
    c_G = inp("c_G", [4, 2, 128, 16, 512], BF16)
    c_cos = inp("c_cos", [128, L])
    c_sin = inp("c_sin", [128, L])
    c_PT = inp("c_PT", [128, 128])
    c_mask = inp("c_mask", [64, 2, 512])

    xT = scratch("xT", [SPC, D, L], F32)
    oT = scratch("oT", [SPC, D, L], BF16)
    hx1 = scratch("hx1", [SPC, 1024, L], F32)
    hvx = scratch("hvx", [SPC, 1024, L], F32)
    dq = scratch("dq", [SPC, 1024, L], BF16)
    dk = scratch("dk", [SPC, 1024, L], BF16)
    dv = scratch("dv", [SPC, L, 1024], BF16)
    gk = scratch("gk", [SPC, 256, L], BF16)
    gv = scratch("gv", [SPC, L, 256], BF16)
    KA = scratch("KA", [16, 128, 1024], F32)
    KBt = scratch("KB", [16, 128, 1024], F32)
    KD0 = scratch("KD0", [128, 1024], F32)
    xTv = [xT[s].rearrange("(c p) t -> p c t", p=128) for s in range(SPC)]
    oTv = [oT[s].rearrange("(c p) t -> p c t", p=128) for s in range(SPC)]

    es = contextlib.ExitStack()
    with es:
        kb = KB(nc, es)
        C.kb = kb
        b_xT = [[[kb.buf("xT") for _ in range(KC)] for _ in range(4)] for _ in range(SPC)]
        b_oT = [[kb.buf("oT") for _ in range(KC)] for _ in range(SPC)]
        b_scr = {n: [kb.buf(n) for _ in range(SPC)] for n in ("hx1", "hvx", "dq", "dk", "dv", "gk", "gv")}
        b_K = kb.buf("K")
        psA = es.enter_context(nc.psum_tensor("psA", [128, 4, 512], F32))
        psB = es.enter_context(nc.psum_tensor("psB", [128, 4, 512], F32))
        pst = [psA, psA, psA, psA, psB, psB, psB, psB]
        b_ps = [kb.buf("ps") for _ in range(8)]

        def bank(i):
            return pst[i][:, i % 4, :]

        def bank_bf(i):
            return pst[i][:, i % 4, :].bitcast(BF16)

        def xall(s, oc):
            return [b_xT[s][tb][oc] for tb in range(4)]

        def xblk(s, tb):
            return list(b_xT[s][tb])

        def evac(i, out, in_, r, w):
            if i % 2 == 0:
                kb.op("act", lambda e: e.copy(out=out, in_=in_), r=r, w=w)
            else:
                kb.op("dve", lambda e: e.tensor_copy(out=out, in_=in_), r=r, w=w)

        def mm(out, lhsT, rhs, start, stop, r, w, inc=None):
            kb.op("pe", lambda e: e.matmul(out, lhsT=lhsT, rhs=rhs, start=start, stop=stop), r=r, w=w,
                  inc=(stop if inc is None else inc))

        def rstd_inplace(ap, b, src, src_b, inv_n):
            kb.op("dve", lambda e: e.tensor_scalar(out=ap, in0=src, scalar1=inv_n, scalar2=EPS, op0=ALU.mult, op1=ALU.add),
                  r=src_b, w=[b])
            kb.op("act", lambda e: e.activation(out=ap, in_=ap, func=AF.Ln), r=[b], w=[b])
            kb.op("act", lambda e: e.activation(out=ap, in_=ap, func=AF.Exp, scale=-0.5), r=[b], w=[b])

        psA4 = psA[:, :, :]
        psB4 = psB[:, :, :]
        bA = b_ps[0:4]
        bB = b_ps[4:8]

        with Stage(kb) as g:
            identf, b_c = g.sb("identf", [128, 128], F32)
            b_c = kb.buf("const")
            kb.dma(identf[:], c_identf[:, :], w=[b_c])
            identb, _ = g.sb("identb", [128, 128], BF16)
            onesb, _ = g.sb("onesb", [128, 128], BF16)
            onesf, _ = g.sb("onesf", [128, 128], F32)
            bd64f, _ = g.sb("bd64f", [128, 128], F32)
            bd64b, _ = g.sb("bd64b", [128, 128], BF16)
            gains, _ = g.sb("gains", [128, 4, 2, KC], F32)
            kb.dma(bd64f[:], c_bd64[:, :], w=[b_c])
            kb.dma(gains[:], gains_d[:, :, :, :], w=[b_c])
            kb.op("dve", lambda e: e.tensor_copy(out=identb[:], in_=identf[:]), r=[b_c], w=[b_c])
            kb.op("dve", lambda e: e.tensor_copy(out=bd64b[:], in_=bd64f[:]), r=[b_c], w=[b_c])
            kb.op("dve", lambda e: e.memset(onesb[:], 1.0), w=[b_c])
            kb.op("dve", lambda e: e.memset(onesf[:], 1.0), w=[b_c])
            m0, _ = g.sb("m0", [128, 2], F32)
            kb.dma(m0[:], c_m0[:, :], w=[b_c])
            hycw, _ = g.sb("hycw", [128, 3, 24], F32)
            hycb, _ = g.sb("hycb", [128, 24], F32)
            hyb, _ = g.sb("hyb", [128, 8], F32)
            dfc, _ = g.sb("dfc", [128, 3], F32)
            odc, _ = g.sb("odc", [128, 3], F32)
            cac, _ = g.sb("cac", [128, 2, 2], F32)
            kb.dma(hycw[:], hy_cw_d[:, :, :], w=[b_c])
            kb.dma(hycb[:], hy_cb_d[:, :], w=[b_c])
            kb.dma(hyb[:], hy_bias_d[:, :], w=[b_c])
            kb.dma(dfc[:], df_cols_d[:, :], w=[b_c])
            kb.dma(odc[:], od_cols_d[:, :], w=[b_c])
            kb.dma(cac[:], ca_cols_d[:, :, :], w=[b_c])
            neglam, _ = g.sb("neglam", [128, 1], F32)
            sg2, _ = g.sb("sg2", [128, 1], F32)
            lbs, _ = g.sb("lbs", [128, 2, 8], F32)
            oml, _ = g.sb("oml", [128, 2, 8], F32)
            noml, _ = g.sb("noml", [128, 2, 8], F32)
            RgT, _ = g.sb("RgT", [128, 2, 128], BF16)

            with Stage(kb) as st:
                lamv, b_l = st.sb("lamv", [1, 4, 64], F32)
                kb.dma(lamv[:], df_lam_d[:, :, :], w=[b_l])
                pr, b_pr = st.sb("pr", [1, 2, 64], F32)
                sm, b_sm = st.sb("sm", [1, 2], F32)
                kb.op("dve", lambda e: e.tensor_tensor(out=pr[:, 0, :], in0=lamv[:, 0, :], in1=lamv[:, 1, :], op=ALU.mult), r=[b_l], w=[b_pr])
                kb.op("dve", lambda e: e.tensor_tensor(out=pr[:, 1, :], in0=lamv[:, 2, :], in1=lamv[:, 3, :], op=ALU.mult), r=[b_l], w=[b_pr])
                kb.op("dve", lambda e: e.reduce_sum(out=sm[:], in_=pr[:], axis=AX.X), r=[b_pr], w=[b_sm])
                kb.op("act", lambda e: e.activation(out=sm[:], in_=sm[:], func=AF.Exp), r=[b_sm], w=[b_sm])
                nl, b_nl = st.sb("nl", [1, 1], F32)
                kb.op("dve", lambda e: e.tensor_tensor(out=nl[:], in0=sm[:, 1:2], in1=sm[:, 0:1], op=ALU.subtract), r=[b_sm], w=[b_nl])
                kb.op("dve", lambda e: e.tensor_scalar_add(out=nl[:], in0=nl[:], scalar1=-LAM_INIT0), r=[b_nl], w=[b_nl])
                mm(bank(0)[:, 0:1], onesf[0:1, :], nl[0:1, 0:1], True, True, r=[b_nl, b_c], w=[b_ps[0]])
                kb.op("dve", lambda e: e.tensor_copy(out=neglam[:], in_=bank(0)[:, 0:1]), r=[b_ps[0]], w=[b_c])
                kb.op("dve", lambda e: e.tensor_scalar(out=sg2[:], in0=dfc[:, 2:3], scalar1=(1.0 - LAM_INIT0), scalar2=None, op0=ALU.mult), r=[b_c], w=[b_c])
                lg, b_lg = st.sb("lg", [128, 2, 2, 8], F32)
                kb.dma(lg[:], hg_lb_d[:, :, :, :], w=[b_lg])
                kb.op("dve", lambda e: e.tensor_tensor(out=lbs[:], in0=lg[:, :, 1, :], in1=lg[:, :, 0, :], op=ALU.subtract), r=[b_lg], w=[b_c])
                kb.op("act", lambda e: e.activation(out=lbs[:], in_=lbs[:], func=AF.Sigmoid), r=[b_c], w=[b_c])
                kb.op("dve", lambda e: e.tensor_scalar(out=oml[:], in0=lbs[:], scalar1=-1.0, scalar2=1.0, op0=ALU.mult, op1=ALU.add), r=[b_c], w=[b_c])
                kb.op("dve", lambda e: e.tensor_scalar(out=noml[:], in0=oml[:], scalar1=-1.0, scalar2=None, op0=ALU.mult), r=[b_c], w=[b_c])
                ptf, b_pt = st.sb("ptf", [128, 128], F32)
                kb.dma(ptf[:], c_PT[:, :], w=[b_pt])
                kb.op("dve", lambda e: e.tensor_scalar(out=RgT[:, 0, :], in0=ptf[:], scalar1=odc[:, 1:2], scalar2=None, op0=ALU.mult), r=[b_pt, b_c], w=[b_c])
                kb.op("dve", lambda e: e.tensor_scalar(out=RgT[:, 1, :], in0=ptf[:], scalar1=odc[:, 2:3], scalar2=None, op0=ALU.mult), r=[b_pt, b_c], w=[b_c])

            def norm_to_hT(s, which, layer, hT, b_hT):
                with Stage(kb) as st:
                    xb = st.rot("nx", [128, KC, 512], F32, 2)
                    sq, b_sq = st.sb("nsq", [128, KC, 512], BF16)
                    rs = st.rot("nrs", [128, 512], F32, 2)
                    for tb in range(4):
                        xt, b_xt = xb.next()
                        kb.dma(xt[:], xTv[s][:, :, tb * 512:(tb + 1) * 512], r=xblk(s, tb), w=[b_xt])
                        kb.op("act", lambda e: e.activation(out=sq[:], in_=xt[:], func=AF.Square), r=[b_xt], w=[b_sq])
                        for c in range(KC):
                            mm(bank(4), onesb[:], sq[:, c, :], c == 0, c == KC - 1, r=[b_sq, b_c], w=[b_ps[4]])
                        r_, b_r = rs.next()
                        rstd_inplace(r_[:], b_r, bank(4), [b_ps[4]], 1.0 / D)
                        for c in range(KC):
                            kb.op("dve", lambda e: e.scalar_tensor_tensor(out=hT[:, c, tb * 512:(tb + 1) * 512], in0=xt[:, c, :],
                                                                         scalar=gains[:, which, layer, c:c + 1], in1=r_[:],
                                                                         op0=ALU.mult, op1=ALU.mult),
                                  r=[b_xt, b_r, b_c], w=[b_hT[c]])

            def lin_fm(wt_rot, w_ap, hT, b_hT, kcn, ntb=4, q="pool", pbase=0):
                wt, b_wt = wt_rot.next()
                kb.dma(wt[:], w_ap, w=[b_wt], q=q)
                for tb in range(ntb):
                    for kc in range(kcn):
                        mm(bank(pbase + tb), wt[:, kc, :], hT[:, kc, tb * 512:(tb + 1) * 512], kc == 0, kc == kcn - 1,
                           r=[b_wt, b_hT[kc]], w=[b_ps[pbase + tb]])

            def residual_add(st_rot, s, oc, pbase=0):
                xr, b_xr = st_rot.next()
                kb.dma(xr[:], xT[s, oc * 128:(oc + 1) * 128, :], r=xall(s, oc), w=[b_xr])
                kb.op("dve", lambda e: e.tensor_tensor(out=xr[:].rearrange("p (b t) -> p b t", b=4), in0=xr[:].rearrange("p (b t) -> p b t", b=4),
                                                       in1=(psA4 if pbase == 0 else psB4), op=ALU.add), r=[b_xr] + (bA if pbase == 0 else bB), w=[b_xr])
                kb.dma(xT[s, oc * 128:(oc + 1) * 128, :], xr[:], r=[b_xr], w=xall(s, oc))

            def rmsnorm_fm(st, src_f, b_src, ones_l, inv_n, gcol, out_bf, b_out, extra=None):
                sqb, b_sqb = st["sqb"].next()
                rs, b_rs = st["rs"].next()
                kb.op("act", lambda e: e.activation(out=sqb[:], in_=src_f, func=AF.Square), r=[b_src], w=[b_sqb])
                for tb in range(4):
                    mm(bank(4 + tb), ones_l, sqb[:, tb * 512:(tb + 1) * 512], True, True, r=[b_sqb, b_c], w=[b_ps[4 + tb]])
                rstd_inplace(rs[:].rearrange("p (b t) -> p b t", b=4), b_rs, psB4, bB, inv_n)
                return rs, b_rs

            if "pro" in stages:
                with Stage(kb) as st:
                    xin = st.rot("xin", [128, D], F32, 2)
                    stg = st.rot("stg", [128, KC, 512], F32, 2)
                    ev = 0
                    for s in range(SPC):
                        for tb in range(4):
                            sg, b_sg = stg.next()
                            for tt in range(4):
                                xi, b_xi = xin.next()
                                t0 = tb * 512 + tt * 128
                                kb.dma(xi[:], x[s, t0:t0 + 128, :], w=[b_xi])
                                for j in range(4):
                                    for c4 in range(4):
                                        c = j * 4 + c4
                                        kb.op("pe", lambda e: e.transpose(out=bank(j)[:, c4 * 128:(c4 + 1) * 128],
                                                                          in_=xi[:, c * 128:(c + 1) * 128], identity=identf[:]),
                                              r=[b_xi, b_c], w=[b_ps[j]], inc=(c4 == 3))
                                    src = bank(j).rearrange("p (c t) -> p c t", c=4)
                                    dst = sg[:, j * 4:(j + 1) * 4, tt * 128:(tt + 1) * 128]
                                    evac(ev, dst, src, [b_ps[j]], [b_sg])
                                    ev += 1
                            kb.dma(xTv[s][:, :, tb * 512:(tb + 1) * 512], sg[:], r=[b_sg], w=xblk(s, tb))

            def sin_layer(st, dst, src_ps, b_src, col, n):
                u, b_u = st["u"].next()
                qf, b_qf = st["qf"].next()
                qi, b_qi = st["qi"].next()
                kb.op("dve", lambda e: e.tensor_scalar(out=u[:], in0=src_ps, scalar1=st["fcols"][:, 3:4], scalar2=st["ffb"][:, col:col + 1],
                                                       op0=ALU.mult, op1=ALU.add), r=b_src + [st["b_f"]], w=[b_u])
                kb.op("dve", lambda e: e.tensor_scalar(out=u[:], in0=u[:], scalar1=17.0 * math.pi, scalar2=None, op0=ALU.add), r=[b_u], w=[b_u])
                kb.op("dve", lambda e: e.tensor_scalar(out=qf[:], in0=u[:], scalar1=1.0 / TWO_PI, scalar2=None, op0=ALU.mult), r=[b_u], w=[b_qf])
                kb.op("dve", lambda e: e.tensor_copy(out=qi[:], in_=qf[:]), r=[b_qf], w=[b_qi])
                kb.op("dve", lambda e: e.tensor_copy(out=qf[:], in_=qi[:]), r=[b_qi], w=[b_qf])
                kb.op("dve", lambda e: e.scalar_tensor_tensor(out=u[:], in0=qf[:], scalar=-TWO_PI, in1=u[:], op0=ALU.mult, op1=ALU.add), r=[b_qf, b_u], w=[b_u])
                kb.op("dve", lambda e: e.tensor_scalar(out=qf[:], in0=u[:], scalar1=0.0, scalar2=TWO_PI, op0=ALU.is_lt, op1=ALU.mult), r=[b_u], w=[b_qf])
                kb.op("dve", lambda e: e.tensor_tensor(out=u[:], in0=u[:], in1=qf[:], op=ALU.add), r=[b_u, b_qf], w=[b_u])
                kb.op("act", lambda e: e.activation(out=dst, in_=u[:], func=AF.Sin, bias=st["mpi"][:], scale=1.0), r=[b_u, st["b_f"]], w=[st["b_h"]])

            if "filt" in stages:
                with Stage(kb) as st0:
                    zT, b_f = st0.sb("zT", [33, L], F32)
                    w1, _ = st0.sb("w1", [33, 64], F32)
                    w2, _ = st0.sb("w2", [64, 64], F32)
                    w3, _ = st0.sb("w3", [64, 64], F32)
                    w4, _ = st0.sb("w4", [64, 2048], F32)
                    fcols, _ = st0.sb("fcols", [64, 4], F32)
                    ffb, _ = st0.sb("ffb", [64, 3], F32)
                    mpi, _ = st0.sb("mpi", [64, 1], F32)
                    for t_, d_ in ((zT, c_zT), (w1, hy_fw1_d), (w2, hy_fw2_d), (w3, hy_fw3_d), (w4, hy_fw4_d), (fcols, hy_fcols_d)):
                        kb.dma(t_[:], d_[:, :], w=[b_f])
                    kb.op("dve", lambda e: e.memset(mpi[:], -math.pi), w=[b_f])
                    kb.op("dve", lambda e: e.tensor_scalar(out=ffb[:], in0=fcols[:, 0:3], scalar1=fcols[:, 3:4], scalar2=None, op0=ALU.mult), r=[b_f], w=[b_f])
                    hA, b_hA = st0.sb("hA", [64, L], F32)
                    hB, b_hB = st0.sb("hB", [64, L], F32)
                    sl = {"u": st0.rot("u", [64, 512], F32, 2), "qf": st0.rot("qf", [64, 512], F32, 2),
                          "qi": st0.rot("qi", [64, 512], mybir.dt.int32, 2), "fcols": fcols, "ffb": ffb, "mpi": mpi, "b_f": b_f}
                    for li, (wl, kdim, src, b_s, dst, b_d) in enumerate(((w1, 33, zT, b_f, hA, b_hA), (w2, 64, hA, b_hA, hB, b_hB), (w3, 64, hB, b_hB, hA, b_hA))):
                        sl["b_h"] = b_d
                        for tb in range(4):
                            mm(bank(tb)[0:64, :], wl[0:kdim, :], src[0:kdim, tb * 512:(tb + 1) * 512], True, True, r=[b_f, b_s], w=[b_ps[tb]])
                            sin_layer(sl, dst[:, tb * 512:(tb + 1) * 512], bank(tb)[0:64, :], [b_ps[tb]], li, 512)
                    h3 = hA
                    b_h3 = b_hA
                    for cb in range(2):
                        with Stage(kb) as st:
                            kp, b_kp = st.sb("kp", [128, 16, 512], BF16)
                            km, b_km = st.sb("km", [128, 16, 512], BF16)
                            wn = st.rot("wn", [128, 512], F32, 2)
                            ta = st.rot("ta", [128, 512], F32, 2)
                            tbb = st.rot("tbb", [128, 512], F32, 2)
                            for tc in range(16):
                                mm(bank(0), h3[:, tc * 128:(tc + 1) * 128], w4[:, cb * 512:(cb + 1) * 512], True, True, r=[b_h3, b_f], w=[b_ps[0]])
                                mm(bank(1), h3[:, tc * 128:(tc + 1) * 128], w4[:, 1024 + cb * 512:1024 + (cb + 1) * 512], True, True, r=[b_h3, b_f], w=[b_ps[1]])
                                wt_, b_w = wn.next()
                                kb.dma(wt_[:], c_win[tc, :, cb * 512:(cb + 1) * 512], w=[b_w])
                                a_, b_a = ta.next()
                                b__, b_b = tbb.next()
                                kb.op("dve", lambda e: e.tensor_tensor(out=a_[:], in0=bank(0), in1=wt_[:], op=ALU.mult), r=[b_ps[0], b_w], w=[b_a])
                                kb.op("dve", lambda e: e.tensor_tensor(out=b__[:], in0=bank(1), in1=wt_[:], op=ALU.mult), r=[b_ps[1], b_w], w=[b_b])
                                if tc == 0:
                                    kb.op("dve", lambda e: e.tensor_scalar(out=b__[:], in0=b__[:], scalar1=m0[:, 0:1], scalar2=None, op0=ALU.mult), r=[b_b, b_c], w=[b_b])
                                kb.op("pool", lambda e: e.tensor_tensor(out=kp[:, tc, :], in0=a_[:], in1=b__[:], op=ALU.add), r=[b_a, b_b], w=[b_kp])
                                kb.op("pool", lambda e: e.tensor_tensor(out=km[:, tc, :], in0=a_[:], in1=b__[:], op=ALU.subtract), r=[b_a, b_b], w=[b_km])
                            ftr = st.rot("ft", [128, 16, 128], BF16, 3)
                            ko = st.rot("ko", [128, 512], F32, 3)
                            for j in range(16):
                                ftc, b_fc = ftr.next()
                                kb.dma(ftc[:], c_FT[j], w=[b_fc])
                                fts, b_fs = ftr.next()
                                kb.dma(fts[:], c_FT[16 + j], w=[b_fs])
                                for tc in range(16):
                                    mm(bank(2), ftc[:, tc, :], kp[:, tc, :], tc == 0, tc == 15, r=[b_fc, b_kp], w=[b_ps[2]])
                                for tc in range(16):
                                    mm(bank(3), fts[:, tc, :], km[:, tc, :], tc == 0, tc == 15, r=[b_fs, b_km], w=[b_ps[3]])
                                ka_, b_ka = ko.next()
                                kb.op("act", lambda e: e.copy(out=ka_[:], in_=bank(2)), r=[b_ps[2]], w=[b_ka])
                                kb.dma(KA[j, :, cb * 512:(cb + 1) * 512], ka_[:], r=[b_ka], w=[b_K])
                                kb_, b_kb = ko.next()
                                if j == 0:
                                    kb.op("dve", lambda e: e.tensor_scalar(out=kb_[:], in0=bank(3), scalar1=m0[:, 0:1], scalar2=None, op0=ALU.mult), r=[b_ps[3], b_c], w=[b_kb])
                                else:
                                    kb.op("dve", lambda e: e.tensor_copy(out=kb_[:], in_=bank(3)), r=[b_ps[3]], w=[b_kb])
                                kb.dma(KBt[j, :, cb * 512:(cb + 1) * 512], kb_[:], r=[b_kb], w=[b_K])
                                if j == 0:
                                    for tc in range(16):
                                        mm(bank(4), fts[:, tc, :], kp[:, tc, :], tc == 0, tc == 15, r=[b_fs, b_kp], w=[b_ps[4]])
                                    kd_, b_kd = ko.next()
                                    kb.op("dve", lambda e: e.tensor_scalar(out=kd_[:], in0=bank(4), scalar1=m0[:, 1:2], scalar2=None, op0=ALU.mult), r=[b_ps[4], b_c], w=[b_kd])
                                    kb.op("dve", lambda e: e.scalar_tensor_tensor(out=kd_[:], in0=ka_[:], scalar=m0[:, 0:1], in1=kd_[:], op0=ALU.mult, op1=ALU.add),
                                          r=[b_ka, b_kd, b_c], w=[b_kd])
                                    kb.dma(KD0[:, cb * 512:(cb + 1) * 512], kd_[:], r=[b_kd], w=[b_K])

            def out_proj(s, w_t):
                with Stage(kb) as st:
                    ot = st.sb("oTs", [128, KC, L], BF16)[0]
                    b_ot = [kb.buf("oTs") for _ in range(KC)]
                    for c in range(KC):
                        kb.dma(ot[:, c, :], oT[s, c * 128:(c + 1) * 128, :], r=[b_oT[s][c]], w=[b_ot[c]])
                    wr = st.rot("wo", [128, KC, 128], BF16, 2)
                    xr = st.rot("xr", [128, L], F32, 2)
                    for oc in range(16):
                        lin_fm(wr, w_t[oc], ot, b_ot, KC, pbase=4 * (oc % 2))
                        residual_add(xr, s, oc, pbase=4 * (oc % 2))

            def mlp(layer, s):
                with Stage(kb) as sh:
                    hT = sh.sb("hT", [128, KC, L], BF16)[0]
                    b_hT = [kb.buf("hT") for _ in range(KC)]
                    norm_to_hT(s, 3, layer, hT, b_hT)
                    with Stage(kb) as st:
                        hid = st.sb("hid", [128, 32, 1024], BF16)[0]
                        b_hid = [kb.buf("hid") for _ in range(32)]
                        w1r = st.rot("w1t", [128, KC, 128], BF16, 4)
                        w2r = st.rot("w2t", [128, 32, 128], BF16, 3)
                        tmp = st.rot("mt", [128, 512], F32, 3)
                        xr = st.rot("mxr", [128, 512], F32, 4)
                        nb1 = 0
                        nb2 = 0
                        for tt in range(2):
                            for half in range(2):
                                for hcl in range(32):
                                    hc = half * 32 + hcl
                                    w1_, b_w1 = w1r.next()
                                    kb.dma(w1_[:], mlp_w1_t[layer, hc], w=[b_w1], q="pool")
                                    pbs = [(nb1 + t2) % 4 for t2 in range(2)]
                                    nb1 += 2
                                    for kc in range(KC):
                                        for t2 in range(2):
                                            tok = tt * 1024 + t2 * 512
                                            mm(bank(pbs[t2]), w1_[:, kc, :], hT[:, kc, tok:tok + 512], kc == 0, kc == KC - 1, r=[b_w1, b_hT[kc]], w=[b_ps[pbs[t2]]])
                                    for t2 in range(2):
                                        t_, b_t = tmp.next()
                                        kb.op("dve", lambda e: e.tensor_scalar_max(out=t_[:], in0=bank(pbs[t2]), scalar1=0.0), r=[b_ps[pbs[t2]]], w=[b_t])
                                        kb.op("act", lambda e: e.activation(out=hid[:, hcl, t2 * 512:(t2 + 1) * 512], in_=t_[:], func=AF.Square), r=[b_t], w=[b_hid[hcl]])
                                for oc in range(16):
                                    w2_, b_w2 = w2r.next()
                                    for q2 in range(2):
                                        kb.dma(w2_[:, q2 * 16:(q2 + 1) * 16, :], mlp_w2_t[layer, oc, :, half * 32 + q2 * 16:half * 32 + (q2 + 1) * 16, :], w=[b_w2], q="pool")
                                    pbs = [4 + (nb2 + t2) % 4 for t2 in range(2)]
                                    nb2 += 2
                                    for k_ in range(32):
                                        for t2 in range(2):
                                            mm(bank(pbs[t2]), w2_[:, k_, :], hid[:, k_, t2 * 512:(t2 + 1) * 512], k_ == 0, k_ == 31, r=[b_w2, b_hid[k_]], w=[b_ps[pbs[t2]]])
                                    for t2 in range(2):
                                        tb = tt * 2 + t2
                                        x_, b_x = xr.next()
                                        kb.dma(x_[:], xT[s, oc * 128:(oc + 1) * 128, tb * 512:(tb + 1) * 512], r=[b_xT[s][tb][oc]], w=[b_x])
                                        kb.op("dve", lambda e: e.tensor_tensor(out=x_[:], in0=x_[:], in1=bank(pbs[t2]), op=ALU.add), r=[b_x, b_ps[pbs[t2]]], w=[b_x])
                                        kb.dma(xT[s, oc * 128:(oc + 1) * 128, tb * 512:(tb + 1) * 512], x_[:], r=[b_x], w=[b_xT[s][tb][oc]])

            def cross_attn(layer, s):
                with Stage(kb) as so:
                    kTm, b_kTm = so.sb("kTm", [128, 4, MEM], BF16)
                    vm, b_vm = so.sb("vm", [128, 2, 512], BF16)
                    ocaT = so.sb("ocaT", [128, 4, L], BF16)[0]
                    b_oca = [kb.buf("oca") for _ in range(4)]
                    with Stage(kb) as st:
                        memT = st.sb("memT", [128, KC, MEM], BF16)[0]
                        b_memT = [kb.buf("memT") for _ in range(KC)]
                        mrow = st.rot("mrow", [128, D], F32, 2)
                        msq, b_msq = st.sb("msq", [128, D], F32)
                        mb = st.rot("mb", [128, D], BF16, 2)
                        col = st.rot("mcol", [128, 1], F32, 2)
                        for mt in range(2):
                            mr, b_mr = mrow.next()
                            kb.dma(mr[:], mem[s, mt * 128:(mt + 1) * 128, :], w=[b_mr])
                            kb.op("dve", lambda e: e.tensor_tensor(out=msq[:], in0=mr[:], in1=mr[:], op=ALU.mult), r=[b_mr], w=[b_msq])
                            cl, b_cl = col.next()
                            kb.op("dve", lambda e: e.reduce_sum(out=cl[:], in_=msq[:], axis=AX.X), r=[b_msq], w=[b_cl])
                            rstd_inplace(cl[:], b_cl, cl[:], [b_cl], 1.0 / D)
                            mb_, b_mb = mb.next()
                            kb.op("dve", lambda e: e.tensor_scalar(out=mb_[:], in0=mr[:], scalar1=cl[:, 0:1], scalar2=None, op0=ALU.mult), r=[b_mr, b_cl], w=[b_mb])
                            for hf in range(2):
                                pb = 4 + hf
                                for j in range(8):
                                    c = hf * 8 + j
                                    kb.op("pe", lambda e: e.transpose(out=bank_bf(pb)[:, j * 128:(j + 1) * 128], in_=mb_[:, c * 128:(c + 1) * 128], identity=identb[:]),
                                          r=[b_mb, b_c], w=[b_ps[pb]], inc=(j == 7))
                                for j in range(8):
                                    c = hf * 8 + j
                                    kb.op("dve", lambda e: e.tensor_scalar(out=memT[:, c, mt * 128:(mt + 1) * 128], in0=bank_bf(pb)[:, j * 128:(j + 1) * 128],
                                                                           scalar1=gains[:, 2, layer, c:c + 1], scalar2=None, op0=ALU.mult), r=[b_ps[pb], b_c], w=[b_memT[c]])
                        wr = st.rot("cwk", [128, KC, 128], BF16, 2)
                        kf, b_kf = st.sb("kf", [128, MEM], F32)
                        ksq, b_ksq = st.sb("ksq", [128, MEM], BF16)
                        krs, b_krs = st.sb("krs", [128, MEM], F32)
                        for h in range(4):
                            wt, b_wt = wr.next()
                            kb.dma(wt[:], ca_wk_t[layer, h], w=[b_wt], q="pool")
                            for kc in range(KC):
                                mm(bank(0)[:, 0:MEM], wt[:, kc, :], memT[:, kc, :], kc == 0, kc == KC - 1, r=[b_wt, b_memT[kc]], w=[b_ps[0]])
                            kb.op("act", lambda e: e.copy(out=kf[:], in_=bank(0)[:, 0:MEM]), r=[b_ps[0]], w=[b_kf])
                            kb.op("act", lambda e: e.activation(out=ksq[:], in_=bank(0)[:, 0:MEM], func=AF.Square), r=[b_ps[0]], w=[b_ksq])
                            mm(bank(1)[:, 0:MEM], onesb[:], ksq[:], True, True, r=[b_ksq, b_c], w=[b_ps[1]])
                            rstd_inplace(krs[:], b_krs, bank(1)[:, 0:MEM], [b_ps[1]], 1.0 / 128)
                            kb.op("dve", lambda e: e.scalar_tensor_tensor(out=kTm[:, h, :], in0=kf[:], scalar=cac[:, 1, layer:layer + 1], in1=krs[:], op0=ALU.mult, op1=ALU.mult),
                                  r=[b_kf, b_krs, b_c], w=[b_kTm])
                        wv, b_wv = st.sb("cwv", [128, KC, 512], BF16)
                        kb.dma(wv[:], ca_wv_t[layer], w=[b_wv], q="pool")
                        for mt in range(2):
                            for kc in range(KC):
                                mm(bank(2 + mt), memT[:, kc, mt * 128:(mt + 1) * 128], wv[:, kc, :], kc == 0, kc == KC - 1, r=[b_wv, b_memT[kc]], w=[b_ps[2 + mt]])
                            evac(mt, vm[:, mt, :], bank(2 + mt), [b_ps[2 + mt]], [b_vm])
                    scale = 128 ** -0.5
                    with Stage(kb) as sh:
                        hT = sh.sb("hT", [128, KC, L], BF16)[0]
                        b_hT = [kb.buf("hT") for _ in range(KC)]
                        norm_to_hT(s, 1, layer, hT, b_hT)
                        with Stage(kb) as st:
                            wr = st.rot("cwq", [128, KC, 128], BF16, 2)
                            qf = st.rot("cqf", [128, L], F32, 1)
                            pool = {"sqb": st.rot("csqb", [128, L], BF16, 1), "rs": st.rot("crs", [128, L], F32, 1)}
                            qn = st.rot("cqn", [128, L], BF16, 2)
                            pt = st.rot("cpt", [128, 512], BF16, 3)
                            rc = st.rot("crc", [128, 512], F32, 2)
                            for h in range(4):
                                lin_fm(wr, ca_wq_t[layer, h], hT, b_hT, KC)
                                qf_, b_qf = qf.next()
                                kb.op("act", lambda e: e.copy(out=qf_[:].rearrange("p (b t) -> p b t", b=4), in_=psA4), r=bA, w=[b_qf])
                                rs, b_rs = rmsnorm_fm(pool, qf_[:], b_qf, onesb[:], 1.0 / 128, None, None, None)
                                qn_, b_qn = qn.next()
                                kb.op("dve", lambda e: e.scalar_tensor_tensor(out=qn_[:], in0=qf_[:], scalar=cac[:, 0, layer:layer + 1], in1=rs[:], op0=ALU.mult, op1=ALU.mult),
                                      r=[b_qf, b_rs, b_c], w=[b_qn])
                                for qb in range(4):
                                    for mc in range(2):
                                        sbk = 4 + mc
                                        mm(bank(sbk), kTm[:, h, mc * 128:(mc + 1) * 128], qn_[:, qb * 512:(qb + 1) * 512], True, True, r=[b_kTm, b_qn], w=[b_ps[sbk]])
                                        p_, b_p = pt.next()
                                        kb.op("act", lambda e: e.activation(out=p_[:], in_=bank(sbk), func=AF.Exp, scale=scale), r=[b_ps[sbk]], w=[b_p])
                                        mm(bank(6), vm[:, mc, h * 128:(h + 1) * 128], p_[:], mc == 0, mc == 1, r=[b_vm, b_p], w=[b_ps[6]])
                                        mm(bank(7), onesb[:], p_[:], mc == 0, mc == 1, r=[b_c, b_p], w=[b_ps[7]])
                                    rc_, b_rc = rc.next()
                                    kb.op("act", lambda e: e.activation(out=rc_[:], in_=bank(7), func=AF.Ln), r=[b_ps[7]], w=[b_rc])
                                    kb.op("act", lambda e: e.activation(out=rc_[:], in_=rc_[:], func=AF.Exp, scale=-1.0), r=[b_rc], w=[b_rc])
                                    kb.op("dve", lambda e: e.tensor_tensor(out=ocaT[:, h, qb * 512:(qb + 1) * 512], in0=bank(6), in1=rc_[:], op=ALU.mult),
                                          r=[b_ps[6], b_rc], w=[b_oca[h]])
                    with Stage(kb) as st:
                        wr = st.rot("cwo", [128, 4, 128], BF16, 2)
                        xr = st.rot("xr", [128, L], F32, 2)
                        for oc in range(16):
                            lin_fm(wr, ca_wo_t[layer, oc], ocaT, b_oca, 4, pbase=4 * (oc % 2))
                            residual_add(xr, s, oc, pbase=4 * (oc % 2))

            def view4(ap):
                return ap.rearrange("p (b t) -> p b t", b=4)


            def even_mixer(s):
                with Stage(kb) as so:
                    vx2T = [so.sb("vx2T%d" % cb, [128, 16, 512], BF16) for cb in range(2)]
                    with Stage(kb) as sh:
                        hT = sh.sb("hT", [128, KC, L], BF16)[0]
                        b_hT = [kb.buf("hT") for _ in range(KC)]
                        norm_to_hT(s, 0, 0, hT, b_hT)
                        with Stage(kb) as st:
                            wr = st.rot("hw", [128, KC, 128], BF16, 2)
                            upr = st.rot("up", [128, L + 2], F32, 2)
                            for u_, b_u in upr.items:
                                kb.op("pool", lambda e: e.memset(u_[:, 0:1], 0.0), w=[b_u])
                                kb.op("pool", lambda e: e.memset(u_[:, L + 1:L + 2], 0.0), w=[b_u])
                            accr = st.rot("acc", [128, L], F32, 4)
                            vbr = st.rot("vx2b", [128, L], BF16, 2)
                            for cb in range(2):
                                for cc in range(4):
                                    ch8 = cb * 4 + cc
                                    res = {}
                                    for gi in (1, 2, 0):
                                        ti = gi * 8 + ch8
                                        lin_fm(wr, ev_w_in_t[ti], hT, b_hT, KC)
                                        up, b_up = upr.next()
                                        kb.op("act", lambda e: e.copy(out=view4(up[:, 1:L + 1]), in_=psA4), r=bA, w=[b_up])
                                        acc, b_acc = accr.next()
                                        kb.op("dve", lambda e: e.tensor_scalar(out=acc[:], in0=up[:, 1:L + 1], scalar1=hycw[:, 1, ti:ti + 1], scalar2=hycb[:, ti:ti + 1],
                                                                               op0=ALU.mult, op1=ALU.add), r=[b_up, b_c], w=[b_acc])
                                        kb.op("dve", lambda e: e.scalar_tensor_tensor(out=acc[:], in0=up[:, 0:L], scalar=hycw[:, 0, ti:ti + 1], in1=acc[:],
                                                                                       op0=ALU.mult, op1=ALU.add), r=[b_up, b_c, b_acc], w=[b_acc])
                                        kb.op("dve", lambda e: e.scalar_tensor_tensor(out=acc[:], in0=up[:, 2:L + 2], scalar=hycw[:, 2, ti:ti + 1], in1=acc[:],
                                                                                      op0=ALU.mult, op1=ALU.add), r=[b_up, b_c, b_acc], w=[b_acc])
                                        res[gi] = (acc, b_acc)
                                    x1c, b_x1 = res[0]
                                    x2c, b_x2 = res[1]
                                    vc, b_v = res[2]
                                    kb.dma(hx1[s, ch8 * 128:(ch8 + 1) * 128, :], x1c[:], r=[b_x1], w=[b_scr["hx1"][s]])
                                    kb.op("pool", lambda e: e.tensor_tensor(out=vc[:], in0=vc[:], in1=x2c[:], op=ALU.mult), r=[b_v, b_x2], w=[b_v])
                                    kb.dma(hvx[s, ch8 * 128:(ch8 + 1) * 128, :], vc[:], r=[b_v], w=[b_scr["hvx"][s]])
                                    vb_, b_vb = vbr.next()
                                    kb.op("act", lambda e: e.copy(out=vb_[:], in_=vc[:]), r=[b_v], w=[b_vb])
                                    for hf in range(2):
                                        pb = 4 + hf
                                        for j in range(8):
                                            tc = hf * 8 + j
                                            kb.op("pe", lambda e: e.transpose(out=bank_bf(pb)[:, j * 128:(j + 1) * 128], in_=vb_[:, tc * 128:(tc + 1) * 128], identity=identb[:]),
                                                  r=[b_vb, b_c], w=[b_ps[pb]], inc=(j == 7))
                                        evac(hf, vx2T[cb][0][:, hf * 8:(hf + 1) * 8, cc * 128:(cc + 1) * 128],
                                             bank_bf(pb).rearrange("p (j c) -> p j c", j=8), [b_ps[pb]], [vx2T[cb][1]])
                        with Stage(kb) as st:
                            wr = st.rot("dw", [128, KC, 128], BF16, 2)
                            qf = st.rot("dqf", [128, L], F32, 2)
                            pool_ = {"sqb": st.rot("dsqb", [128, L], BF16, 1), "rs": st.rot("drs", [128, L], F32, 1)}
                            qn = st.rot("dqn", [128, L], BF16, 2)
                            for which, dst, nm in ((0, dq, "dq"), (1, dk, "dk")):
                                for h in range(8):
                                    lin_fm(wr, ev_w_in_t[24 + which * 8 + h], hT, b_hT, KC)
                                    qf_, b_qf = qf.next()
                                    kb.op("act", lambda e: e.copy(out=view4(qf_[:]), in_=psA4), r=bA, w=[b_qf])
                                    rs, b_rs = rmsnorm_fm(pool_, qf_[:], b_qf, bd64b[:], 1.0 / 64, None, None, None)
                                    qn_, b_qn = qn.next()
                                    kb.op("dve", lambda e: e.scalar_tensor_tensor(out=qn_[:], in0=qf_[:], scalar=dfc[:, which:which + 1], in1=rs[:], op0=ALU.mult, op1=ALU.mult),
                                          r=[b_qf, b_rs, b_c], w=[b_qn])
                                    kb.dma(dst[s, h * 128:(h + 1) * 128, :], qn_[:], r=[b_qn], w=[b_scr[nm][s]])
                            wv = st.rot("dwv", [128, KC, 512], BF16, 1)
                            vrow = st.rot("dvrow", [128, 512], BF16, 3)
                            for vb in range(2):
                                wv_, b_wv = wv.next()
                                for j4 in range(4):
                                    kb.dma(wv_[:, :, j4 * 128:(j4 + 1) * 128], ev_w_in_t[40 + vb * 4 + j4], w=[b_wv], q="pool")
                                for tt in range(16):
                                    pb = 4 + tt % 4
                                    for kc in range(KC):
                                        mm(bank(pb), hT[:, kc, tt * 128:(tt + 1) * 128], wv_[:, kc, :], kc == 0, kc == KC - 1, r=[b_wv, b_hT[kc]], w=[b_ps[pb]])
                                    vr, b_vr = vrow.next()
                                    evac(tt, vr[:], bank(pb), [b_ps[pb]], [b_vr])
                                    kb.dma(dv[s, tt * 128:(tt + 1) * 128, vb * 512:(vb + 1) * 512], vr[:], r=[b_vr], w=[b_scr["dv"][s]])
                    with Stage(kb) as st:
                        ftr = st.rot("ft", [128, 16, 128], BF16, 4)
                        kt = st.rot("kt", [128, 512], F32, 6)
                        tmpr = st.rot("ht", [128, 512], F32, 4)
                        Yf, b_Yf = st.sb("Yf", [128, 32, 512], BF16)
                        gtr = st.rot("gt", [128, 16, 512], BF16, 2)
                        epr = st.rot("ep", [128, 512], F32, 4)
                        obr = st.rot("ob", [128, 512], BF16, 2)
                        for cb in range(2):
                            vt, b_vt = vx2T[cb]
                            for j in range(16):
                                ftc, b_fc = ftr.next()
                                kb.dma(ftc[:], c_FT[j], w=[b_fc])
                                fts, b_fs = ftr.next()
                                kb.dma(fts[:], c_FT[16 + j], w=[b_fs])
                                pa = (j % 2) * 2
                                pbk = pa + 1
                                for tc in range(16):
                                    mm(bank(pa), ftc[:, tc, :], vt[:, tc, :], tc == 0, tc == 15, r=[b_fc, b_vt], w=[b_ps[pa]])
                                for tc in range(16):
                                    mm(bank(pbk), fts[:, tc, :], vt[:, tc, :], tc == 0, tc == 15, r=[b_fs, b_vt], w=[b_ps[pbk]])
                                ka_, b_ka = kt.next()
                                kb.dma(ka_[:], KA[j, :, cb * 512:(cb + 1) * 512], r=[b_K], w=[b_ka])
                                kb_, b_kb = kt.next()
                                kb.dma(kb_[:], KBt[j, :, cb * 512:(cb + 1) * 512], r=[b_K], w=[b_kb])
                                if j == 0:
                                    kd_, b_kd = kt.next()
                                    kb.dma(kd_[:], KD0[:, cb * 512:(cb + 1) * 512], r=[b_K], w=[b_kd])
                                else:
                                    kd_, b_kd = ka_, b_ka
                                t1, b_t1 = tmpr.next()
                                t2, b_t2 = tmpr.next()
                                kb.op("dve", lambda e: e.tensor_tensor(out=t1[:], in0=bank(pa), in1=ka_[:], op=ALU.mult), r=[b_ps[pa], b_ka], w=[b_t1])
                                kb.op("dve", lambda e: e.tensor_tensor(out=t2[:], in0=bank(pbk), in1=kb_[:], op=ALU.mult), r=[b_ps[pbk], b_kb], w=[b_t2])
                                kb.op("pool", lambda e: e.tensor_tensor(out=Yf[:, j, :], in0=t1[:], in1=t2[:], op=ALU.subtract), r=[b_t1, b_t2], w=[b_Yf])
                                t3, b_t3 = tmpr.next()
                                t4, b_t4 = tmpr.next()
                                kb.op("dve", lambda e: e.tensor_tensor(out=t3[:], in0=bank(pa), in1=kb_[:], op=ALU.mult), r=[b_ps[pa], b_kb], w=[b_t3])
                                kb.op("dve", lambda e: e.tensor_tensor(out=t4[:], in0=bank(pbk), in1=kd_[:], op=ALU.mult), r=[b_ps[pbk], b_kd], w=[b_t4])
                                kb.op("pool", lambda e: e.tensor_tensor(out=Yf[:, 16 + j, :], in0=t3[:], in1=t4[:], op=ALU.add), r=[b_t3, b_t4], w=[b_Yf])
                            for tb in range(4):
                                for hf in range(2):
                                    gt, b_gt = gtr.next()
                                    kb.dma(gt[:], c_G[tb, hf], w=[b_gt])
                                    for cc in range(4):
                                        for fj in range(16):
                                            mm(bank(4 + cc), Yf[:, hf * 16 + fj, cc * 128:(cc + 1) * 128], gt[:, fj, :], hf == 0 and fj == 0, hf == 1 and fj == 15,
                                               r=[b_Yf, b_gt], w=[b_ps[4 + cc]], inc=(fj == 15))
                                for cc in range(4):
                                    ch8 = cb * 4 + cc
                                    vxb, b_vxb = epr.next()
                                    x1b, b_x1b = epr.next()
                                    kb.dma(vxb[:], hvx[s, ch8 * 128:(ch8 + 1) * 128, tb * 512:(tb + 1) * 512], r=[b_scr["hvx"][s]], w=[b_vxb])
                                    kb.dma(x1b[:], hx1[s, ch8 * 128:(ch8 + 1) * 128, tb * 512:(tb + 1) * 512], r=[b_scr["hx1"][s]], w=[b_x1b])
                                    kb.op("dve", lambda e: e.scalar_tensor_tensor(out=vxb[:], in0=vxb[:], scalar=hyb[:, ch8:ch8 + 1], in1=bank(4 + cc), op0=ALU.mult, op1=ALU.add),
                                          r=[b_vxb, b_c, b_ps[4 + cc]], w=[b_vxb])
                                    ob, b_ob = obr.next()
                                    kb.op("pool", lambda e: e.tensor_tensor(out=ob[:], in0=vxb[:], in1=x1b[:], op=ALU.mult), r=[b_vxb, b_x1b], w=[b_ob])
                                    kb.dma(oT[s, ch8 * 128:(ch8 + 1) * 128, tb * 512:(tb + 1) * 512], ob[:], r=[b_ob], w=[b_oT[s][ch8]])
                scale = 64 ** -0.5
                with Stage(kb) as st:
                    base, b_base = st.sb("alibi", [128, 3968], F32)
                    kb.dma(base[:], c_alibi[:, :], w=[b_base])
                    qzr = Rot([(st.sb("aqz0%d" % i, [128, L], BF16), st.sb("aqz1%d" % i, [128, L], BF16)) for i in range(2)])
                    for (z0, b_z0), (z1, b_z1) in qzr.items:
                        kb.op("pool", lambda e: e.memset(z0[64:128, :], 0.0), w=[b_z0])
                        kb.op("pool", lambda e: e.memset(z1[0:64, :], 0.0), w=[b_z1])
                    kr = st.rot("ak", [128, L], BF16, 2)
                    vr = st.rot("av", [128, 16, 128], BF16, 2)
                    tmpr = st.rot("as", [128, 512], BF16, 3)
                    ptr = st.rot("ap", [128, 512], BF16, 3)
                    etr = st.rot("aet", [128, 3968], BF16, 2)
                    rcr = st.rot("arc", [128, 512], F32, 2)
                    rr = st.rot("ar", [128, 512], F32, 4)
                    osr = st.rot("aos", [128, 512], F32, 2)
                    sqr = st.rot("asq", [128, 512], BF16, 2)
                    rsr = st.rot("ars", [128, 512], F32, 2)
                    obr = st.rot("aob", [128, 512], BF16, 2)
                    LA = 2
                    iters = [(h, qb, j, kc) for h in range(8) for qb in range(4) for j in range(2) for kc in range(16)]
                    heads = {}
                    etabs = {}
                    rj = {}
                    pending = []

                    def emit_S(i):
                        h, qb, j, kc = iters[i]
                        if h not in heads:
                            (z0, b_z0), (z1, b_z1) = qzr.next()
                            kT_, b_k = kr.next()
                            v_, b_v = vr.next()
                            kb.dma(z0[0:64, :], dq[s, h * 128:h * 128 + 64, :], r=[b_scr["dq"][s]], w=[b_z0])
                            kb.dma(z1[64:128, :], dq[s, h * 128 + 64:(h + 1) * 128, :], r=[b_scr["dq"][s]], w=[b_z1])
                            qT_, b_q = (z0, z1), (b_z0, b_z1)
                            et_, b_et = etr.next()
                            kb.op("act", lambda e: e.activation(out=et_[:], in_=base[:], func=AF.Exp, scale=-(2.0 ** (-(h + 1)))), r=[b_base], w=[b_et])
                            etabs[h] = (et_, b_et)
                            kb.dma(kT_[:], dk[s, h * 128:(h + 1) * 128, :], r=[b_scr["dk"][s]], w=[b_k])
                            kb.dma(v_[:], dv[s].rearrange("(tc p) c -> p tc c", p=128)[:, :, h * 128:(h + 1) * 128], r=[b_scr["dv"][s]], w=[b_v])
                            heads[h] = (qT_, b_q, kT_, b_k, v_, b_v)
                        qT_, b_q, kT_, b_k, v_, b_v = heads[h]
                        sb_ = i % 3
                        mm(bank(sb_), kT_[:, kc * 128:(kc + 1) * 128], qT_[j][:, qb * 512:(qb + 1) * 512], True, True,
                           r=[b_k, b_q[j]], w=[b_ps[sb_]])

                    def emit_rest(i):
                        h, qb, j, kc = iters[i]
                        slope = 2.0 ** (-(h + 1))
                        qT_, b_q, kT_, b_k, v_, b_v = heads[h]
                        sb_ = i % 3
                        off = qb * 512 - kc * 128 + 1920
                        et_, b_et = etabs[h]
                        t_, b_t = tmpr.next()
                        kb.op("act", lambda e: e.activation(out=t_[:], in_=bank(sb_), func=AF.Exp, scale=scale), r=[b_ps[sb_]], w=[b_t])
                        p_, b_p = ptr.next()
                        kb.op("dve", lambda e: e.tensor_tensor(out=p_[:], in0=t_[:], in1=et_[:, off:off + 512], op=ALU.mult), r=[b_t, b_et], w=[b_p])
                        mm(bank(4 + j), v_[:, kc, :], p_[:], kc == 0, kc == 15, r=[b_v, b_p], w=[b_ps[4 + j]])
                        mm(bank(6 + j), onesb[:], p_[:], kc == 0, kc == 15, r=[b_c, b_p], w=[b_ps[6 + j]])
                        if kc != 15:
                            return
                        rc_, b_rc = rcr.next()
                        kb.op("act", lambda e: e.activation(out=rc_[:], in_=bank(6 + j), func=AF.Ln), r=[b_ps[6 + j]], w=[b_rc])
                        kb.op("act", lambda e: e.activation(out=rc_[:], in_=rc_[:], func=AF.Exp, scale=-1.0), r=[b_rc], w=[b_rc])
                        os_, b_os = osr.next()
                        kb.op("act", lambda e: e.copy(out=os_[:], in_=bank(4 + j)), r=[b_ps[4 + j]], w=[b_os])
                        r_, b_r = rr.next()
                        kb.op("pool", lambda e: e.tensor_tensor(out=r_[:], in0=os_[:], in1=rc_[:], op=ALU.mult), r=[b_os, b_rc], w=[b_r])
                        rj[j] = (r_, b_r)
                        if j != 1:
                            return
                        (r0, b_r0), (r1, b_r1) = rj[0], rj[1]
                        kb.op("dve", lambda e: e.scalar_tensor_tensor(out=r0[:], in0=r1[:], scalar=neglam[:, 0:1], in1=r0[:], op0=ALU.mult, op1=ALU.add),
                              r=[b_r0, b_r1, b_c], w=[b_r0])
                        sq_, b_sq = sqr.next()
                        kb.op("pool", lambda e: e.tensor_tensor(out=sq_[:], in0=r0[:], in1=r0[:], op=ALU.mult), r=[b_r0], w=[b_sq])
                        pending.append((i + 8, lambda: finalize_b(h, qb, r0, b_r0, sq_, b_sq)))

                    def finalize_b(h, qb, r0, b_r0, sq_, b_sq):
                        mm(bank(3), onesb[:], sq_[:], True, True, r=[b_c, b_sq], w=[b_ps[3]])
                        rs_, b_rs = rsr.next()
                        kb.op("dve", lambda e: e.tensor_scalar(out=rs_[:], in0=bank(3), scalar1=1.0 / 128, scalar2=EPS, op0=ALU.mult, op1=ALU.add), r=[b_ps[3]], w=[b_rs])
                        kb.op("act", lambda e: e.activation(out=rs_[:], in_=rs_[:], func=AF.Ln), r=[b_rs], w=[b_rs])
                        kb.op("act", lambda e: e.activation(out=rs_[:], in_=rs_[:], func=AF.Exp, scale=-0.5), r=[b_rs], w=[b_rs])
                        ob, b_ob = obr.next()
                        kb.op("dve", lambda e: e.scalar_tensor_tensor(out=ob[:], in0=r0[:], scalar=sg2[:, 0:1], in1=rs_[:], op0=ALU.mult, op1=ALU.mult),
                              r=[b_r0, b_rs, b_c], w=[b_ob])
                        kb.dma(oT[s, 1024 + h * 128:1024 + (h + 1) * 128, qb * 512:(qb + 1) * 512], ob[:], r=[b_ob], w=[b_oT[s][8 + h]])

                    n_it = len(iters)
                    for i in range(n_it + LA):
                        if i < n_it:
                            emit_S(i)
                        if i - LA >= 0:
                            emit_rest(i - LA)
                            while pending and pending[0][0] <= i - LA:
                                pending.pop(0)[1]()
                    while pending:
                        pending.pop(0)[1]()


            C.even_mixer = even_mixer


            def odd_mixer(s):
                with Stage(kb) as sh:
                    hT = sh.sb("hT", [128, KC, L], BF16)[0]
                    b_hT = [kb.buf("hT") for _ in range(KC)]
                    norm_to_hT(s, 0, 1, hT, b_hT)
                    with Stage(kb) as st:
                        cosT, b_cs = st.sb("cosT", [128, L], F32)
                        sinT, _ = st.sb("sinT", [128, L], F32)
                        kb.dma(cosT[:], c_cos[:, :], w=[b_cs])
                        kb.dma(sinT[:], c_sin[:, :], w=[b_cs])
                        wr = st.rot("gw", [128, KC, 128], BF16, 2)
                        qf = st.rot("gqf", [128, L], F32, 1)
                        qb16 = st.rot("gqb", [128, L], BF16, 1)
                        pool_ = {"sqb": st.rot("gsqb", [128, L], BF16, 1), "rs": st.rot("grs", [128, L], F32, 1)}
                        ta = st.rot("gta", [128, L], F32, 1)
                        tb_ = st.rot("gtb", [128, L], F32, 1)
                        qn = st.rot("gqn", [128, L], BF16, 2)
                        for hh in range(10):
                            isk = hh >= 8
                            ti = 40 + hh
                            lin_fm(wr, od_w_in_t[ti], hT, b_hT, KC)
                            qf_, b_qf = qf.next()
                            kb.op("act", lambda e: e.copy(out=view4(qf_[:]), in_=psA4), r=bA, w=[b_qf])
                            q16, b_q16 = qb16.next()
                            kb.op("act", lambda e: e.copy(out=view4(q16[:]), in_=psA4), r=bA, w=[b_q16])
                            rs, b_rs = rmsnorm_fm(pool_, qf_[:], b_qf, onesb[:], 1.0 / 128, None, None, None)
                            for tb in range(4):
                                mm(bank(4 + tb), RgT[:, 1 if isk else 0, :], q16[:, tb * 512:(tb + 1) * 512], True, True, r=[b_c, b_q16], w=[b_ps[4 + tb]])
                            a_, b_a = ta.next()
                            b__, b_b = tb_.next()
                            gcol = odc[:, 2:3] if isk else odc[:, 1:2]
                            kb.op("dve", lambda e: e.scalar_tensor_tensor(out=a_[:], in0=qf_[:], scalar=gcol, in1=cosT[:], op0=ALU.mult, op1=ALU.mult), r=[b_qf, b_c, b_cs], w=[b_a])
                            kb.op("dve", lambda e: e.tensor_tensor(out=view4(b__[:]), in0=psB4, in1=view4(sinT[:]), op=ALU.mult), r=bB + [b_cs], w=[b_b])
                            kb.op("pool", lambda e: e.tensor_tensor(out=a_[:], in0=a_[:], in1=b__[:], op=ALU.add), r=[b_a, b_b], w=[b_a])
                            qn_, b_qn = qn.next()
                            kb.op("dve", lambda e: e.tensor_tensor(out=qn_[:], in0=a_[:], in1=rs[:], op=ALU.mult), r=[b_a, b_rs], w=[b_qn])
                            if isk:
                                kb.dma(gk[s, (hh - 8) * 128:(hh - 7) * 128, :], qn_[:], r=[b_qn], w=[b_scr["gk"][s]])
                            else:
                                kb.dma(dq[s, hh * 128:(hh + 1) * 128, :], qn_[:], r=[b_qn], w=[b_scr["dq"][s]])
                        wv, b_wv = st.sb("gwv", [128, KC, 256], BF16)
                        for j4 in range(2):
                            kb.dma(wv[:, :, j4 * 128:(j4 + 1) * 128], od_w_in_t[50 + j4], w=[b_wv], q="pool")
                        vrow = st.rot("gvrow", [128, 256], BF16, 3)
                        for tt in range(16):
                            pb = 4 + tt % 4
                            for kc in range(KC):
                                mm(bank(pb)[:, 0:256], hT[:, kc, tt * 128:(tt + 1) * 128], wv[:, kc, :], kc == 0, kc == KC - 1, r=[b_wv, b_hT[kc]], w=[b_ps[pb]])
                            vr, b_vr = vrow.next()
                            evac(tt, vr[:], bank(pb)[:, 0:256], [b_ps[pb]], [b_vr])
                            kb.dma(gv[s, tt * 128:(tt + 1) * 128, :], vr[:], r=[b_vr], w=[b_scr["gv"][s]])
                        wi = st.rot("gwi", [128, KC, 512], BF16, 1)
                        irow = st.rot("girow", [128, 512], BF16, 3)
                        for vb in range(2):
                            wi_, b_wi = wi.next()
                            for j4 in range(4):
                                kb.dma(wi_[:, :, j4 * 128:(j4 + 1) * 128], od_w_in_t[24 + vb * 4 + j4], w=[b_wi], q="pool")
                            for tt in range(16):
                                pb = 4 + tt % 4
                                for kc in range(KC):
                                    mm(bank(pb), hT[:, kc, tt * 128:(tt + 1) * 128], wi_[:, kc, :], kc == 0, kc == KC - 1, r=[b_wi, b_hT[kc]], w=[b_ps[pb]])
                                ir, b_ir = irow.next()
                                evac(tt, ir[:], bank(pb), [b_ps[pb]], [b_ir])
                                kb.dma(dv[s, tt * 128:(tt + 1) * 128, vb * 512:(vb + 1) * 512], ir[:], r=[b_ir], w=[b_scr["dv"][s]])
                    with Stage(kb) as st:
                        msk, b_msk = st.sb("scanmask", [128, L], F32)
                        kb.op("pool", lambda e: e.memset(msk[:], 1.0), w=[b_msk])
                        kb.op("pool", lambda e: e.memset(msk[:].rearrange("p (c j) -> p c j", j=64)[:, :, 0:1], 0.0), w=[b_msk])
                        cm, b_cm = st.sb("cmask", [64, 2, 512], F32)
                        kb.dma(cm[:], c_mask[:, :, :], w=[b_cm])
                        wr = st.rot("hgw", [128, KC, 128], BF16, 2)
                        qs, b_qs = st.sb("hqs", [128, L], F32)
                        vtok, b_vtok = st.sb("hvtok", [64, 32, 128], BF16)
                        fa, b_fa = st.sb("hfa", [128, L], F32)
                        ka, b_kk = st.sb("hka", [128, L], F32)
                        Bc, b_B = st.sb("hB", [128, L], F32)
                        eb, b_eb = st.sb("heb", [128, L], F32)
                        enb, b_enb = st.sb("henb", [128, L], F32)
                        ebl, b_ebl = st.sb("hebl", [128, 32], F32)
                        qe, b_qe = st.sb("hqe", [128, L], BF16)
                        ke, b_ke = st.sb("hke", [128, L], BF16)
                        klT, b_klT = st.sb("hklT", [128, L], BF16)
                        kltok, b_kltok = st.sb("hkltok", [64, 32, 128], BF16)
                        attT, b_att = st.sb("hatt", [64, 32, 64], BF16)
                        sgt, b_sgt = enb, b_enb
                        Sbr = st.rot("hSb", [128, 128], BF16, 6)
                        oacc, b_oacc = st.sb("hoacc", [128, L], F32)
                        oaccb, b_oaccb = st.sb("hoaccb", [128, L], F32)
                        pool_ = {"sqb": st.rot("hsqb", [128, L], BF16, 1), "rs": st.rot("hrs", [128, L], F32, 1)}
                        ob, b_ob = st.sb("hob", [128, L], BF16)
                        c3 = lambda ap: ap.rearrange("p (c j) -> p c j", j=64)
                        for h in range(8):
                            kb.dma(vtok[:], dv[s].rearrange("(ch p) c -> p ch c", p=64)[:, :, h * 128:(h + 1) * 128], r=[b_scr["dv"][s]], w=[b_vtok])
                            lin_fm(wr, od_w_in_t[h], hT, b_hT, KC)
                            kb.op("act", lambda e: e.mul(out=view4(qs[:]), in_=psA4, mul=128 ** -0.5), r=bA, w=[b_qs])
                            for di in range(2):
                                lin_fm(wr, od_w_in_t[8 + di * 8 + h], hT, b_hT, KC)
                                kb.op("act", lambda e: e.activation(out=view4(fa[:]), in_=psA4, func=AF.Sigmoid), r=bA, w=[b_fa])
                                kb.op("dve", lambda e: e.tensor_scalar(out=ka[:], in0=fa[:], scalar1=noml[:, di, h:h + 1], scalar2=oml[:, di, h:h + 1], op0=ALU.mult, op1=ALU.add),
                                      r=[b_fa, b_c], w=[b_kk])
                                kb.op("act", lambda e: e.activation(out=fa[:], in_=fa[:], func=AF.Ln, scale=oml[:, di, h:h + 1], bias=lbs[:, di, h:h + 1]), r=[b_fa, b_c], w=[b_fa])
                                kb.op("dve", lambda e: e.tensor_tensor_scan(out=Bc[:], data0=msk[:], data1=fa[:], initial=0.0, op0=ALU.mult, op1=ALU.add), r=[b_msk, b_fa], w=[b_B])
                                if di == 1:
                                    kb.op("dve", lambda e: e.tensor_tensor(out=c3(eb[:]), in0=c3(Bc[:]), in1=c3(Bc[:])[:, :, 63:64].to_broadcast([128, 32, 64]), op=ALU.subtract),
                                          r=[b_B], w=[b_eb])
                                    kb.op("dve", lambda e: e.tensor_tensor(out=eb[:], in0=eb[:], in1=fa[:], op=ALU.subtract), r=[b_eb, b_fa], w=[b_eb])
                                    kb.op("act", lambda e: e.activation(out=enb[:], in_=eb[:], func=AF.Exp, scale=1.0), r=[b_eb], w=[b_enb])
                                    kb.op("act", lambda e: e.activation(out=eb[:], in_=eb[:], func=AF.Exp, scale=-1.0), r=[b_eb], w=[b_eb])
                                    last = 0
                                else:
                                    kb.op("act", lambda e: e.activation(out=eb[:], in_=Bc[:], func=AF.Exp, scale=1.0), r=[b_B], w=[b_eb])
                                    kb.op("act", lambda e: e.activation(out=enb[:], in_=Bc[:], func=AF.Exp, scale=-1.0), r=[b_B], w=[b_enb])
                                    last = 63
                                kb.op("pool", lambda e: e.tensor_copy(out=ebl[:], in_=c3(eb[:])[:, :, last]), r=[b_eb], w=[b_ebl])
                                kb.op("dve", lambda e: e.tensor_tensor(out=qe[:], in0=qs[:], in1=eb[:], op=ALU.mult), r=[b_qs, b_eb], w=[b_qe])
                                kb.op("dve", lambda e: e.tensor_tensor(out=ke[:], in0=ka[:], in1=enb[:], op=ALU.mult), r=[b_kk, b_enb], w=[b_ke])
                                kb.op("dve", lambda e: e.tensor_tensor(out=c3(klT[:]), in0=c3(ke[:]), in1=ebl[:].unsqueeze(2).to_broadcast([128, 32, 64]), op=ALU.mult),
                                      r=[b_ke, b_ebl], w=[b_klT])
                                for g4 in range(4):
                                    pb = 4 + g4
                                    for j in range(8):
                                        ch = g4 * 8 + j
                                        kb.op("pe", lambda e: e.transpose(out=bank_bf(pb)[0:64, j * 128:(j + 1) * 128], in_=klT[:, ch * 64:(ch + 1) * 64], identity=identb[:]),
                                              r=[b_klT, b_c], w=[b_ps[pb]], inc=(j == 7))
                                    evac(g4, kltok[:, g4 * 8:(g4 + 1) * 8, :], bank_bf(pb)[0:64, :].rearrange("p (j c) -> p j c", j=8), [b_ps[pb]], [b_kltok])
                                for g4 in range(4):
                                    pb = 4 + g4
                                    for j in range(8):
                                        ch = g4 * 8 + j
                                        mm(bank(pb)[0:64, j * 64:(j + 1) * 64], ke[:, ch * 64:(ch + 1) * 64], qe[:, ch * 64:(ch + 1) * 64], True, True,
                                           r=[b_ke, b_qe], w=[b_ps[pb]], inc=(j == 7))
                                    kb.op("dve", lambda e: e.tensor_tensor(out=attT[:, g4 * 8:(g4 + 1) * 8, :], in0=bank(pb)[0:64, :].rearrange("p (j c) -> p j c", j=8),
                                                                           in1=cm[:, di, :].rearrange("p (j c) -> p j c", j=8), op=ALU.mult), r=[b_ps[pb], b_cm], w=[b_att])
                                order = list(range(32)) if di == 0 else list(range(31, -1, -1))
                                Sb, b_Sb = Sbr.next()
                                kb.op("pool", lambda e: e.memset(Sb[:], 0.0), w=[b_Sb])
                                odst = oacc if di == 0 else oaccb
                                b_odst = b_oacc if di == 0 else b_oaccb
                                for gi in range(8):
                                    pb = 2 + gi % 2
                                    for j in range(4):
                                        ch = order[gi * 4 + j]
                                        mm(bank(pb)[:, j * 128:(j + 1) * 128], kltok[:, ch, :], vtok[:, ch, :], True, True, r=[b_kltok, b_vtok], w=[b_ps[pb]], inc=(j == 3))
                                    for j in range(4):
                                        i = gi * 4 + j
                                        ch = order[i]
                                        po = (i // 8) % 2
                                        sl = ch % 8
                                        mm(bank(po)[:, sl * 64:(sl + 1) * 64], Sb[:], qe[:, ch * 64:(ch + 1) * 64], True, False, r=[b_Sb, b_qe], w=[b_ps[po]], inc=False)
                                        mm(bank(po)[:, sl * 64:(sl + 1) * 64], vtok[:, ch, :], attT[:, ch, :], False, True, r=[b_vtok, b_att], w=[b_ps[po]])
                                        Sn, b_Sn = Sbr.next()
                                        kb.op("dve", lambda e: e.scalar_tensor_tensor(out=Sn[:], in0=Sb[:], scalar=ebl[:, ch:ch + 1], in1=bank(pb)[:, j * 128:(j + 1) * 128], op0=ALU.mult, op1=ALU.add),
                                              r=[b_Sb, b_ebl, b_ps[pb]], w=[b_Sn])
                                        Sb, b_Sb = Sn, b_Sn
                                        if i % 8 == 7:
                                            g8 = ch // 8
                                            kb.op("act", lambda e: e.copy(out=odst[:, g8 * 512:(g8 + 1) * 512], in_=bank(po)), r=[b_ps[po]], w=[b_odst])
                            kb.op("pool", lambda e: e.tensor_tensor(out=oacc[:], in0=oacc[:], in1=oaccb[:], op=ALU.add), r=[b_oacc, b_oaccb], w=[b_oacc])
                            lin_fm(wr, od_w_in_t[32 + h], hT, b_hT, KC)
                            kb.op("act", lambda e: e.activation(out=view4(sgt[:]), in_=psA4, func=AF.Silu), r=bA, w=[b_sgt])
                            rs, b_rs = rmsnorm_fm(pool_, oacc[:], b_oacc, onesb[:], 1.0 / 128, None, None, None)
                            kb.op("dve", lambda e: e.scalar_tensor_tensor(out=oacc[:], in0=oacc[:], scalar=odc[:, 0:1], in1=rs[:], op0=ALU.mult, op1=ALU.mult), r=[b_oacc, b_rs, b_c], w=[b_oacc])
                            kb.op("pool", lambda e: e.tensor_tensor(out=ob[:], in0=oacc[:], in1=sgt[:], op=ALU.mult), r=[b_oacc, b_sgt], w=[b_ob])
                            kb.dma(oT[s, h * 128:(h + 1) * 128, :], ob[:], r=[b_ob], w=[b_oT[s][h]])
                scale = 128 ** -0.5
                with Stage(kb) as st:
                    qr = st.rot("bq", [128, L], BF16, 2)
                    kr = st.rot("bk", [128, L], BF16, 2)
                    vr = st.rot("bv", [128, 16, 128], BF16, 2)
                    ptr = st.rot("bp", [128, 512], BF16, 3)
                    rcr = st.rot("brc", [128, 512], F32, 2)
                    obr = st.rot("bob", [128, 512], BF16, 2)
                    LA = 2
                    iters = [(hq, qb, kc) for hq in range(8) for qb in range(4) for kc in range(16)]
                    heads = {}
                    kvs = {}

                    def emit_S(i):
                        hq, qb, kc = iters[i]
                        g_ = hq // 4
                        if g_ not in kvs:
                            kT_, b_k = kr.next()
                            v_, b_v = vr.next()
                            kb.dma(kT_[:], gk[s, g_ * 128:(g_ + 1) * 128, :], r=[b_scr["gk"][s]], w=[b_k])
                            kb.dma(v_[:], gv[s].rearrange("(tc p) c -> p tc c", p=128)[:, :, g_ * 128:(g_ + 1) * 128], r=[b_scr["gv"][s]], w=[b_v])
                            kvs[g_] = (kT_, b_k, v_, b_v)
                        if hq not in heads:
                            qT_, b_q = qr.next()
                            kb.dma(qT_[:], dq[s, hq * 128:(hq + 1) * 128, :], r=[b_scr["dq"][s]], w=[b_q])
                            heads[hq] = (qT_, b_q)
                        kT_, b_k, v_, b_v = kvs[g_]
                        qT_, b_q = heads[hq]
                        sb_ = i % 3
                        mm(bank(sb_), kT_[:, kc * 128:(kc + 1) * 128], qT_[:, qb * 512:(qb + 1) * 512], True, True, r=[b_k, b_q], w=[b_ps[sb_]])

                    def emit_rest(i):
                        hq, qb, kc = iters[i]
                        g_ = hq // 4
                        kT_, b_k, v_, b_v = kvs[g_]
                        grp = hq * 4 + qb
                        po = 4 + grp % 2
                        pn = 6 + grp % 2
                        sb_ = i % 3
                        p_, b_p = ptr.next()
                        kb.op("act", lambda e: e.activation(out=p_[:], in_=bank(sb_), func=AF.Exp, scale=scale), r=[b_ps[sb_]], w=[b_p])
                        mm(bank(po), v_[:, kc, :], p_[:], kc == 0, kc == 15, r=[b_v, b_p], w=[b_ps[po]])
                        mm(bank(pn), onesb[:], p_[:], kc == 0, kc == 15, r=[b_c, b_p], w=[b_ps[pn]])
                        if kc != 15:
                            return
                        rc_, b_rc = rcr.next()
                        kb.op("dve", lambda e: e.reciprocal(out=rc_[:], in_=bank(pn)), r=[b_ps[pn]], w=[b_rc])
                        ob, b_ob = obr.next()
                        kb.op("dve", lambda e: e.tensor_tensor(out=ob[:], in0=bank(po), in1=rc_[:], op=ALU.mult), r=[b_ps[po], b_rc], w=[b_ob])
                        kb.dma(oT[s, 1024 + hq * 128:1024 + (hq + 1) * 128, qb * 512:(qb + 1) * 512], ob[:], r=[b_ob], w=[b_oT[s][8 + hq]])

                    n_it = len(iters)
                    for i in range(n_it + LA):
                        if i < n_it:
                            emit_S(i)
                        if i - LA >= 0:
                            emit_rest(i - LA)


            C.odd_mixer = odd_mixer

            for layer in range(2):
                for s in range(SPC):
                    if ("mix%d" % layer) in stages:
                        if layer == 0:
                            C.even_mixer(s)
                            out_proj(s, ev_w_out_t)
                        else:
                            C.odd_mixer(s)
                            out_proj(s, od_w_out_t)
                    if ("ca%d" % layer) in stages:
                        cross_attn(layer, s)
                    if ("mlp%d" % layer) in stages:
                        mlp(layer, s)

            if "epi" in stages:
                with Stage(kb) as st:
                    blk = st.rot("blk", [128, KC, 512], F32, 2)
                    yo = st.rot("yo", [128, D], F32, 2)
                    ev = 0
                    for s in range(SPC):
                        for tb in range(4):
                            bl, b_bl = blk.next()
                            kb.dma(bl[:], xTv[s][:, :, tb * 512:(tb + 1) * 512], r=xblk(s, tb), w=[b_bl])
                            for tt in range(4):
                                yt, b_yt = yo.next()
                                for j in range(4):
                                    for c4 in range(4):
                                        c = j * 4 + c4
                                        kb.op("pe", lambda e: e.transpose(out=bank(j)[:, c4 * 128:(c4 + 1) * 128],
                                                                          in_=bl[:, c, tt * 128:(tt + 1) * 128], identity=identf[:]),
                                              r=[b_bl, b_c], w=[b_ps[j]], inc=(c4 == 3))
                                    evac(ev, yt[:, j * 512:(j + 1) * 512], bank(j), [b_ps[j]], [b_yt])
                                    ev += 1
                                t0 = tb * 512 + tt * 128
                                kb.dma(y[s, t0:t0 + 128, :], yt[:], r=[b_yt], w=[])
        kb.final_wait()
    return C


def _tile_w(W, ncol):
    K, N = W.shape
    return np.ascontiguousarray(W.reshape(K // 128, 128, N // ncol, ncol).transpose(2, 1, 0, 3))


def _col(v, n=128):
    return np.ascontiguousarray(np.asarray(v).reshape(-1, n).T)


def host_consts():
    c = {}
    c["c_identf"] = np.eye(128, dtype=np.float32)
    bd = np.zeros((128, 128), np.float32)
    bd[:64, :64] = 1.0
    bd[64:, 64:] = 1.0
    c["c_bd64"] = bd
    p = np.arange(128)[:, None]
    j = np.arange(3968)[None, :]
    c["c_alibi"] = np.abs(j - p - 1920).astype(np.float32)
    t = np.linspace(0.0, 1.0, L, dtype=np.float32)[:, None]
    w = (np.float32(2.0 * math.pi / L) * np.arange(L, dtype=np.float32))[:, None]
    f = np.linspace(1e-4, 15, 16, dtype=np.float32)[None, :]
    z = np.concatenate([t, np.cos(f * w), -np.sin(f * w)], axis=-1).astype(np.float32)
    c["c_zT"] = np.ascontiguousarray(z.T)
    max_decay = math.log(1e-2) / 0.3
    min_decay = math.log(1e-2) / 1.5
    deltas = np.linspace(min_decay, max_decay, 1024, dtype=np.float32)
    win = np.exp(-t * np.abs(deltas)[None, :]).astype(np.float32)
    c["c_win"] = np.ascontiguousarray(win.reshape(16, 128, 1024))
    m0 = np.ones((128, 2), np.float32)
    m0[0, 0] = 0.0
    m0[:, 1] = 1.0 - m0[:, 0]
    c["c_m0"] = m0
    N = 2 * L
    tt = np.arange(L, dtype=np.float64)[:, None]
    ff = np.arange(L, dtype=np.float64)[None, :]
    ang = 2.0 * np.pi * tt * ff / N
    FT = np.concatenate([np.cos(ang), np.sin(ang)], axis=1)
    FT[:, L] = np.where(np.arange(L) % 2 == 0, 1.0, -1.0)
    FTt = FT.reshape(16, 128, 32, 128).transpose(2, 1, 0, 3)
    c["c_FT"] = np.ascontiguousarray(FTt).astype(ml_dtypes.bfloat16)
    wf = np.full((L,), 2.0 / N)
    wf[0] = 1.0 / N
    G = np.concatenate([(np.cos(ang) * wf[None, :]).T, (np.sin(ang) * (2.0 / N)).T], axis=0)
    G[L, :] = np.where(np.arange(L) % 2 == 0, 1.0, -1.0) / N
    Gt = G.reshape(2, 16, 128, 4, 512).transpose(3, 0, 2, 1, 4)
    c["c_G"] = np.ascontiguousarray(Gt).astype(ml_dtypes.bfloat16)
    rows = L // 64
    r = np.repeat(np.arange(rows, dtype=np.float32), 64)
    cc = np.tile(np.arange(64, dtype=np.float32), rows)
    half = 64
    inv = (np.float32(10000.0) ** (-np.arange(0, half, 2, dtype=np.float32) / half)).astype(np.float32)
    ang_r = r[:, None] * inv[None, :]
    ang_c = cc[:, None] * inv[None, :]
    a = np.concatenate([ang_r, ang_r, ang_c, ang_c], axis=-1).astype(np.float32)
    c["c_cos"] = np.ascontiguousarray(np.cos(a).T.astype(np.float32))
    c["c_sin"] = np.ascontiguousarray(np.sin(a).T.astype(np.float32))
    Pm = np.zeros((128, 128), np.float32)
    for m in range(32):
        Pm[m, m + 32] = -1.0
        Pm[m + 32, m] = 1.0
        Pm[m + 64, m + 96] = -1.0
        Pm[m + 96, m + 64] = 1.0
    c["c_PT"] = np.ascontiguousarray(Pm.T)
    s_ = np.arange(64)[:, None]
    t_ = np.arange(64)[None, :]
    mk = np.stack([np.tile((s_ <= t_).astype(np.float32), (1, 8)), np.tile((s_ >= t_).astype(np.float32), (1, 8))], axis=1)
    c["c_mask"] = np.ascontiguousarray(mk)
    return c


def host_layout(inp):
    o = {}
    g = np.stack([inp["norm_mix_g"], inp["norm_mem_q_g"], inp["norm_mem_kv_g"], inp["norm_ffn_g"]], axis=0)
    o["gains"] = np.ascontiguousarray(g.reshape(4, 2, KC, 128).transpose(3, 0, 1, 2))
    ev = inp["ev_w_in"][0]
    o["ev_w_in_t"] = _tile_w(ev, 128)
    o["ev_w_out_t"] = _tile_w(inp["ev_w_out"][0], 128)
    od = inp["od_w_in"][0]
    o["od_w_in_t"] = _tile_w(od, 128)
    o["od_w_out_t"] = _tile_w(inp["od_w_out"][0], 128)
    o["ca_wq_t"] = np.stack([_tile_w(inp["ca_w_q"][l], 128) for l in range(2)])
    o["ca_wk_t"] = np.stack([_tile_w(inp["ca_w_kv"][l][:, :512], 128) for l in range(2)])
    o["ca_wv_t"] = np.stack([_tile_w(inp["ca_w_kv"][l][:, 512:], 512)[0] for l in range(2)])
    o["ca_wo_t"] = np.stack([_tile_w(inp["ca_w_out"][l], 128) for l in range(2)])
    o["mlp_w1_t"] = np.stack([_tile_w(inp["mlp_w1"][l], 128) for l in range(2)])
    o["mlp_w2_t"] = np.stack([_tile_w(inp["mlp_w2"][l], 128) for l in range(2)])
    cw = inp["hy_conv_w"][0]
    o["hy_cw"] = np.ascontiguousarray(cw.reshape(3, 24, 128).transpose(2, 0, 1))
    o["hy_cb"] = _col(inp["hy_conv_b"][0])
    o["hy_bias"] = _col(inp["hy_bias"][0])
    o["hy_fw1"] = np.ascontiguousarray(inp["hy_fw1"][0])
    o["hy_fw2"] = np.ascontiguousarray(inp["hy_fw2"][0])
    o["hy_fw3"] = np.ascontiguousarray(inp["hy_fw3"][0])
    o["hy_fw4"] = np.ascontiguousarray(inp["hy_fw4"][0])
    o["hy_fcols"] = np.ascontiguousarray(np.stack([inp["hy_fb1"][0], inp["hy_fb2"][0], inp["hy_fb3"][0], inp["hy_sin_freq"][0]], axis=1))
    qg = np.concatenate([inp["df_q_g"][0], inp["df_q_g"][0]])
    kg = np.concatenate([inp["df_k_g"][0], inp["df_k_g"][0]])
    o["df_cols"] = np.ascontiguousarray(np.stack([qg, kg, inp["df_subln_g"][0]], axis=1))
    o["df_lam"] = np.ascontiguousarray(np.stack([inp["df_lam_q1"][0], inp["df_lam_k1"][0], inp["df_lam_q2"][0], inp["df_lam_k2"][0]])[None])
    lb = inp["hg_lb_logits"]
    o["hg_lb"] = np.ascontiguousarray(lb.reshape(2, 2, 8, 128).transpose(3, 0, 1, 2))
    o["od_cols"] = np.ascontiguousarray(np.stack([inp["hg_norm_g"][0], inp["gq_q_g"][0], inp["gq_k_g"][0]], axis=1))
    o["ca_cols"] = np.ascontiguousarray(np.stack([inp["ca_q_g"], inp["ca_k_g"]], axis=0).transpose(2, 0, 1))
    return {k: np.ascontiguousarray(v, dtype=np.float32) for k, v in o.items()}


_CACHE = {}


def run(inputs, ncores=NCORES, stages=ALL_STAGES, debug=()):
    key = (tuple(stages), tuple(debug))
    if key not in _CACHE:
        _CACHE[key] = build(stages=stages, debug=debug)
    if "consts" not in _CACHE:
        _CACHE["consts"] = host_consts()
    C = _CACHE[key]
    shared = dict(_CACHE["consts"])
    shared.update(host_layout(inputs))
    x = np.ascontiguousarray(inputs["x"], dtype=np.float32)
    mem = np.ascontiguousarray(inputs["mem"], dtype=np.float32)
    in_maps = []
    for c in range(ncores):
        m = {"x": x[c * SPC:(c + 1) * SPC], "mem": mem[c * SPC:(c + 1) * SPC]}
        m.update(shared)
        in_maps.append({k: v for k, v in m.items() if k in C.din})
    res = run_bass_kernel_spmd(C.nc, in_maps, core_ids=list(range(ncores)))
    return res.results


def kernel(**inputs):
    results = run(inputs)
    out = np.concatenate([r["y"] for r in results], axis=0)
    return out.astype(np.float32)
```

```python
import contextlib
import math
import os
import numpy as np
import ml_dtypes
import concourse.bass as bass
import concourse.mybir as mybir
from concourse.bass_utils import run_bass_kernel_spmd

F32 = mybir.dt.float32
BF16 = mybir.dt.bfloat16
AF = mybir.ActivationFunctionType
ALU = mybir.AluOpType
AX = mybir.AxisListType

NCORES = 8
SPC = 2
L = 2048
D = 2048
KC = D // 128
MEM = 256
EPS = 1e-6
FFN = 8192
EVEN_IN = 6144
ODD_IN = 6656


class Buf:
    __slots__ = ("name", "w", "r")

    def __init__(self, name):
        self.name = name
        self.w = {}
        self.r = {}


class Eng:
    def __init__(self, name, h, sem):
        self.name, self.h, self.sem = name, h, sem
        self.n = 0
        self.waited = {}


class KB:
    def __init__(self, nc, es):
        self.nc, self.es = nc, es
        self.sems = {}
        self.eng = {}
        for name, h in (("pe", nc.tensor), ("act", nc.scalar), ("dve", nc.vector),
                        ("pool", nc.gpsimd), ("sp", nc.sync)):
            sem = es.enter_context(nc.semaphore("s_" + name))
            self.eng[name] = Eng(name, h, sem)
            self.sems[name] = sem
        self.dq = {}
        for q, nslots in (("sp", 8), ("pool", 8), ("act", 4)):
            slots = []
            for i in range(nslots):
                key = "d_%s%d" % (q, i)
                self.sems[key] = es.enter_context(nc.semaphore(key))
                slots.append([key, 0])
            self.dq[q] = [slots, 0]
        self.nbuf = 0

    def buf(self, name="b"):
        self.nbuf += 1
        return Buf("%s%d" % (name, self.nbuf))

    def _wait(self, E, key, val):
        if E.waited.get(key, 0) >= val:
            return
        if key == "pe" and E.name == "pe":
            return
        E.h.wait_ge(self.sems[key], val)
        E.waited[key] = val

    def _deps(self, E, r, w):
        need = {}
        for b in r:
            for k_, v in b.w.items():
                if need.get(k_, 0) < v:
                    need[k_] = v
        for b in w:
            for k_, v in b.w.items():
                if need.get(k_, 0) < v:
                    need[k_] = v
            for k_, v in b.r.items():
                if need.get(k_, 0) < v:
                    need[k_] = v
        for k_, v in need.items():
            self._wait(E, k_, v)

    def _reg(self, key, val, r, w):
        for b in r:
            if b.r.get(key, 0) < val:
                b.r[key] = val
        for b in w:
            b.w = {key: val}
            b.r = {}

    def op(self, eng, fn, r=(), w=(), inc=True):
        E = self.eng[eng]
        self._deps(E, r, w)
        ins = fn(E.h)
        val = E.n + 1
        if inc:
            ins.then_inc(E.sem, 1)
            E.n = val
        self._reg(eng, val, r, w)
        return ins

    def dma(self, out, in_, r=(), w=(), q="sp", **kw):
        E = self.eng[q]
        slots, n = self.dq[q]
        slot = slots[n % len(slots)]
        self.dq[q][1] = n + 1
        self._deps(E, r, w)
        if slot[1] > 0:
            self._wait(E, slot[0], slot[1])
        ins = E.h.dma_start(out=out, in_=in_, **kw)
        slot[1] += 16
        ins.then_inc(self.sems[slot[0]], 16)
        self._reg(slot[0], slot[1], r, w)

    def barrier(self, engines=("pe", "act", "dve", "pool", "sp")):
        cur = {}
        for name, E in self.eng.items():
            if E.n > 0:
                cur[name] = E.n
        for q, (slots, n) in self.dq.items():
            for key, v in slots:
                if v > 0:
                    cur[key] = v
        for name in engines:
            E = self.eng[name]
            for key, v in cur.items():
                if key == name:
                    continue
                self._wait(E, key, v)

    def final_wait(self):
        self.barrier(engines=("sp",))


class Stage:
    def __init__(self, kb):
        self.kb = kb
        self.es = contextlib.ExitStack()
        self.cnt = 0

    def __enter__(self):
        self.es.__enter__()
        return self

    def __exit__(self, *a):
        self.kb.barrier()
        return self.es.__exit__(*a)

    def sb(self, name, shape, dt):
        self.kb.nbuf += 1
        t = self.es.enter_context(self.kb.nc.sbuf_tensor("%s_%d" % (name, self.kb.nbuf), list(shape), dt))
        return t, self.kb.buf(name)

    def rot(self, name, shape, dt, n):
        return Rot([self.sb(name + str(i), shape, dt) for i in range(n)])


class Rot:
    def __init__(self, items):
        self.items = items
        self.i = 0

    def next(self):
        it = self.items[self.i % len(self.items)]
        self.i += 1
        return it


TWO_PI = 2.0 * math.pi
LAM_INIT0 = 0.8 - 0.6 * math.exp(-0.3 * 0)
ALL_STAGES = ("pro", "filt", "mix0", "ca0", "mlp0", "mix1", "ca1", "mlp1", "epi")


class Ctx:
    pass


def build(stages=ALL_STAGES, debug=()):
    C = Ctx()
    nc = bass.Bass("TRN2", target_bir_lowering=False)
    C.nc = nc
    C.din = {}
    debug = set(debug)

    def inp(name, shape, dt=F32):
        t = nc.dram_tensor(name, list(shape), dt, kind="ExternalInput").ap()
        C.din[name] = t
        return t

    def scratch(name, shape, dt):
        kind = "ExternalOutput" if name in debug else "Internal"
        return nc.dram_tensor(name, list(shape), dt, kind=kind).ap()

    x = inp("x", [SPC, L, D])
    mem = inp("mem", [SPC, MEM, D])
    y = nc.dram_tensor("y", [SPC, L, D], F32, kind="ExternalOutput").ap()
    gains_d = inp("gains", [128, 4, 2, KC])
    ev_w_in_t = inp("ev_w_in_t", [48, 128, KC, 128])
    ev_w_out_t = inp("ev_w_out_t", [16, 128, KC, 128])
    od_w_in_t = inp("od_w_in_t", [52, 128, KC, 128])
    od_w_out_t = inp("od_w_out_t", [16, 128, KC, 128])
    ca_wq_t = inp("ca_wq_t", [2, 4, 128, KC, 128])
    ca_wk_t = inp("ca_wk_t", [2, 4, 128, KC, 128])
    ca_wv_t = inp("ca_wv_t", [2, 128, KC, 512])
    ca_wo_t = inp("ca_wo_t", [2, 16, 128, 4, 128])
    mlp_w1_t = inp("mlp_w1_t", [2, 64, 128, KC, 128])
    mlp_w2_t = inp("mlp_w2_t", [2, 16, 128, 64, 128])
    hy_cw_d = inp("hy_cw", [128, 3, 24])
    hy_cb_d = inp("hy_cb", [128, 24])
    hy_bias_d = inp("hy_bias", [128, 8])
    hy_fw1_d = inp("hy_fw1", [33, 64])
    hy_fw2_d = inp("hy_fw2", [64, 64])
    hy_fw3_d = inp("hy_fw3", [64, 64])
    hy_fw4_d = inp("hy_fw4", [64, 2048])
    hy_fcols_d = inp("hy_fcols", [64, 4])
    df_cols_d = inp("df_cols", [128, 3])
    df_lam_d = inp("df_lam", [1, 4, 64])
    hg_lb_d = inp("hg_lb", [128, 2, 2, 8])
    od_cols_d = inp("od_cols", [128, 3])
    ca_cols_d = inp("ca_cols", [128, 2, 2])
    c_identf = inp("c_identf", [128, 128])
    c_bd64 = inp("c_bd64", [128, 128])
    c_alibi = inp("c_alibi", [128, 3968])
    c_zT = inp("c_zT", [33, L])
    c_win = inp("c_win", [16, 128, 1024])
    c_m0 = inp("c_m0", [128, 2])
    c_FT = inp("c_FT", [32, 128, 16, 128], BF16)
    c_G = inp("c_G", [4, 2, 128, 16, 512], BF16)
    c_cos = inp("c_cos", [128, L])
    c_sin = inp("c_sin", [128, L])
    c_PT = inp("c_PT", [128, 128])
    c_mask = inp("c_mask", [64, 2, 512])

    xT = scratch("xT", [SPC, D, L], F32)
    oT = scratch("oT", [SPC, D, L], BF16)
    hx1 = scratch("hx1", [SPC, 1024, L], F32)
    hvx = scratch("hvx", [SPC, 1024, L], F32)
    dq = scratch("dq", [SPC, 1024, L], BF16)
    dk = scratch("dk", [SPC, 1024, L], BF16)
    dv = scratch("dv", [SPC, L, 1024], BF16)
    gk = scratch("gk", [SPC, 256, L], BF16)
    gv = scratch("gv", [SPC, L, 256], BF16)
    KA = scratch("KA", [16, 128, 1024], F32)
    KBt = scratch("KB", [16, 128, 1024], F32)
    KD0 = scratch("KD0", [128, 1024], F32)
    xTv = [xT[s].rearrange("(c p) t -> p c t", p=128) for s in range(SPC)]
    oTv = [oT[s].rearrange("(c p) t -> p c t", p=128) for s in range(SPC)]

    es = contextlib.ExitStack()
    with es:
        kb = KB(nc, es)
        C.kb = kb
        b_xT = [[[kb.buf("xT") for _ in range(KC)] for _ in range(4)] for _ in range(SPC)]
        b_oT = [[kb.buf("oT") for _ in range(KC)] for _ in range(SPC)]
        b_scr = {n: [kb.buf(n) for _ in range(SPC)] for n in ("hx1", "hvx", "dq", "dk", "dv", "gk", "gv")}
        b_K = kb.buf("K")
        psA = es.enter_context(nc.psum_tensor("psA", [128, 4, 512], F32))
        psB = es.enter_context(nc.psum_tensor("psB", [128, 4, 512], F32))
        pst = [psA, psA, psA, psA, psB, psB, psB, psB]
        b_ps = [kb.buf("ps") for _ in range(8)]

        def bank(i):
            return pst[i][:, i % 4, :]

        def bank_bf(i):
            return pst[i][:, i % 4, :].bitcast(BF16)

        def xall(s, oc):
            return [b_xT[s][tb][oc] for tb in range(4)]

        def xblk(s, tb):
            return list(b_xT[s][tb])

        def evac(i, out, in_, r, w):
            if i % 2 == 0:
                kb.op("act", lambda e: e.copy(out=out, in_=in_), r=r, w=w)
            else:
                kb.op("dve", lambda e: e.tensor_copy(out=out, in_=in_), r=r, w=w)

        def mm(out, lhsT, rhs, start, stop, r, w, inc=None):
            kb.op("pe", lambda e: e.matmul(out, lhsT=lhsT, rhs=rhs, start=start, stop=stop), r=r, w=w,
                  inc=(stop if inc is None else inc))

        def rstd_inplace(ap, b, src, src_b, inv_n):
            kb.op("dve", lambda e: e.tensor_scalar(out=ap, in0=src, scalar1=inv_n, scalar2=EPS, op0=ALU.mult, op1=ALU.add),
                  r=src_b, w=[b])
            kb.op("act", lambda e: e.activation(out=ap, in_=ap, func=AF.Ln), r=[b], w=[b])
            kb.op("act", lambda e: e.activation(out=ap, in_=ap, func=AF.Exp, scale=-0.5), r=[b], w=[b])

        psA4 = psA[:, :, :]
        psB4 = psB[:, :, :]
        bA = b_ps[0:4]
        bB = b_ps[4:8]

        with Stage(kb) as g:
            identf, b_c = g.sb("identf", [128, 128], F32)
            b_c = kb.buf("const")
            kb.dma(identf[:], c_identf[:, :], w=[b_c])
            identb, _ = g.sb("identb", [128, 128], BF16)
            onesb, _ = g.sb("onesb", [128, 128], BF16)
            onesf, _ = g.sb("onesf", [128, 128], F32)
            bd64f, _ = g.sb("bd64f", [128, 128], F32)
            bd64b, _ = g.sb("bd64b", [128, 128], BF16)
            gains, _ = g.sb("gains", [128, 4, 2, KC], F32)
            kb.dma(bd64f[:], c_bd64[:, :], w=[b_c])
            kb.dma(gains[:], gains_d[:, :, :, :], w=[b_c])
            kb.op("dve", lambda e: e.tensor_copy(out=identb[:], in_=identf[:]), r=[b_c], w=[b_c])
            kb.op("dve", lambda e: e.tensor_copy(out=bd64b[:], in_=bd64f[:]), r=[b_c], w=[b_c])
            kb.op("dve", lambda e: e.memset(onesb[:], 1.0), w=[b_c])
            kb.op("dve", lambda e: e.memset(onesf[:], 1.0), w=[b_c])
            m0, _ = g.sb("m0", [128, 2], F32)
            kb.dma(m0[:], c_m0[:, :], w=[b_c])
            hycw, _ = g.sb("hycw", [128, 3, 24], F32)
            hycb, _ = g.sb("hycb", [128, 24], F32)
            hyb, _ = g.sb("hyb", [128, 8], F32)
            dfc, _ = g.sb("dfc", [128, 3], F32)
            odc, _ = g.sb("odc", [128, 3], F32)
            cac, _ = g.sb("cac", [128, 2, 2], F32)
            kb.dma(hycw[:], hy_cw_d[:, :, :], w=[b_c])
            kb.dma(hycb[:], hy_cb_d[:, :], w=[b_c])
            kb.dma(hyb[:], hy_bias_d[:, :], w=[b_c])
            kb.dma(dfc[:], df_cols_d[:, :], w=[b_c])
            kb.dma(odc[:], od_cols_d[:, :], w=[b_c])
            kb.dma(cac[:], ca_cols_d[:, :, :], w=[b_c])
            neglam, _ = g.sb("neglam", [128, 1], F32)
            sg2, _ = g.sb("sg2", [128, 1], F32)
            lbs, _ = g.sb("lbs", [128, 2, 8], F32)
            oml, _ = g.sb("oml", [128, 2, 8], F32)
            noml, _ = g.sb("noml", [128, 2, 8], F32)
            RgT, _ = g.sb("RgT", [128, 2, 128], BF16)

            with Stage(kb) as st:
                lamv, b_l = st.sb("lamv", [1, 4, 64], F32)
                kb.dma(lamv[:], df_lam_d[:, :, :], w=[b_l])
                pr, b_pr = st.sb("pr", [1, 2, 64], F32)
                sm, b_sm = st.sb("sm", [1, 2], F32)
                kb.op("dve", lambda e: e.tensor_tensor(out=pr[:, 0, :], in0=lamv[:, 0, :], in1=lamv[:, 1, :], op=ALU.mult), r=[b_l], w=[b_pr])
                kb.op("dve", lambda e: e.tensor_tensor(out=pr[:, 1, :], in0=lamv[:, 2, :], in1=lamv[:, 3, :], op=ALU.mult), r=[b_l], w=[b_pr])
                kb.op("dve", lambda e: e.reduce_sum(out=sm[:], in_=pr[:], axis=AX.X), r=[b_pr], w=[b_sm])
                kb.op("act", lambda e: e.activation(out=sm[:], in_=sm[:], func=AF.Exp), r=[b_sm], w=[b_sm])
                nl, b_nl = st.sb("nl", [1, 1], F32)
                kb.op("dve", lambda e: e.tensor_tensor(out=nl[:], in0=sm[:, 1:2], in1=sm[:, 0:1], op=ALU.subtract), r=[b_sm], w=[b_nl])
                kb.op("dve", lambda e: e.tensor_scalar_add(out=nl[:], in0=nl[:], scalar1=-LAM_INIT0), r=[b_nl], w=[b_nl])
                mm(bank(0)[:, 0:1], onesf[0:1, :], nl[0:1, 0:1], True, True, r=[b_nl, b_c], w=[b_ps[0]])
                kb.op("dve", lambda e: e.tensor_copy(out=neglam[:], in_=bank(0)[:, 0:1]), r=[b_ps[0]], w=[b_c])
                kb.op("dve", lambda e: e.tensor_scalar(out=sg2[:], in0=dfc[:, 2:3], scalar1=(1.0 - LAM_INIT0), scalar2=None, op0=ALU.mult), r=[b_c], w=[b_c])
                lg, b_lg = st.sb("lg", [128, 2, 2, 8], F32)
                kb.dma(lg[:], hg_lb_d[:, :, :, :], w=[b_lg])
                kb.op("dve", lambda e: e.tensor_tensor(out=lbs[:], in0=lg[:, :, 1, :], in1=lg[:, :, 0, :], op=ALU.subtract), r=[b_lg], w=[b_c])
                kb.op("act", lambda e: e.activation(out=lbs[:], in_=lbs[:], func=AF.Sigmoid), r=[b_c], w=[b_c])
                kb.op("dve", lambda e: e.tensor_scalar(out=oml[:], in0=lbs[:], scalar1=-1.0, scalar2=1.0, op0=ALU.mult, op1=ALU.add), r=[b_c], w=[b_c])
                kb.op("dve", lambda e: e.tensor_scalar(out=noml[:], in0=oml[:], scalar1=-1.0, scalar2=None, op0=ALU.mult), r=[b_c], w=[b_c])
                ptf, b_pt = st.sb("ptf", [128, 128], F32)
                kb.dma(ptf[:], c_PT[:, :], w=[b_pt])
                kb.op("dve", lambda e: e.tensor_scalar(out=RgT[:, 0, :], in0=ptf[:], scalar1=odc[:, 1:2], scalar2=None, op0=ALU.mult), r=[b_pt, b_c], w=[b_c])
                kb.op("dve", lambda e: e.tensor_scalar(out=RgT[:, 1, :], in0=ptf[:], scalar1=odc[:, 2:3], scalar2=None, op0=ALU.mult), r=[b_pt, b_c], w=[b_c])

            def norm_to_hT(s, which, layer, hT, b_hT):
                with Stage(kb) as st:
                    xb = st.rot("nx", [128, KC, 512], F32, 2)
                    sq, b_sq = st.sb("nsq", [128, KC, 512], BF16)
                    rs = st.rot("nrs", [128, 512], F32, 2)
                    for tb in range(4):
                        xt, b_xt = xb.next()
                        kb.dma(xt[:], xTv[s][:, :, tb * 512:(tb + 1) * 512], r=xblk(s, tb), w=[b_xt])
                        kb.op("act", lambda e: e.activation(out=sq[:], in_=xt[:], func=AF.Square), r=[b_xt], w=[b_sq])
                        for c in range(KC):
                            mm(bank(4), onesb[:], sq[:, c, :], c == 0, c == KC - 1, r=[b_sq, b_c], w=[b_ps[4]])
                        r_, b_r = rs.next()
                        rstd_inplace(r_[:], b_r, bank(4), [b_ps[4]], 1.0 / D)
                        for c in range(KC):
                            kb.op("dve", lambda e: e.scalar_tensor_tensor(out=hT[:, c, tb * 512:(tb + 1) * 512], in0=xt[:, c, :],
                                                                         scalar=gains[:, which, layer, c:c + 1], in1=r_[:],
                                                                         op0=ALU.mult, op1=ALU.mult),
                                  r=[b_xt, b_r, b_c], w=[b_hT[c]])

            def lin_fm(wt_rot, w_ap, hT, b_hT, kcn, ntb=4, q="pool", pbase=0):
                wt, b_wt = wt_rot.next()
                kb.dma(wt[:], w_ap, w=[b_wt], q=q)
                for tb in range(ntb):
                    for kc in range(kcn):
                        mm(bank(pbase + tb), wt[:, kc, :], hT[:, kc, tb * 512:(tb + 1) * 512], kc == 0, kc == kcn - 1,
                           r=[b_wt, b_hT[kc]], w=[b_ps[pbase + tb]])

            def residual_add(st_rot, s, oc, pbase=0):
                xr, b_xr = st_rot.next()
                kb.dma(xr[:], xT[s, oc * 128:(oc + 1) * 128, :], r=xall(s, oc), w=[b_xr])
                kb.op("dve", lambda e: e.tensor_tensor(out=xr[:].rearrange("p (b t) -> p b t", b=4), in0=xr[:].rearrange("p (b t) -> p b t", b=4),
                                                       in1=(psA4 if pbase == 0 else psB4), op=ALU.add), r=[b_xr] + (bA if pbase == 0 else bB), w=[b_xr])
                kb.dma(xT[s, oc * 128:(oc + 1) * 128, :], xr[:], r=[b_xr], w=xall(s, oc))

            def rmsnorm_fm(st, src_f, b_src, ones_l, inv_n, gcol, out_bf, b_out, extra=None):
                sqb, b_sqb = st["sqb"].next()
                rs, b_rs = st["rs"].next()
                kb.op("act", lambda e: e.activation(out=sqb[:], in_=src_f, func=AF.Square), r=[b_src], w=[b_sqb])
                for tb in range(4):
                    mm(bank(4 + tb), ones_l, sqb[:, tb * 512:(tb + 1) * 512], True, True, r=[b_sqb, b_c], w=[b_ps[4 + tb]])
                rstd_inplace(rs[:].rearrange("p (b t) -> p b t", b=4), b_rs, psB4, bB, inv_n)
                return rs, b_rs

            if "pro" in stages:
                with Stage(kb) as st:
                    xin = st.rot("xin", [128, D], F32, 2)
                    stg = st.rot("stg", [128, KC, 512], F32, 2)
                    ev = 0
                    for s in range(SPC):
                        for tb in range(4):
                            sg, b_sg = stg.next()
                            for tt in range(4):
                                xi, b_xi = xin.next()
                                t0 = tb * 512 + tt * 128
                                kb.dma(xi[:], x[s, t0:t0 + 128, :], w=[b_xi])
                                for j in range(4):
                                    for c4 in range(4):
                                        c = j * 4 + c4
                                        kb.op("pe", lambda e: e.transpose(out=bank(j)[:, c4 * 128:(c4 + 1) * 128],
                                                                          in_=xi[:, c * 128:(c + 1) * 128], identity=identf[:]),
                                              r=[b_xi, b_c], w=[b_ps[j]], inc=(c4 == 3))
                                    src = bank(j).rearrange("p (c t) -> p c t", c=4)
                                    dst = sg[:, j * 4:(j + 1) * 4, tt * 128:(tt + 1) * 128]
                                    evac(ev, dst, src, [b_ps[j]], [b_sg])
                                    ev += 1
                            kb.dma(xTv[s][:, :, tb * 512:(tb + 1) * 512], sg[:], r=[b_sg], w=xblk(s, tb))

            def sin_layer(st, dst, src_ps, b_src, col, n):
                u, b_u = st["u"].next()
                qf, b_qf = st["qf"].next()
                qi, b_qi = st["qi"].next()
                kb.op("dve", lambda e: e.tensor_scalar(out=u[:], in0=src_ps, scalar1=st["fcols"][:, 3:4], scalar2=st["ffb"][:, col:col + 1],
                                                       op0=ALU.mult, op1=ALU.add), r=b_src + [st["b_f"]], w=[b_u])
                kb.op("dve", lambda e: e.tensor_scalar(out=u[:], in0=u[:], scalar1=17.0 * math.pi, scalar2=None, op0=ALU.add), r=[b_u], w=[b_u])
                kb.op("dve", lambda e: e.tensor_scalar(out=qf[:], in0=u[:], scalar1=1.0 / TWO_PI, scalar2=None, op0=ALU.mult), r=[b_u], w=[b_qf])
                kb.op("dve", lambda e: e.tensor_copy(out=qi[:], in_=qf[:]), r=[b_qf], w=[b_qi])
                kb.op("dve", lambda e: e.tensor_copy(out=qf[:], in_=qi[:]), r=[b_qi], w=[b_qf])
                kb.op("dve", lambda e: e.scalar_tensor_tensor(out=u[:], in0=qf[:], scalar=-TWO_PI, in1=u[:], op0=ALU.mult, op1=ALU.add), r=[b_qf, b_u], w=[b_u])
                kb.op("dve", lambda e: e.tensor_scalar(out=qf[:], in0=u[:], scalar1=0.0, scalar2=TWO_PI, op0=ALU.is_lt, op1=ALU.mult), r=[b_u], w=[b_qf])
                kb.op("dve", lambda e: e.tensor_tensor(out=u[:], in0=u[:], in1=qf[:], op=ALU.add), r=[b_u, b_qf], w=[b_u])
                kb.op("act", lambda e: e.activation(out=dst, in_=u[:], func=AF.Sin, bias=st["mpi"][:], scale=1.0), r=[b_u, st["b_f"]], w=[st["b_h"]])

            if "filt" in stages:
                with Stage(kb) as st0:
                    zT, b_f = st0.sb("zT", [33, L], F32)
                    w1, _ = st0.sb("w1", [33, 64], F32)
                    w2, _ = st0.sb("w2", [64, 64], F32)
                    w3, _ = st0.sb("w3", [64, 64], F32)
                    w4, _ = st0.sb("w4", [64, 2048], F32)
                    fcols, _ = st0.sb("fcols", [64, 4], F32)
                    ffb, _ = st0.sb("ffb", [64, 3], F32)
                    mpi, _ = st0.sb("mpi", [64, 1], F32)
                    for t_, d_ in ((zT, c_zT), (w1, hy_fw1_d), (w2, hy_fw2_d), (w3, hy_fw3_d), (w4, hy_fw4_d), (fcols, hy_fcols_d)):
                        kb.dma(t_[:], d_[:, :], w=[b_f])
                    kb.op("dve", lambda e: e.memset(mpi[:], -math.pi), w=[b_f])
                    kb.op("dve", lambda e: e.tensor_scalar(out=ffb[:], in0=fcols[:, 0:3], scalar1=fcols[:, 3:4], scalar2=None, op0=ALU.mult), r=[b_f], w=[b_f])
                    hA, b_hA = st0.sb("hA", [64, L], F32)
                    hB, b_hB = st0.sb("hB", [64, L], F32)
                    sl = {"u": st0.rot("u", [64, 512], F32, 2), "qf": st0.rot("qf", [64, 512], F32, 2),
                          "qi": st0.rot("qi", [64, 512], mybir.dt.int32, 2), "fcols": fcols, "ffb": ffb, "mpi": mpi, "b_f": b_f}
                    for li, (wl, kdim, src, b_s, dst, b_d) in enumerate(((w1, 33, zT, b_f, hA, b_hA), (w2, 64, hA, b_hA, hB, b_hB), (w3, 64, hB, b_hB, hA, b_hA))):
                        sl["b_h"] = b_d
                        for tb in range(4):
                            mm(bank(tb)[0:64, :], wl[0:kdim, :], src[0:kdim, tb * 512:(tb + 1) * 512], True, True, r=[b_f, b_s], w=[b_ps[tb]])
                            sin_layer(sl, dst[:, tb * 512:(tb + 1) * 512], bank(tb)[0:64, :], [b_ps[tb]], li, 512)
                    h3 = hA
                    b_h3 = b_hA
                    for cb in range(2):
                        with Stage(kb) as st:
                            kp, b_kp = st.sb("kp", [128, 16, 512], BF16)
                            km, b_km = st.sb("km", [128, 16, 512], BF16)
                            wn = st.rot("wn", [128, 512], F32, 2)
                            ta = st.rot("ta", [128, 512], F32, 2)
                            tbb = st.rot("tbb", [128, 512], F32, 2)
                            for tc in range(16):
                                mm(bank(0), h3[:, tc * 128:(tc + 1) * 128], w4[:, cb * 512:(cb + 1) * 512], True, True, r=[b_h3, b_f], w=[b_ps[0]])
                                mm(bank(1), h3[:, tc * 128:(tc + 1) * 128], w4[:, 1024 + cb * 512:1024 + (cb + 1) * 512], True, True, r=[b_h3, b_f], w=[b_ps[1]])
                                wt_, b_w = wn.next()
                                kb.dma(wt_[:], c_win[tc, :, cb * 512:(cb + 1) * 512], w=[b_w])
                                a_, b_a = ta.next()
                                b__, b_b = tbb.next()
                                kb.op("dve", lambda e: e.tensor_tensor(out=a_[:], in0=bank(0), in1=wt_[:], op=ALU.mult), r=[b_ps[0], b_w], w=[b_a])
                                kb.op("dve", lambda e: e.tensor_tensor(out=b__[:], in0=bank(1), in1=wt_[:], op=ALU.mult), r=[b_ps[1], b_w], w=[b_b])
                                if tc == 0:
                                    kb.op("dve", lambda e: e.tensor_scalar(out=b__[:], in0=b__[:], scalar1=m0[:, 0:1], scalar2=None, op0=ALU.mult), r=[b_b, b_c], w=[b_b])
                                kb.op("pool", lambda e: e.tensor_tensor(out=kp[:, tc, :], in0=a_[:], in1=b__[:], op=ALU.add), r=[b_a, b_b], w=[b_kp])
                                kb.op("pool", lambda e: e.tensor_tensor(out=km[:, tc, :], in0=a_[:], in1=b__[:], op=ALU.subtract), r=[b_a, b_b], w=[b_km])
                            ftr = st.rot("ft", [128, 16, 128], BF16, 3)
                            ko = st.rot("ko", [128, 512], F32, 3)
                            for j in range(16):
                                ftc, b_fc = ftr.next()
                                kb.dma(ftc[:], c_FT[j], w=[b_fc])
                                fts, b_fs = ftr.next()
                                kb.dma(fts[:], c_FT[16 + j], w=[b_fs])
                                for tc in range(16):
                                    mm(bank(2), ftc[:, tc, :], kp[:, tc, :], tc == 0, tc == 15, r=[b_fc, b_kp], w=[b_ps[2]])
                                for tc in range(16):
                                    mm(bank(3), fts[:, tc, :], km[:, tc, :], tc == 0, tc == 15, r=[b_fs, b_km], w=[b_ps[3]])
                                ka_, b_ka = ko.next()
                                kb.op("act", lambda e: e.copy(out=ka_[:], in_=bank(2)), r=[b_ps[2]], w=[b_ka])
                                kb.dma(KA[j, :, cb * 512:(cb + 1) * 512], ka_[:], r=[b_ka], w=[b_K])
                                kb_, b_kb = ko.next()
                                if j == 0:
                                    kb.op("dve", lambda e: e.tensor_scalar(out=kb_[:], in0=bank(3), scalar1=m0[:, 0:1], scalar2=None, op0=ALU.mult), r=[b_ps[3], b_c], w=[b_kb])
                                else:
                                    kb.op("dve", lambda e: e.tensor_copy(out=kb_[:], in_=bank(3)), r=[b_ps[3]], w=[b_kb])
                                kb.dma(KBt[j, :, cb * 512:(cb + 1) * 512], kb_[:], r=[b_kb], w=[b_K])
                                if j == 0:
                                    for tc in range(16):
                                        mm(bank(4), fts[:, tc, :], kp[:, tc, :], tc == 0, tc == 15, r=[b_fs, b_kp], w=[b_ps[4]])
                                    kd_, b_kd = ko.next()
                                    kb.op("dve", lambda e: e.tensor_scalar(out=kd_[:], in0=bank(4), scalar1=m0[:, 1:2], scalar2=None, op0=ALU.mult), r=[b_ps[4], b_c], w=[b_kd])
                                    kb.op("dve", lambda e: e.scalar_tensor_tensor(out=kd_[:], in0=ka_[:], scalar=m0[:, 0:1], in1=kd_[:], op0=ALU.mult, op1=ALU.add),
                                          r=[b_ka, b_kd, b_c], w=[b_kd])
                                    kb.dma(KD0[:, cb * 512:(cb + 1) * 512], kd_[:], r=[b_kd], w=[b_K])

            def out_proj(s, w_t):
                with Stage(kb) as st:
                    ot = st.sb("oTs", [128, KC, L], BF16)[0]
                    b_ot = [kb.buf("oTs") for _ in range(KC)]
                    for c in range(KC):
                        kb.dma(ot[:, c, :], oT[s, c * 128:(c + 1) * 128, :], r=[b_oT[s][c]], w=[b_ot[c]])
                    wr = st.rot("wo", [128, KC, 128], BF16, 2)
                    xr = st.rot("xr", [128, L], F32, 2)
                    for oc in range(16):
                        lin_fm(wr, w_t[oc], ot, b_ot, KC, pbase=4 * (oc % 2))
                        residual_add(xr, s, oc, pbase=4 * (oc % 2))

            def mlp(layer, s):
                with Stage(kb) as sh:
                    hT = sh.sb("hT", [128, KC, L], BF16)[0]
                    b_hT = [kb.buf("hT") for _ in range(KC)]
                    norm_to_hT(s, 3, layer, hT, b_hT)
                    with Stage(kb) as st:
                        hid = st.sb("hid", [128, 32, 1024], BF16)[0]
                        b_hid = [kb.buf("hid") for _ in range(32)]
                        w1r = st.rot("w1t", [128, KC, 128], BF16, 4)
                        w2r = st.rot("w2t", [128, 32, 128], BF16, 3)
                        tmp = st.rot("mt", [128, 512], F32, 3)
                        xr = st.rot("mxr", [128, 512], F32, 4)
                        nb1 = 0
                        nb2 = 0
                        for tt in range(2):
                            for half in range(2):
                                for hcl in range(32):
                                    hc = half * 32 + hcl
                                    w1_, b_w1 = w1r.next()
                                    kb.dma(w1_[:], mlp_w1_t[layer, hc], w=[b_w1], q="pool")
                                    pbs = [(nb1 + t2) % 4 for t2 in range(2)]
                                    nb1 += 2
                                    for kc in range(KC):
                                        for t2 in range(2):
                                            tok = tt * 1024 + t2 * 512
                                            mm(bank(pbs[t2]), w1_[:, kc, :], hT[:, kc, tok:tok + 512], kc == 0, kc == KC - 1, r=[b_w1, b_hT[kc]], w=[b_ps[pbs[t2]]])
                                    for t2 in range(2):
                                        t_, b_t = tmp.next()
                                        kb.op("dve", lambda e: e.tensor_scalar_max(out=t_[:], in0=bank(pbs[t2]), scalar1=0.0), r=[b_ps[pbs[t2]]], w=[b_t])
                                        kb.op("act", lambda e: e.activation(out=hid[:, hcl, t2 * 512:(t2 + 1) * 512], in_=t_[:], func=AF.Square), r=[b_t], w=[b_hid[hcl]])
                                for oc in range(16):
                                    w2_, b_w2 = w2r.next()
                                    for q2 in range(2):
                                        kb.dma(w2_[:, q2 * 16:(q2 + 1) * 16, :], mlp_w2_t[layer, oc, :, half * 32 + q2 * 16:half * 32 + (q2 + 1) * 16, :], w=[b_w2], q="pool")
                                    pbs = [4 + (nb2 + t2) % 4 for t2 in range(2)]
                                    nb2 += 2
                                    for k_ in range(32):
                                        for t2 in range(2):
                                            mm(bank(pbs[t2]), w2_[:, k_, :], hid[:, k_, t2 * 512:(t2 + 1) * 512], k_ == 0, k_ == 31, r=[b_w2, b_hid[k_]], w=[b_ps[pbs[t2]]])
                                    for t2 in range(2):
                                        tb = tt * 2 + t2
                                        x_, b_x = xr.next()
                                        kb.dma(x_[:], xT[s, oc * 128:(oc + 1) * 128, tb * 512:(tb + 1) * 512], r=[b_xT[s][tb][oc]], w=[b_x])
                                        kb.op("dve", lambda e: e.tensor_tensor(out=x_[:], in0=x_[:], in1=bank(pbs[t2]), op=ALU.add), r=[b_x, b_ps[pbs[t2]]], w=[b_x])
                                        kb.dma(xT[s, oc * 128:(oc + 1) * 128, tb * 512:(tb + 1) * 512], x_[:], r=[b_x], w=[b_xT[s][tb][oc]])

            def cross_attn(layer, s):
                with Stage(kb) as so:
                    kTm, b_kTm = so.sb("kTm", [128, 4, MEM], BF16)
                    vm, b_vm = so.sb("vm", [128, 2, 512], BF16)
                    ocaT = so.sb("ocaT", [128, 4, L], BF16)[0]
                    b_oca = [kb.buf("oca") for _ in range(4)]
                    with Stage(kb) as st:
                        memT = st.sb("memT", [128, KC, MEM], BF16)[0]
                        b_memT = [kb.buf("memT") for _ in range(KC)]
                        mrow = st.rot("mrow", [128, D], F32, 2)
                        msq, b_msq = st.sb("msq", [128, D], F32)
                        mb = st.rot("mb", [128, D], BF16, 2)
                        col = st.rot("mcol", [128, 1], F32, 2)
                        for mt in range(2):
                            mr, b_mr = mrow.next()
                            kb.dma(mr[:], mem[s, mt * 128:(mt + 1) * 128, :], w=[b_mr])
                            kb.op("dve", lambda e: e.tensor_tensor(out=msq[:], in0=mr[:], in1=mr[:], op=ALU.mult), r=[b_mr], w=[b_msq])
                            cl, b_cl = col.next()
                            kb.op("dve", lambda e: e.reduce_sum(out=cl[:], in_=msq[:], axis=AX.X), r=[b_msq], w=[b_cl])
                            rstd_inplace(cl[:], b_cl, cl[:], [b_cl], 1.0 / D)
                            mb_, b_mb = mb.next()
                            kb.op("dve", lambda e: e.tensor_scalar(out=mb_[:], in0=mr[:], scalar1=cl[:, 0:1], scalar2=None, op0=ALU.mult), r=[b_mr, b_cl], w=[b_mb])
                            for hf in range(2):
                                pb = 4 + hf
                                for j in range(8):
                                    c = hf * 8 + j
                                    kb.op("pe", lambda e: e.transpose(out=bank_bf(pb)[:, j * 128:(j + 1) * 128], in_=mb_[:, c * 128:(c + 1) * 128], identity=identb[:]),
                                          r=[b_mb, b_c], w=[b_ps[pb]], inc=(j == 7))
                                for j in range(8):
                                    c = hf * 8 + j
                                    kb.op("dve", lambda e: e.tensor_scalar(out=memT[:, c, mt * 128:(mt + 1) * 128], in0=bank_bf(pb)[:, j * 128:(j + 1) * 128],
                                                                           scalar1=gains[:, 2, layer, c:c + 1], scalar2=None, op0=ALU.mult), r=[b_ps[pb], b_c], w=[b_memT[c]])
                        wr = st.rot("cwk", [128, KC, 128], BF16, 2)
                        kf, b_kf = st.sb("kf", [128, MEM], F32)
                        ksq, b_ksq = st.sb("ksq", [128, MEM], BF16)
                        krs, b_krs = st.sb("krs", [128, MEM], F32)
                        for h in range(4):
                            wt, b_wt = wr.next()
                            kb.dma(wt[:], ca_wk_t[layer, h], w=[b_wt], q="pool")
                            for kc in range(KC):
                                mm(bank(0)[:, 0:MEM], wt[:, kc, :], memT[:, kc, :], kc == 0, kc == KC - 1, r=[b_wt, b_memT[kc]], w=[b_ps[0]])
                            kb.op("act", lambda e: e.copy(out=kf[:], in_=bank(0)[:, 0:MEM]), r=[b_ps[0]], w=[b_kf])
                            kb.op("act", lambda e: e.activation(out=ksq[:], in_=bank(0)[:, 0:MEM], func=AF.Square), r=[b_ps[0]], w=[b_ksq])
                            mm(bank(1)[:, 0:MEM], onesb[:], ksq[:], True, True, r=[b_ksq, b_c], w=[b_ps[1]])
                            rstd_inplace(krs[:], b_krs, bank(1)[:, 0:MEM], [b_ps[1]], 1.0 / 128)
                            kb.op("dve", lambda e: e.scalar_tensor_tensor(out=kTm[:, h, :], in0=kf[:], scalar=cac[:, 1, layer:layer + 1], in1=krs[:], op0=ALU.mult, op1=ALU.mult),
                                  r=[b_kf, b_krs, b_c], w=[b_kTm])
                        wv, b_wv = st.sb("cwv", [128, KC, 512], BF16)
                        kb.dma(wv[:], ca_wv_t[layer], w=[b_wv], q="pool")
                        for mt in range(2):
                            for kc in range(KC):
                                mm(bank(2 + mt), memT[:, kc, mt * 128:(mt + 1) * 128], wv[:, kc, :], kc == 0, kc == KC - 1, r=[b_wv, b_memT[kc]], w=[b_ps[2 + mt]])
                            evac(mt, vm[:, mt, :], bank(2 + mt), [b_ps[2 + mt]], [b_vm])
                    scale = 128 ** -0.5
                    with Stage(kb) as sh:
                        hT = sh.sb("hT", [128, KC, L], BF16)[0]
                        b_hT = [kb.buf("hT") for _ in range(KC)]
                        norm_to_hT(s, 1, layer, hT, b_hT)
                        with Stage(kb) as st:
                            wr = st.rot("cwq", [128, KC, 128], BF16, 2)
                            qf = st.rot("cqf", [128, L], F32, 1)
                            pool = {"sqb": st.rot("csqb", [128, L], BF16, 1), "rs": st.rot("crs", [128, L], F32, 1)}
                            qn = st.rot("cqn", [128, L], BF16, 2)
                            pt = st.rot("cpt", [128, 512], BF16, 3)
                            rc = st.rot("crc", [128, 512], F32, 2)
                            for h in range(4):
                                lin_fm(wr, ca_wq_t[layer, h], hT, b_hT, KC)
                                qf_, b_qf = qf.next()
                                kb.op("act", lambda e: e.copy(out=qf_[:].rearrange("p (b t) -> p b t", b=4), in_=psA4), r=bA, w=[b_qf])
                                rs, b_rs = rmsnorm_fm(pool, qf_[:], b_qf, onesb[:], 1.0 / 128, None, None, None)
                                qn_, b_qn = qn.next()
                                kb.op("dve", lambda e: e.scalar_tensor_tensor(out=qn_[:], in0=qf_[:], scalar=cac[:, 0, layer:layer + 1], in1=rs[:], op0=ALU.mult, op1=ALU.mult),
                                      r=[b_qf, b_rs, b_c], w=[b_qn])
                                for qb in range(4):
                                    for mc in range(2):
                                        sbk = 4 + mc
                                        mm(bank(sbk), kTm[:, h, mc * 128:(mc + 1) * 128], qn_[:, qb * 512:(qb + 1) * 512], True, True, r=[b_kTm, b_qn], w=[b_ps[sbk]])
                                        p_, b_p = pt.next()
                                        kb.op("act", lambda e: e.activation(out=p_[:], in_=bank(sbk), func=AF.Exp, scale=scale), r=[b_ps[sbk]], w=[b_p])
                                        mm(bank(6), vm[:, mc, h * 128:(h + 1) * 128], p_[:], mc == 0, mc == 1, r=[b_vm, b_p], w=[b_ps[6]])
                                        mm(bank(7), onesb[:], p_[:], mc == 0, mc == 1, r=[b_c, b_p], w=[b_ps[7]])
                                    rc_, b_rc = rc.next()
                                    kb.op("dve", lambda e: e.reciprocal(out=rc_[:], in_=bank(7)), r=[b_ps[7]], w=[b_rc])
                                    kb.op("dve", lambda e: e.tensor_tensor(out=ocaT[:, h, qb * 512:(qb + 1) * 512], in0=bank(6), in1=rc_[:], op=ALU.mult),
                                          r=[b_ps[6], b_rc], w=[b_oca[h]])
                    with Stage(kb) as st:
                        wr = st.rot("cwo", [128, 4, 128], BF16, 2)
                        xr = st.rot("xr", [128, L], F32, 2)
                        for oc in range(16):
                            lin_fm(wr, ca_wo_t[layer, oc], ocaT, b_oca, 4, pbase=4 * (oc % 2))
                            residual_add(xr, s, oc, pbase=4 * (oc % 2))

            def view4(ap):
                return ap.rearrange("p (b t) -> p b t", b=4)


            def even_mixer(s):
                with Stage(kb) as so:
                    vx2T = [so.sb("vx2T%d" % cb, [128, 16, 512], BF16) for cb in range(2)]
                    with Stage(kb) as sh:
                        hT = sh.sb("hT", [128, KC, L], BF16)[0]
                        b_hT = [kb.buf("hT") for _ in range(KC)]
                        norm_to_hT(s, 0, 0, hT, b_hT)
                        with Stage(kb) as st:
                            wr = st.rot("hw", [128, KC, 128], BF16, 2)
                            upr = st.rot("up", [128, L + 2], F32, 2)
                            for u_, b_u in upr.items:
                                kb.op("pool", lambda e: e.memset(u_[:, 0:1], 0.0), w=[b_u])
                                kb.op("pool", lambda e: e.memset(u_[:, L + 1:L + 2], 0.0), w=[b_u])
                            accr = st.rot("acc", [128, L], F32, 4)
                            vbr = st.rot("vx2b", [128, L], BF16, 2)
                            for cb in range(2):
                                for cc in range(4):
                                    ch8 = cb * 4 + cc
                                    res = {}
                                    for gi in (1, 2, 0):
                                        ti = gi * 8 + ch8
                                        lin_fm(wr, ev_w_in_t[ti], hT, b_hT, KC)
                                        up, b_up = upr.next()
                                        kb.op("act", lambda e: e.copy(out=view4(up[:, 1:L + 1]), in_=psA4), r=bA, w=[b_up])
                                        acc, b_acc = accr.next()
                                        kb.op("dve", lambda e: e.tensor_scalar(out=acc[:], in0=up[:, 1:L + 1], scalar1=hycw[:, 1, ti:ti + 1], scalar2=hycb[:, ti:ti + 1],
                                                                               op0=ALU.mult, op1=ALU.add), r=[b_up, b_c], w=[b_acc])
                                        kb.op("dve", lambda e: e.scalar_tensor_tensor(out=acc[:], in0=up[:, 0:L], scalar=hycw[:, 0, ti:ti + 1], in1=acc[:],
                                                                                       op0=ALU.mult, op1=ALU.add), r=[b_up, b_c, b_acc], w=[b_acc])
                                        kb.op("dve", lambda e: e.scalar_tensor_tensor(out=acc[:], in0=up[:, 2:L + 2], scalar=hycw[:, 2, ti:ti + 1], in1=acc[:],
                                                                                      op0=ALU.mult, op1=ALU.add), r=[b_up, b_c, b_acc], w=[b_acc])
                                        res[gi] = (acc, b_acc)
                                    x1c, b_x1 = res[0]
                                    x2c, b_x2 = res[1]
                                    vc, b_v = res[2]
                                    kb.dma(hx1[s, ch8 * 128:(ch8 + 1) * 128, :], x1c[:], r=[b_x1], w=[b_scr["hx1"][s]])
                                    kb.op("pool", lambda e: e.tensor_tensor(out=vc[:], in0=vc[:], in1=x2c[:], op=ALU.mult), r=[b_v, b_x2], w=[b_v])
                                    kb.dma(hvx[s, ch8 * 128:(ch8 + 1) * 128, :], vc[:], r=[b_v], w=[b_scr["hvx"][s]])
                                    vb_, b_vb = vbr.next()
                                    kb.op("act", lambda e: e.copy(out=vb_[:], in_=vc[:]), r=[b_v], w=[b_vb])
                                    for hf in range(2):
                                        pb = 4 + hf
                                        for j in range(8):
                                            tc = hf * 8 + j
                                            kb.op("pe", lambda e: e.transpose(out=bank_bf(pb)[:, j * 128:(j + 1) * 128], in_=vb_[:, tc * 128:(tc + 1) * 128], identity=identb[:]),
                                                  r=[b_vb, b_c], w=[b_ps[pb]], inc=(j == 7))
                                        evac(hf, vx2T[cb][0][:, hf * 8:(hf + 1) * 8, cc * 128:(cc + 1) * 128],
                                             bank_bf(pb).rearrange("p (j c) -> p j c", j=8), [b_ps[pb]], [vx2T[cb][1]])
                        with Stage(kb) as st:
                            wr = st.rot("dw", [128, KC, 128], BF16, 2)
                            qf = st.rot("dqf", [128, L], F32, 2)
                            pool_ = {"sqb": st.rot("dsqb", [128, L], BF16, 1), "rs": st.rot("drs", [128, L], F32, 1)}
                            qn = st.rot("dqn", [128, L], BF16, 2)
                            for which, dst, nm in ((0, dq, "dq"), (1, dk, "dk")):
                                for h in range(8):
                                    lin_fm(wr, ev_w_in_t[24 + which * 8 + h], hT, b_hT, KC)
                                    qf_, b_qf = qf.next()
                                    kb.op("act", lambda e: e.copy(out=view4(qf_[:]), in_=psA4), r=bA, w=[b_qf])
                                    rs, b_rs = rmsnorm_fm(pool_, qf_[:], b_qf, bd64b[:], 1.0 / 64, None, None, None)
                                    qn_, b_qn = qn.next()
                                    kb.op("dve", lambda e: e.scalar_tensor_tensor(out=qn_[:], in0=qf_[:], scalar=dfc[:, which:which + 1], in1=rs[:], op0=ALU.mult, op1=ALU.mult),
                                          r=[b_qf, b_rs, b_c], w=[b_qn])
                                    kb.dma(dst[s, h * 128:(h + 1) * 128, :], qn_[:], r=[b_qn], w=[b_scr[nm][s]])
                            wv = st.rot("dwv", [128, KC, 512], BF16, 1)
                            vrow = st.rot("dvrow", [128, 512], BF16, 3)
                            for vb in range(2):
                                wv_, b_wv = wv.next()
                                for j4 in range(4):
                                    kb.dma(wv_[:, :, j4 * 128:(j4 + 1) * 128], ev_w_in_t[40 + vb * 4 + j4], w=[b_wv], q="pool")
                                for tt in range(16):
                                    pb = 4 + tt % 4
                                    for kc in range(KC):
                                        mm(bank(pb), hT[:, kc, tt * 128:(tt + 1) * 128], wv_[:, kc, :], kc == 0, kc == KC - 1, r=[b_wv, b_hT[kc]], w=[b_ps[pb]])
                                    vr, b_vr = vrow.next()
                                    evac(tt, vr[:], bank(pb), [b_ps[pb]], [b_vr])
                                    kb.dma(dv[s, tt * 128:(tt + 1) * 128, vb * 512:(vb + 1) * 512], vr[:], r=[b_vr], w=[b_scr["dv"][s]])
                    with Stage(kb) as st:
                        ftr = st.rot("ft", [128, 16, 128], BF16, 4)
                        kt = st.rot("kt", [128, 512], F32, 6)
                        tmpr = st.rot("ht", [128, 512], F32, 4)
                        Yf, b_Yf = st.sb("Yf", [128, 32, 512], BF16)
                        gtr = st.rot("gt", [128, 16, 512], BF16, 2)
                        epr = st.rot("ep", [128, 512], F32, 4)
                        obr = st.rot("ob", [128, 512], BF16, 2)
                        for cb in range(2):
                            vt, b_vt = vx2T[cb]
                            for j in range(16):
                                ftc, b_fc = ftr.next()
                                kb.dma(ftc[:], c_FT[j], w=[b_fc])
                                fts, b_fs = ftr.next()
                                kb.dma(fts[:], c_FT[16 + j], w=[b_fs])
                                pa = (j % 2) * 2
                                pbk = pa + 1
                                for tc in range(16):
                                    mm(bank(pa), ftc[:, tc, :], vt[:, tc, :], tc == 0, tc == 15, r=[b_fc, b_vt], w=[b_ps[pa]])
                                for tc in range(16):
                                    mm(bank(pbk), fts[:, tc, :], vt[:, tc, :], tc == 0, tc == 15, r=[b_fs, b_vt], w=[b_ps[pbk]])
                                ka_, b_ka = kt.next()
                                kb.dma(ka_[:], KA[j, :, cb * 512:(cb + 1) * 512], r=[b_K], w=[b_ka])
                                kb_, b_kb = kt.next()
                                kb.dma(kb_[:], KBt[j, :, cb * 512:(cb + 1) * 512], r=[b_K], w=[b_kb])
                                if j == 0:
                                    kd_, b_kd = kt.next()
                                    kb.dma(kd_[:], KD0[:, cb * 512:(cb + 1) * 512], r=[b_K], w=[b_kd])
                                else:
                                    kd_, b_kd = ka_, b_ka
                                t1, b_t1 = tmpr.next()
                                t2, b_t2 = tmpr.next()
                                kb.op("dve", lambda e: e.tensor_tensor(out=t1[:], in0=bank(pa), in1=ka_[:], op=ALU.mult), r=[b_ps[pa], b_ka], w=[b_t1])
                                kb.op("dve", lambda e: e.tensor_tensor(out=t2[:], in0=bank(pbk), in1=kb_[:], op=ALU.mult), r=[b_ps[pbk], b_kb], w=[b_t2])
                                kb.op("pool", lambda e: e.tensor_tensor(out=Yf[:, j, :], in0=t1[:], in1=t2[:], op=ALU.subtract), r=[b_t1, b_t2], w=[b_Yf])
                                t3, b_t3 = tmpr.next()
                                t4, b_t4 = tmpr.next()
                                kb.op("dve", lambda e: e.tensor_tensor(out=t3[:], in0=bank(pa), in1=kb_[:], op=ALU.mult), r=[b_ps[pa], b_kb], w=[b_t3])
                                kb.op("dve", lambda e: e.tensor_tensor(out=t4[:], in0=bank(pbk), in1=kd_[:], op=ALU.mult), r=[b_ps[pbk], b_kd], w=[b_t4])
                                kb.op("pool", lambda e: e.tensor_tensor(out=Yf[:, 16 + j, :], in0=t3[:], in1=t4[:], op=ALU.add), r=[b_t3, b_t4], w=[b_Yf])
                            for tb in range(4):
                                for hf in range(2):
                                    gt, b_gt = gtr.next()
                                    kb.dma(gt[:], c_G[tb, hf], w=[b_gt])
                                    for cc in range(4):
                                        for fj in range(16):
                                            mm(bank(4 + cc), Yf[:, hf * 16 + fj, cc * 128:(cc + 1) * 128], gt[:, fj, :], hf == 0 and fj == 0, hf == 1 and fj == 15,
                                               r=[b_Yf, b_gt], w=[b_ps[4 + cc]], inc=(fj == 15))
                                for cc in range(4):
                                    ch8 = cb * 4 + cc
                                    vxb, b_vxb = epr.next()
                                    x1b, b_x1b = epr.next()
                                    kb.dma(vxb[:], hvx[s, ch8 * 128:(ch8 + 1) * 128, tb * 512:(tb + 1) * 512], r=[b_scr["hvx"][s]], w=[b_vxb])
                                    kb.dma(x1b[:], hx1[s, ch8 * 128:(ch8 + 1) * 128, tb * 512:(tb + 1) * 512], r=[b_scr["hx1"][s]], w=[b_x1b])
                                    kb.op("dve", lambda e: e.scalar_tensor_tensor(out=vxb[:], in0=vxb[:], scalar=hyb[:, ch8:ch8 + 1], in1=bank(4 + cc), op0=ALU.mult, op1=ALU.add),
                                          r=[b_vxb, b_c, b_ps[4 + cc]], w=[b_vxb])
                                    ob, b_ob = obr.next()
                                    kb.op("pool", lambda e: e.tensor_tensor(out=ob[:], in0=vxb[:], in1=x1b[:], op=ALU.mult), r=[b_vxb, b_x1b], w=[b_ob])
                                    kb.dma(oT[s, ch8 * 128:(ch8 + 1) * 128, tb * 512:(tb + 1) * 512], ob[:], r=[b_ob], w=[b_oT[s][ch8]])
                scale = 64 ** -0.5
                with Stage(kb) as st:
                    base, b_base = st.sb("alibi", [128, 3968], F32)
                    kb.dma(base[:], c_alibi[:, :], w=[b_base])
                    qzr = Rot([(st.sb("aqz0%d" % i, [128, L], BF16), st.sb("aqz1%d" % i, [128, L], BF16)) for i in range(2)])
                    for (z0, b_z0), (z1, b_z1) in qzr.items:
                        kb.op("pool", lambda e: e.memset(z0[64:128, :], 0.0), w=[b_z0])
                        kb.op("pool", lambda e: e.memset(z1[0:64, :], 0.0), w=[b_z1])
                    kr = st.rot("ak", [128, L], BF16, 2)
                    vr = st.rot("av", [128, 16, 128], BF16, 2)
                    tmpr = st.rot("as", [128, 512], BF16, 3)
                    ptr = st.rot("ap", [128, 512], BF16, 3)
                    etr = st.rot("aet", [128, 3968], BF16, 2)
                    rcr = st.rot("arc", [128, 512], F32, 2)
                    rr = st.rot("ar", [128, 512], F32, 4)
                    osr = st.rot("aos", [128, 512], F32, 2)
                    sqr = st.rot("asq", [128, 512], BF16, 2)
                    rsr = st.rot("ars", [128, 512], F32, 2)
                    obr = st.rot("aob", [128, 512], BF16, 2)
                    LA = 2
                    iters = [(h, qb, j, kc) for h in range(8) for qb in range(4) for j in range(2) for kc in range(16)]
                    heads = {}
                    etabs = {}
                    rj = {}
                    pending = []

                    def emit_S(i):
                        h, qb, j, kc = iters[i]
                        if h not in heads:
                            (z0, b_z0), (z1, b_z1) = qzr.next()
                            kT_, b_k = kr.next()
                            v_, b_v = vr.next()
                            kb.dma(z0[0:64, :], dq[s, h * 128:h * 128 + 64, :], r=[b_scr["dq"][s]], w=[b_z0])
                            kb.dma(z1[64:128, :], dq[s, h * 128 + 64:(h + 1) * 128, :], r=[b_scr["dq"][s]], w=[b_z1])
                            qT_, b_q = (z0, z1), (b_z0, b_z1)
                            et_, b_et = etr.next()
                            kb.op("act", lambda e: e.activation(out=et_[:], in_=base[:], func=AF.Exp, scale=-(2.0 ** (-(h + 1)))), r=[b_base], w=[b_et])
                            etabs[h] = (et_, b_et)
                            kb.dma(kT_[:], dk[s, h * 128:(h + 1) * 128, :], r=[b_scr["dk"][s]], w=[b_k])
                            kb.dma(v_[:], dv[s].rearrange("(tc p) c -> p tc c", p=128)[:, :, h * 128:(h + 1) * 128], r=[b_scr["dv"][s]], w=[b_v])
                            heads[h] = (qT_, b_q, kT_, b_k, v_, b_v)
                        qT_, b_q, kT_, b_k, v_, b_v = heads[h]
                        sb_ = i % 3
                        mm(bank(sb_), kT_[:, kc * 128:(kc + 1) * 128], qT_[j][:, qb * 512:(qb + 1) * 512], True, True,
                           r=[b_k, b_q[j]], w=[b_ps[sb_]])

                    def emit_rest(i):
                        h, qb, j, kc = iters[i]
                        slope = 2.0 ** (-(h + 1))
                        qT_, b_q, kT_, b_k, v_, b_v = heads[h]
                        sb_ = i % 3
                        off = qb * 512 - kc * 128 + 1920
                        et_, b_et = etabs[h]
                        t_, b_t = tmpr.next()
                        kb.op("act", lambda e: e.activation(out=t_[:], in_=bank(sb_), func=AF.Exp, scale=scale), r=[b_ps[sb_]], w=[b_t])
                        p_, b_p = ptr.next()
                        kb.op("dve", lambda e: e.tensor_tensor(out=p_[:], in0=t_[:], in1=et_[:, off:off + 512], op=ALU.mult), r=[b_t, b_et], w=[b_p])
                        mm(bank(4 + j), v_[:, kc, :], p_[:], kc == 0, kc == 15, r=[b_v, b_p], w=[b_ps[4 + j]])
                        mm(bank(6 + j), onesb[:], p_[:], kc == 0, kc == 15, r=[b_c, b_p], w=[b_ps[6 + j]])
                        if kc != 15:
                            return
                        rc_, b_rc = rcr.next()
                        os_, b_os = osr.next()
                        r_, b_r = rr.next()
                        rj[j] = (r_, b_r)
                        kb.op("act", lambda e: e.activation(out=rc_[:], in_=bank(6 + j), func=AF.Ln), r=[b_ps[6 + j]], w=[b_rc])
                        kb.op("act", lambda e: e.activation(out=rc_[:], in_=rc_[:], func=AF.Exp, scale=-1.0), r=[b_rc], w=[b_rc])
                        kb.op("act", lambda e: e.copy(out=os_[:], in_=bank(4 + j)), r=[b_ps[4 + j]], w=[b_os])
                        pending.append((i + 2, lambda: kb.op("pool", lambda e: e.tensor_tensor(out=r_[:], in0=os_[:], in1=rc_[:], op=ALU.mult), r=[b_os, b_rc], w=[b_r])))
                        if j != 1:
                            return
                        (r0, b_r0), (r1, b_r1) = rj[0], rj[1]
                        sq_, b_sq = sqr.next()
                        rs_, b_rs = rsr.next()
                        ob, b_ob = obr.next()
                        pending.append((i + 5, lambda: kb.op("dve", lambda e: e.scalar_tensor_tensor(out=r0[:], in0=r1[:], scalar=neglam[:, 0:1], in1=r0[:], op0=ALU.mult, op1=ALU.add),
                                                            r=[b_r0, b_r1, b_c], w=[b_r0])))
                        pending.append((i + 7, lambda: kb.op("pool", lambda e: e.tensor_tensor(out=sq_[:], in0=r0[:], in1=r0[:], op=ALU.mult), r=[b_r0], w=[b_sq])))
                        pending.append((i + 10, lambda: mm(bank(3), onesb[:], sq_[:], True, True, r=[b_c, b_sq], w=[b_ps[3]])))
                        pending.append((i + 12, lambda: kb.op("dve", lambda e: e.tensor_scalar(out=rs_[:], in0=bank(3), scalar1=1.0 / 128, scalar2=EPS, op0=ALU.mult, op1=ALU.add), r=[b_ps[3]], w=[b_rs])))

                        def ln_exp():
                            kb.op("act", lambda e: e.activation(out=rs_[:], in_=rs_[:], func=AF.Ln), r=[b_rs], w=[b_rs])
                            kb.op("act", lambda e: e.activation(out=rs_[:], in_=rs_[:], func=AF.Exp, scale=-0.5), r=[b_rs], w=[b_rs])
                        pending.append((i + 14, ln_exp))

                        def fin():
                            kb.op("dve", lambda e: e.scalar_tensor_tensor(out=ob[:], in0=r0[:], scalar=sg2[:, 0:1], in1=rs_[:], op0=ALU.mult, op1=ALU.mult),
                                  r=[b_r0, b_rs, b_c], w=[b_ob])
                            kb.dma(oT[s, 1024 + h * 128:1024 + (h + 1) * 128, qb * 512:(qb + 1) * 512], ob[:], r=[b_ob], w=[b_oT[s][8 + h]])
                        pending.append((i + 17, fin))

                    n_it = len(iters)
                    for i in range(n_it + LA):
                        if i < n_it:
                            emit_S(i)
                        if i - LA >= 0:
                            emit_rest(i - LA)
                            due = [p for p in pending if p[0] <= i - LA]
                            for p in due:
                                pending.remove(p)
                                p[1]()
                    for p in sorted(pending, key=lambda p: p[0]):
                        p[1]()


            C.even_mixer = even_mixer


            def odd_mixer(s):
                with Stage(kb) as sh:
                    hT = sh.sb("hT", [128, KC, L], BF16)[0]
                    b_hT = [kb.buf("hT") for _ in range(KC)]
                    norm_to_hT(s, 0, 1, hT, b_hT)
                    with Stage(kb) as st:
                        cosT, b_cs = st.sb("cosT", [128, L], F32)
                        sinT, _ = st.sb("sinT", [128, L], F32)
                        kb.dma(cosT[:], c_cos[:, :], w=[b_cs])
                        kb.dma(sinT[:], c_sin[:, :], w=[b_cs])
                        wr = st.rot("gw", [128, KC, 128], BF16, 2)
                        qf = st.rot("gqf", [128, L], F32, 1)
                        qb16 = st.rot("gqb", [128, L], BF16, 1)
                        pool_ = {"sqb": st.rot("gsqb", [128, L], BF16, 1), "rs": st.rot("grs", [128, L], F32, 1)}
                        ta = st.rot("gta", [128, L], F32, 1)
                        tb_ = st.rot("gtb", [128, L], F32, 1)
                        qn = st.rot("gqn", [128, L], BF16, 2)
                        for hh in range(10):
                            isk = hh >= 8
                            ti = 40 + hh
                            lin_fm(wr, od_w_in_t[ti], hT, b_hT, KC)
                            qf_, b_qf = qf.next()
                            kb.op("act", lambda e: e.copy(out=view4(qf_[:]), in_=psA4), r=bA, w=[b_qf])
                            q16, b_q16 = qb16.next()
                            kb.op("act", lambda e: e.copy(out=view4(q16[:]), in_=psA4), r=bA, w=[b_q16])
                            rs, b_rs = rmsnorm_fm(pool_, qf_[:], b_qf, onesb[:], 1.0 / 128, None, None, None)
                            for tb in range(4):
                                mm(bank(4 + tb), RgT[:, 1 if isk else 0, :], q16[:, tb * 512:(tb + 1) * 512], True, True, r=[b_c, b_q16], w=[b_ps[4 + tb]])
                            a_, b_a = ta.next()
                            b__, b_b = tb_.next()
                            gcol = odc[:, 2:3] if isk else odc[:, 1:2]
                            kb.op("dve", lambda e: e.scalar_tensor_tensor(out=a_[:], in0=qf_[:], scalar=gcol, in1=cosT[:], op0=ALU.mult, op1=ALU.mult), r=[b_qf, b_c, b_cs], w=[b_a])
                            kb.op("dve", lambda e: e.tensor_tensor(out=view4(b__[:]), in0=psB4, in1=view4(sinT[:]), op=ALU.mult), r=bB + [b_cs], w=[b_b])
                            kb.op("pool", lambda e: e.tensor_tensor(out=a_[:], in0=a_[:], in1=b__[:], op=ALU.add), r=[b_a, b_b], w=[b_a])
                            qn_, b_qn = qn.next()
                            kb.op("dve", lambda e: e.tensor_tensor(out=qn_[:], in0=a_[:], in1=rs[:], op=ALU.mult), r=[b_a, b_rs], w=[b_qn])
                            if isk:
                                kb.dma(gk[s, (hh - 8) * 128:(hh - 7) * 128, :], qn_[:], r=[b_qn], w=[b_scr["gk"][s]])
                            else:
                                kb.dma(dq[s, hh * 128:(hh + 1) * 128, :], qn_[:], r=[b_qn], w=[b_scr["dq"][s]])
                        wv, b_wv = st.sb("gwv", [128, KC, 256], BF16)
                        for j4 in range(2):
                            kb.dma(wv[:, :, j4 * 128:(j4 + 1) * 128], od_w_in_t[50 + j4], w=[b_wv], q="pool")
                        vrow = st.rot("gvrow", [128, 256], BF16, 3)
                        for tt in range(16):
                            pb = 4 + tt % 4
                            for kc in range(KC):
                                mm(bank(pb)[:, 0:256], hT[:, kc, tt * 128:(tt + 1) * 128], wv[:, kc, :], kc == 0, kc == KC - 1, r=[b_wv, b_hT[kc]], w=[b_ps[pb]])
                            vr, b_vr = vrow.next()
                            evac(tt, vr[:], bank(pb)[:, 0:256], [b_ps[pb]], [b_vr])
                            kb.dma(gv[s, tt * 128:(tt + 1) * 128, :], vr[:], r=[b_vr], w=[b_scr["gv"][s]])
                        wi = st.rot("gwi", [128, KC, 512], BF16, 1)
                        irow = st.rot("girow", [128, 512], BF16, 3)
                        for vb in range(2):
                            wi_, b_wi = wi.next()
                            for j4 in range(4):
                                kb.dma(wi_[:, :, j4 * 128:(j4 + 1) * 128], od_w_in_t[24 + vb * 4 + j4], w=[b_wi], q="pool")
                            for tt in range(16):
                                pb = 4 + tt % 4
                                for kc in range(KC):
                                    mm(bank(pb), hT[:, kc, tt * 128:(tt + 1) * 128], wi_[:, kc, :], kc == 0, kc == KC - 1, r=[b_wi, b_hT[kc]], w=[b_ps[pb]])
                                ir, b_ir = irow.next()
                                evac(tt, ir[:], bank(pb), [b_ps[pb]], [b_ir])
                                kb.dma(dv[s, tt * 128:(tt + 1) * 128, vb * 512:(vb + 1) * 512], ir[:], r=[b_ir], w=[b_scr["dv"][s]])
                    with Stage(kb) as st:
                        msk, b_msk = st.sb("scanmask", [128, L], F32)
                        kb.op("pool", lambda e: e.memset(msk[:], 1.0), w=[b_msk])
                        kb.op("pool", lambda e: e.memset(msk[:].rearrange("p (c j) -> p c j", j=64)[:, :, 0:1], 0.0), w=[b_msk])
                        cm, b_cm = st.sb("cmask", [64, 2, 512], F32)
                        kb.dma(cm[:], c_mask[:, :, :], w=[b_cm])
                        wr = st.rot("hgw", [128, KC, 128], BF16, 2)
                        qs, b_qs = st.sb("hqs", [128, L], F32)
                        vtok, b_vtok = st.sb("hvtok", [64, 32, 128], BF16)
                        fa, b_fa = st.sb("hfa", [128, L], F32)
                        ka, b_kk = st.sb("hka", [128, L], F32)
                        Bc, b_B = st.sb("hB", [128, L], F32)
                        eb, b_eb = st.sb("heb", [128, L], F32)
                        enb, b_enb = st.sb("henb", [128, L], F32)
                        ebl, b_ebl = st.sb("hebl", [128, 32], F32)
                        qe, b_qe = st.sb("hqe", [128, L], BF16)
                        ke, b_ke = st.sb("hke", [128, L], BF16)
                        klT, b_klT = st.sb("hklT", [128, L], BF16)
                        kltok, b_kltok = st.sb("hkltok", [64, 32, 128], BF16)
                        attT, b_att = st.sb("hatt", [64, 32, 64], BF16)
                        sgt, b_sgt = enb, b_enb
                        Sbr = st.rot("hSb", [128, 128], BF16, 6)
                        oacc, b_oacc = st.sb("hoacc", [128, L], F32)
                        oaccb, b_oaccb = st.sb("hoaccb", [128, L], F32)
                        pool_ = {"sqb": st.rot("hsqb", [128, L], BF16, 1), "rs": st.rot("hrs", [128, L], F32, 1)}
                        ob, b_ob = st.sb("hob", [128, L], BF16)
                        c3 = lambda ap: ap.rearrange("p (c j) -> p c j", j=64)
                        for h in range(8):
                            kb.dma(vtok[:], dv[s].rearrange("(ch p) c -> p ch c", p=64)[:, :, h * 128:(h + 1) * 128], r=[b_scr["dv"][s]], w=[b_vtok])
                            lin_fm(wr, od_w_in_t[h], hT, b_hT, KC)
                            kb.op("act", lambda e: e.mul(out=view4(qs[:]), in_=psA4, mul=128 ** -0.5), r=bA, w=[b_qs])
                            for di in range(2):
                                lin_fm(wr, od_w_in_t[8 + di * 8 + h], hT, b_hT, KC)
                                kb.op("act", lambda e: e.activation(out=view4(fa[:]), in_=psA4, func=AF.Sigmoid), r=bA, w=[b_fa])
                                kb.op("dve", lambda e: e.tensor_scalar(out=ka[:], in0=fa[:], scalar1=noml[:, di, h:h + 1], scalar2=oml[:, di, h:h + 1], op0=ALU.mult, op1=ALU.add),
                                      r=[b_fa, b_c], w=[b_kk])
                                kb.op("act", lambda e: e.activation(out=fa[:], in_=fa[:], func=AF.Ln, scale=oml[:, di, h:h + 1], bias=lbs[:, di, h:h + 1]), r=[b_fa, b_c], w=[b_fa])
                                kb.op("dve", lambda e: e.tensor_tensor_scan(out=Bc[:], data0=msk[:], data1=fa[:], initial=0.0, op0=ALU.mult, op1=ALU.add), r=[b_msk, b_fa], w=[b_B])
                                if di == 1:
                                    kb.op("dve", lambda e: e.tensor_tensor(out=c3(eb[:]), in0=c3(Bc[:]), in1=c3(Bc[:])[:, :, 63:64].to_broadcast([128, 32, 64]), op=ALU.subtract),
                                          r=[b_B], w=[b_eb])
                                    kb.op("dve", lambda e: e.tensor_tensor(out=eb[:], in0=eb[:], in1=fa[:], op=ALU.subtract), r=[b_eb, b_fa], w=[b_eb])
                                    kb.op("act", lambda e: e.activation(out=enb[:], in_=eb[:], func=AF.Exp, scale=1.0), r=[b_eb], w=[b_enb])
                                    kb.op("act", lambda e: e.activation(out=eb[:], in_=eb[:], func=AF.Exp, scale=-1.0), r=[b_eb], w=[b_eb])
                                    last = 0
                                else:
                                    kb.op("act", lambda e: e.activation(out=eb[:], in_=Bc[:], func=AF.Exp, scale=1.0), r=[b_B], w=[b_eb])
                                    kb.op("act", lambda e: e.activation(out=enb[:], in_=Bc[:], func=AF.Exp, scale=-1.0), r=[b_B], w=[b_enb])
                                    last = 63
                                kb.op("pool", lambda e: e.tensor_copy(out=ebl[:], in_=c3(eb[:])[:, :, last]), r=[b_eb], w=[b_ebl])
                                kb.op("dve", lambda e: e.tensor_tensor(out=qe[:], in0=qs[:], in1=eb[:], op=ALU.mult), r=[b_qs, b_eb], w=[b_qe])
                                kb.op("dve", lambda e: e.tensor_tensor(out=ke[:], in0=ka[:], in1=enb[:], op=ALU.mult), r=[b_kk, b_enb], w=[b_ke])
                                kb.op("dve", lambda e: e.tensor_tensor(out=c3(klT[:]), in0=c3(ke[:]), in1=ebl[:].unsqueeze(2).to_broadcast([128, 32, 64]), op=ALU.mult),
                                      r=[b_ke, b_ebl], w=[b_klT])
                                for g4 in range(4):
                                    pb = 4 + g4
                                    for j in range(8):
                                        ch = g4 * 8 + j
                                        kb.op("pe", lambda e: e.transpose(out=bank_bf(pb)[0:64, j * 128:(j + 1) * 128], in_=klT[:, ch * 64:(ch + 1) * 64], identity=identb[:]),
                                              r=[b_klT, b_c], w=[b_ps[pb]], inc=(j == 7))
                                    evac(g4, kltok[:, g4 * 8:(g4 + 1) * 8, :], bank_bf(pb)[0:64, :].rearrange("p (j c) -> p j c", j=8), [b_ps[pb]], [b_kltok])
                                for g4 in range(4):
                                    pb = 4 + g4
                                    for j in range(8):
                                        ch = g4 * 8 + j
                                        mm(bank(pb)[0:64, j * 64:(j + 1) * 64], ke[:, ch * 64:(ch + 1) * 64], qe[:, ch * 64:(ch + 1) * 64], True, True,
                                           r=[b_ke, b_qe], w=[b_ps[pb]], inc=(j == 7))
                                    kb.op("dve", lambda e: e.tensor_tensor(out=attT[:, g4 * 8:(g4 + 1) * 8, :], in0=bank(pb)[0:64, :].rearrange("p (j c) -> p j c", j=8),
                                                                           in1=cm[:, di, :].rearrange("p (j c) -> p j c", j=8), op=ALU.mult), r=[b_ps[pb], b_cm], w=[b_att])
                                order = list(range(32)) if di == 0 else list(range(31, -1, -1))
                                Sb, b_Sb = Sbr.next()
                                kb.op("pool", lambda e: e.memset(Sb[:], 0.0), w=[b_Sb])
                                odst = oacc if di == 0 else oaccb
                                b_odst = b_oacc if di == 0 else b_oaccb
                                for gi in range(8):
                                    pb = 2 + gi % 2
                                    for j in range(4):
                                        ch = order[gi * 4 + j]
                                        mm(bank(pb)[:, j * 128:(j + 1) * 128], kltok[:, ch, :], vtok[:, ch, :], True, True, r=[b_kltok, b_vtok], w=[b_ps[pb]], inc=(j == 3))
                                    for j in range(4):
                                        i = gi * 4 + j
                                        ch = order[i]
                                        po = (i // 8) % 2
                                        sl = ch % 8
                                        mm(bank(po)[:, sl * 64:(sl + 1) * 64], Sb[:], qe[:, ch * 64:(ch + 1) * 64], True, False, r=[b_Sb, b_qe], w=[b_ps[po]], inc=False)
                                        mm(bank(po)[:, sl * 64:(sl + 1) * 64], vtok[:, ch, :], attT[:, ch, :], False, True, r=[b_vtok, b_att], w=[b_ps[po]])
                                        Sn, b_Sn = Sbr.next()
                                        kb.op("dve", lambda e: e.scalar_tensor_tensor(out=Sn[:], in0=Sb[:], scalar=ebl[:, ch:ch + 1], in1=bank(pb)[:, j * 128:(j + 1) * 128], op0=ALU.mult, op1=ALU.add),
                                              r=[b_Sb, b_ebl, b_ps[pb]], w=[b_Sn])
                                        Sb, b_Sb = Sn, b_Sn
                                        if i % 8 == 7:
                                            g8 = ch // 8
                                            kb.op("act", lambda e: e.copy(out=odst[:, g8 * 512:(g8 + 1) * 512], in_=bank(po)), r=[b_ps[po]], w=[b_odst])
                            kb.op("pool", lambda e: e.tensor_tensor(out=oacc[:], in0=oacc[:], in1=oaccb[:], op=ALU.add), r=[b_oacc, b_oaccb], w=[b_oacc])
                            lin_fm(wr, od_w_in_t[32 + h], hT, b_hT, KC)
                            kb.op("act", lambda e: e.activation(out=view4(sgt[:]), in_=psA4, func=AF.Silu), r=bA, w=[b_sgt])
                            rs, b_rs = rmsnorm_fm(pool_, oacc[:], b_oacc, onesb[:], 1.0 / 128, None, None, None)
                            kb.op("dve", lambda e: e.scalar_tensor_tensor(out=oacc[:], in0=oacc[:], scalar=odc[:, 0:1], in1=rs[:], op0=ALU.mult, op1=ALU.mult), r=[b_oacc, b_rs, b_c], w=[b_oacc])
                            kb.op("pool", lambda e: e.tensor_tensor(out=ob[:], in0=oacc[:], in1=sgt[:], op=ALU.mult), r=[b_oacc, b_sgt], w=[b_ob])
                            kb.dma(oT[s, h * 128:(h + 1) * 128, :], ob[:], r=[b_ob], w=[b_oT[s][h]])
                scale = 128 ** -0.5
                with Stage(kb) as st:
                    qr = st.rot("bq", [128, L], BF16, 2)
                    kr = st.rot("bk", [128, L], BF16, 2)
                    vr = st.rot("bv", [128, 16, 128], BF16, 2)
                    ptr = st.rot("bp", [128, 512], BF16, 3)
                    rcr = st.rot("brc", [128, 512], F32, 2)
                    obr = st.rot("bob", [128, 512], BF16, 2)
                    LA = 2
                    iters = [(hq, qb, kc) for hq in range(8) for qb in range(4) for kc in range(16)]
                    heads = {}
                    kvs = {}

                    def emit_S(i):
                        hq, qb, kc = iters[i]
                        g_ = hq // 4
                        if g_ not in kvs:
                            kT_, b_k = kr.next()
                            v_, b_v = vr.next()
                            kb.dma(kT_[:], gk[s, g_ * 128:(g_ + 1) * 128, :], r=[b_scr["gk"][s]], w=[b_k])
                            kb.dma(v_[:], gv[s].rearrange("(tc p) c -> p tc c", p=128)[:, :, g_ * 128:(g_ + 1) * 128], r=[b_scr["gv"][s]], w=[b_v])
                            kvs[g_] = (kT_, b_k, v_, b_v)
                        if hq not in heads:
                            qT_, b_q = qr.next()
                            kb.dma(qT_[:], dq[s, hq * 128:(hq + 1) * 128, :], r=[b_scr["dq"][s]], w=[b_q])
                            heads[hq] = (qT_, b_q)
                        kT_, b_k, v_, b_v = kvs[g_]
                        qT_, b_q = heads[hq]
                        sb_ = i % 3
                        mm(bank(sb_), kT_[:, kc * 128:(kc + 1) * 128], qT_[:, qb * 512:(qb + 1) * 512], True, True, r=[b_k, b_q], w=[b_ps[sb_]])

                    def emit_rest(i):
                        hq, qb, kc = iters[i]
                        g_ = hq // 4
                        kT_, b_k, v_, b_v = kvs[g_]
                        grp = hq * 4 + qb
                        po = 4 + grp % 2
                        pn = 6 + grp % 2
                        sb_ = i % 3
                        p_, b_p = ptr.next()
                        kb.op("act", lambda e: e.activation(out=p_[:], in_=bank(sb_), func=AF.Exp, scale=scale), r=[b_ps[sb_]], w=[b_p])
                        mm(bank(po), v_[:, kc, :], p_[:], kc == 0, kc == 15, r=[b_v, b_p], w=[b_ps[po]])
                        mm(bank(pn), onesb[:], p_[:], kc == 0, kc == 15, r=[b_c, b_p], w=[b_ps[pn]])
                        if kc != 15:
                            return
                        rc_, b_rc = rcr.next()
                        kb.op("dve", lambda e: e.reciprocal(out=rc_[:], in_=bank(pn)), r=[b_ps[pn]], w=[b_rc])
                        ob, b_ob = obr.next()
                        kb.op("dve", lambda e: e.tensor_tensor(out=ob[:], in0=bank(po), in1=rc_[:], op=ALU.mult), r=[b_ps[po], b_rc], w=[b_ob])
                        kb.dma(oT[s, 1024 + hq * 128:1024 + (hq + 1) * 128, qb * 512:(qb + 1) * 512], ob[:], r=[b_ob], w=[b_oT[s][8 + hq]])

                    n_it = len(iters)
                    for i in range(n_it + LA):
                        if i < n_it:
                            emit_S(i)
                        if i - LA >= 0:
                            emit_rest(i - LA)


            C.odd_mixer = odd_mixer

            for layer in range(2):
                for s in range(SPC):
                    if ("mix%d" % layer) in stages:
                        if layer == 0:
                            C.even_mixer(s)
                            out_proj(s, ev_w_out_t)
                        else:
                            C.odd_mixer(s)
                            out_proj(s, od_w_out_t)
                    if ("ca%d" % layer) in stages:
                        cross_attn(layer, s)
                    if ("mlp%d" % layer) in stages:
                        mlp(layer, s)

            if "epi" in stages:
                with Stage(kb) as st:
                    blk = st.rot("blk", [128, KC, 512], F32, 2)
                    yo = st.rot("yo", [128, D], F32, 2)
                    ev = 0
                    for s in range(SPC):
                        for tb in range(4):
                            bl, b_bl = blk.next()
                            kb.dma(bl[:], xTv[s][:, :, tb * 512:(tb + 1) * 512], r=xblk(s, tb), w=[b_bl])
                            for tt in range(4):
                                yt, b_yt = yo.next()
                                for j in range(4):
                                    for c4 in range(4):
                                        c = j * 4 + c4
                                        kb.op("pe", lambda e: e.transpose(out=bank(j)[:, c4 * 128:(c4 + 1) * 128],
                                                                          in_=bl[:, c, tt * 128:(tt + 1) * 128], identity=identf[:]),
                                              r=[b_bl, b_c], w=[b_ps[j]], inc=(c4 == 3))
                                    evac(ev, yt[:, j * 512:(j + 1) * 512], bank(j), [b_ps[j]], [b_yt])
                                    ev += 1
                                t0 = tb * 512 + tt * 128
                                kb.dma(y[s, t0:t0 + 128, :], yt[:], r=[b_yt], w=[])
        kb.final_wait()
    return C


def _tile_w(W, ncol):
    K, N = W.shape
    return np.ascontiguousarray(W.reshape(K // 128, 128, N // ncol, ncol).transpose(2, 1, 0, 3))


def _col(v, n=128):
    return np.ascontiguousarray(np.asarray(v).reshape(-1, n).T)


def host_consts():
    c = {}
    c["c_identf"] = np.eye(128, dtype=np.float32)
    bd = np.zeros((128, 128), np.float32)
    bd[:64, :64] = 1.0
    bd[64:, 64:] = 1.0
    c["c_bd64"] = bd
    p = np.arange(128)[:, None]
    j = np.arange(3968)[None, :]
    c["c_alibi"] = np.abs(j - p - 1920).astype(np.float32)
    t = np.linspace(0.0, 1.0, L, dtype=np.float32)[:, None]
    w = (np.float32(2.0 * math.pi / L) * np.arange(L, dtype=np.float32))[:, None]
    f = np.linspace(1e-4, 15, 16, dtype=np.float32)[None, :]
    z = np.concatenate([t, np.cos(f * w), -np.sin(f * w)], axis=-1).astype(np.float32)
    c["c_zT"] = np.ascontiguousarray(z.T)
    max_decay = math.log(1e-2) / 0.3
    min_decay = math.log(1e-2) / 1.5
    deltas = np.linspace(min_decay, max_decay, 1024, dtype=np.float32)
    win = np.exp(-t * np.abs(deltas)[None, :]).astype(np.float32)
    c["c_win"] = np.ascontiguousarray(win.reshape(16, 128, 1024))
    m0 = np.ones((128, 2), np.float32)
    m0[0, 0] = 0.0
    m0[:, 1] = 1.0 - m0[:, 0]
    c["c_m0"] = m0
    N = 2 * L
    tt = np.arange(L, dtype=np.float64)[:, None]
    ff = np.arange(L, dtype=np.float64)[None, :]
    ang = 2.0 * np.pi * tt * ff / N
    FT = np.concatenate([np.cos(ang), np.sin(ang)], axis=1)
    FT[:, L] = np.where(np.arange(L) % 2 == 0, 1.0, -1.0)
    FTt = FT.reshape(16, 128, 32, 128).transpose(2, 1, 0, 3)
    c["c_FT"] = np.ascontiguousarray(FTt).astype(ml_dtypes.bfloat16)
    wf = np.full((L,), 2.0 / N)
    wf[0] = 1.0 / N
    G = np.concatenate([(np.cos(ang) * wf[None, :]).T, (np.sin(ang) * (2.0 / N)).T], axis=0)
    G[L, :] = np.where(np.arange(L) % 2 == 0, 1.0, -1.0) / N
    Gt = G.reshape(2, 16, 128, 4, 512).transpose(3, 0, 2, 1, 4)
    c["c_G"] = np.ascontiguousarray(Gt).astype(ml_dtypes.bfloat16)
    rows = L // 64
    r = np.repeat(np.arange(rows, dtype=np.float32), 64)
    cc = np.tile(np.arange(64, dtype=np.float32), rows)
    half = 64
    inv = (np.float32(10000.0) ** (-np.arange(0, half, 2, dtype=np.float32) / half)).astype(np.float32)
    ang_r = r[:, None] * inv[None, :]
    ang_c = cc[:, None] * inv[None, :]
    a = np.concatenate([ang_r, ang_r, ang_c, ang_c], axis=-1).astype(np.float32)
    c["c_cos"] = np.ascontiguousarray(np.cos(a).T.astype(np.float32))
    c["c_sin"] = np.ascontiguousarray(np.sin(a).T.astype(np.float32))
    Pm = np.zeros((128, 128), np.float32)
    for m in range(32):
        Pm[m, m + 32] = -1.0
        Pm[m + 32, m] = 1.0
        Pm[m + 64, m + 96] = -1.0
        Pm[m + 96, m + 64] = 1.0
    c["c_PT"] = np.ascontiguousarray(Pm.T)
    s_ = np.arange(64)[:, None]
    t_ = np.arange(64)[None, :]
    mk = np.stack([np.tile((s_ <= t_).astype(np.float32), (1, 8)), np.tile((s_ >= t_).astype(np.float32), (1, 8))], axis=1)
    c["c_mask"] = np.ascontiguousarray(mk)
    return c


def host_layout(inp):
    o = {}
    g = np.stack([inp["norm_mix_g"], inp["norm_mem_q_g"], inp["norm_mem_kv_g"], inp["norm_ffn_g"]], axis=0)
    o["gains"] = np.ascontiguousarray(g.reshape(4, 2, KC, 128).transpose(3, 0, 1, 2))
    ev = inp["ev_w_in"][0]
    o["ev_w_in_t"] = _tile_w(ev, 128)
    o["ev_w_out_t"] = _tile_w(inp["ev_w_out"][0], 128)
    od = inp["od_w_in"][0]
    o["od_w_in_t"] = _tile_w(od, 128)
    o["od_w_out_t"] = _tile_w(inp["od_w_out"][0], 128)
    o["ca_wq_t"] = np.stack([_tile_w(inp["ca_w_q"][l], 128) for l in range(2)])
    o["ca_wk_t"] = np.stack([_tile_w(inp["ca_w_kv"][l][:, :512], 128) for l in range(2)])
    o["ca_wv_t"] = np.stack([_tile_w(inp["ca_w_kv"][l][:, 512:], 512)[0] for l in range(2)])
    o["ca_wo_t"] = np.stack([_tile_w(inp["ca_w_out"][l], 128) for l in range(2)])
    o["mlp_w1_t"] = np.stack([_tile_w(inp["mlp_w1"][l], 128) for l in range(2)])
    o["mlp_w2_t"] = np.stack([_tile_w(inp["mlp_w2"][l], 128) for l in range(2)])
    cw = inp["hy_conv_w"][0]
    o["hy_cw"] = np.ascontiguousarray(cw.reshape(3, 24, 128).transpose(2, 0, 1))
    o["hy_cb"] = _col(inp["hy_conv_b"][0])
    o["hy_bias"] = _col(inp["hy_bias"][0])
    o["hy_fw1"] = np.ascontiguousarray(inp["hy_fw1"][0])
    o["hy_fw2"] = np.ascontiguousarray(inp["hy_fw2"][0])
    o["hy_fw3"] = np.ascontiguousarray(inp["hy_fw3"][0])
    o["hy_fw4"] = np.ascontiguousarray(inp["hy_fw4"][0])
    o["hy_fcols"] = np.ascontiguousarray(np.stack([inp["hy_fb1"][0], inp["hy_fb2"][0], inp["hy_fb3"][0], inp["hy_sin_freq"][0]], axis=1))
    qg = np.concatenate([inp["df_q_g"][0], inp["df_q_g"][0]])
    kg = np.concatenate([inp["df_k_g"][0], inp["df_k_g"][0]])
    o["df_cols"] = np.ascontiguousarray(np.stack([qg, kg, inp["df_subln_g"][0]], axis=1))
    o["df_lam"] = np.ascontiguousarray(np.stack([inp["df_lam_q1"][0], inp["df_lam_k1"][0], inp["df_lam_q2"][0], inp["df_lam_k2"][0]])[None])
    lb = inp["hg_lb_logits"]
    o["hg_lb"] = np.ascontiguousarray(lb.reshape(2, 2, 8, 128).transpose(3, 0, 1, 2))
    o["od_cols"] = np.ascontiguousarray(np.stack([inp["hg_norm_g"][0], inp["gq_q_g"][0], inp["gq_k_g"][0]], axis=1))
    o["ca_cols"] = np.ascontiguousarray(np.stack([inp["ca_q_g"], inp["ca_k_g"]], axis=0).transpose(2, 0, 1))
    return {k: np.ascontiguousarray(v, dtype=np.float32) for k, v in o.items()}


_CACHE = {}


def run(inputs, ncores=NCORES, stages=ALL_STAGES, debug=()):
    key = (tuple(stages), tuple(debug))
    if key not in _CACHE:
        _CACHE[key] = build(stages=stages, debug=debug)
    if "consts" not in _CACHE:
        _CACHE["consts"] = host_consts()
    C = _CACHE[key]
    shared = dict(_CACHE["consts"])
    shared.update(host_layout(inputs))
    x = np.ascontiguousarray(inputs["x"], dtype=np.float32)
    mem = np.ascontiguousarray(inputs["mem"], dtype=np.float32)
    in_maps = []
    for c in range(ncores):
        m = {"x": x[c * SPC:(c + 1) * SPC], "mem": mem[c * SPC:(c + 1) * SPC]}
        m.update(shared)
        in_maps.append({k: v for k, v in m.items() if k in C.din})
    res = run_bass_kernel_spmd(C.nc, in_maps, core_ids=list(range(ncores)))
    return res.results


def kernel(**inputs):
    results = run(inputs)
    out = np.concatenate([r["y"] for r in results], axis=0)
    return out.astype(np.float32)
```
